# Optimizing a Trainium2 kernel written in Bass

```python
import numpy as np
import jax, jax.numpy as jnp
from jax import lax

D_MODEL = 1024
BATCH = 32
SEQ = 2048
DEPTH = 4

GRID_W = 64
CTX_LEN = 256
N_EVEN = (DEPTH + 1) // 2
N_ODD = DEPTH // 2

GLA_HEADS = 4
GLA_DK = D_MODEL // 16
GLA_DV = D_MODEL // 8
GLA_QK = GLA_HEADS * GLA_DK
GLA_V = GLA_HEADS * GLA_DV
GLA_RANK = 16
GLA_TAU = 16.0
GLA_CHUNK = 64
GLA_SIZES = (GLA_QK, GLA_QK, GLA_V, GLA_V, GLA_RANK, GLA_RANK)
GLA_IN = sum(GLA_SIZES)

RW_HEADS = 8
RW_DH = D_MODEL // 16
RW_W = RW_HEADS * RW_DH
RW_DECAY_RANK = 32
RW_A_RANK = 32
RW_G_RANK = 96
RW_GN_EPS = 64e-5
RW_SIZES = (RW_W, RW_W, RW_W, RW_DECAY_RANK, RW_DECAY_RANK, RW_A_RANK, RW_G_RANK)
RW_IN = sum(RW_SIZES)
EVEN_IN = GLA_IN + RW_IN
EVEN_MIX = GLA_V + RW_W

HEAD_DIM = 64
C_HEADS = 8
C_KV = 2
Q_BLOCK = 128
ROPE_THETA = 10000.0
NA_HEADS = 8
NA_KH = 8
NA_KW = 16
ODD_SIZES = (C_HEADS * HEAD_DIM, C_KV * HEAD_DIM, C_KV * HEAD_DIM,
             NA_HEADS * HEAD_DIM, NA_HEADS * HEAD_DIM, NA_HEADS * HEAD_DIM)
ODD_IN = sum(ODD_SIZES)
ODD_MIX = (C_HEADS + NA_HEADS) * HEAD_DIM

D_FF = -(-8 * D_MODEL // (3 * 256)) * 256

kernel_name = "hybrid_gla_rwkv7_gqa_natten_dit"

F32 = jnp.float32


def _split(z, sizes):
    return jnp.split(z, [int(i) for i in np.cumsum(sizes)[:-1]], axis=-1)


def rmsnorm(x, g, eps=1e-6):
    xf = x.astype(F32)
    y = xf * lax.rsqrt(jnp.mean(xf * xf, axis=-1, keepdims=True) + eps)
    return (y * g.astype(F32)).astype(x.dtype)


def swiglu(h, w13, w2):
    a, b = jnp.split(h @ w13, 2, axis=-1)
    return (jax.nn.silu(a) * b) @ w2


def token_shift(p, mu_prev, mu_next):
    prev = jnp.pad(p, ((0, 0), (1, 0), (0, 0)))[:, :-1]
    nxt = jnp.pad(p, ((0, 0), (0, 1), (0, 0)))[:, 1:]
    return p + mu_prev * (prev - p) + mu_next * (nxt - p)


def gla_chunked(q, k, v, log_a, s0):
    B, T, H, DK = q.shape
    DV = v.shape[-1]
    L = GLA_CHUNK
    n = T // L
    qf = q.astype(F32).reshape(B, n, L, H, DK)
    kf = k.astype(F32).reshape(B, n, L, H, DK)
    vf = v.astype(F32).reshape(B, n, L, H, DV)
    b = jnp.cumsum(log_a.astype(F32).reshape(B, n, L, H, DK), axis=2)
    total = b[:, :, -1]
    qb = qf * jnp.exp(b)
    kb = kf * jnp.exp(-b)
    lower = jnp.tril(jnp.ones((L, L), dtype=bool))
    att = jnp.where(lower, jnp.einsum('bnihd,bnjhd->bnhij', qb, kb), 0.0)
    o = jnp.einsum('bnhij,bnjhv->bnihv', att, vf)
    u = jnp.einsum('bnjhd,bnjhv->bnhdv', kf * jnp.exp(total[:, :, None] - b), vf)

    def step(s, inp):
        dec, uc = inp
        return dec[..., None] * s + uc, s

    s_fin, s_prev = lax.scan(step, s0, (jnp.exp(total).transpose(1, 0, 2, 3), u.transpose(1, 0, 2, 3, 4)))
    o = o + jnp.einsum('bnihd,nbhdv->bnihv', qb, s_prev)
    return o.reshape(B, T, H, DV).astype(v.dtype), s_fin


def rwkv7_scan(r, w, k, v, a, b, s0):
    xs = tuple(t.astype(F32).transpose(1, 0, 2, 3) for t in (r, w, k, v, a, b))

    def step(s, inp):
        rt, wt, kt, vt, at, bt = inp
        sa = jnp.einsum('bhvk,bhk->bhv', s, at)
        s = s * wt[:, :, None, :] + sa[..., None] * bt[:, :, None, :] + vt[..., None] * kt[:, :, None, :]
        return s, jnp.einsum('bhvk,bhk->bhv', s, rt)

    s_fin, o = lax.scan(step, s0, xs)
    return o.transpose(1, 0, 2, 3).astype(v.dtype), s_fin


def bidir(fn, fwd_in, bwd_in, s0_f, s0_b):
    o_f, s_f = fn(*fwd_in, s0_f)
    o_b, s_b = fn(*[jnp.flip(t, axis=1) for t in bwd_in], s0_b)
    return o_f + jnp.flip(o_b, axis=1), s_f, s_b


def gla_features(z, a_up, a_bias):
    B, T = z.shape[:2]
    q, k, v, g, ad_f, ad_b = _split(z, GLA_SIZES)
    la = [(jax.nn.log_sigmoid((ad @ a_up[d] + a_bias[d]).astype(F32)) / GLA_TAU).reshape(B, T, GLA_HEADS, GLA_DK)
          for d, ad in enumerate((ad_f, ad_b))]
    return (q.reshape(B, T, GLA_HEADS, GLA_DK) * GLA_DK ** -0.5, k.reshape(B, T, GLA_HEADS, GLA_DK),
            v.reshape(B, T, GLA_HEADS, GLA_DV), g, la[0], la[1])


def rwkv_features(z, mu, w0, w_up, a0, a_up, g_up, k_k, k_a):
    B, T = z.shape[:2]
    z = token_shift(z, mu[0], mu[1])
    r, k, v, wd_f, wd_b, ad, gd = _split(z, RW_SIZES)
    hd = lambda t: t.reshape(B, T, RW_HEADS, RW_DH)
    decays = []
    for d, wd in enumerate((wd_f, wd_b)):
        wl = -jax.nn.softplus(-(w0[d] + jnp.tanh(wd) @ w_up[d]).astype(F32)) - 0.5
        decays.append(hd(jnp.exp(-jnp.exp(wl))))
    a = jax.nn.sigmoid(a0 + ad @ a_up)
    g = jax.nn.sigmoid(gd) @ g_up
    kk = hd(k * k_k).astype(F32)
    kk = kk / jnp.maximum(jnp.sqrt(jnp.sum(kk * kk, axis=-1, keepdims=True)), 1e-12)
    k = k * (1 + (a - 1) * k_a)
    return hd(r), decays[0], decays[1], hd(k), hd(v), -kk, kk * hd(a).astype(F32), g


def even_mixer(h, hc, w_in, w_out, a_up, a_bias, gla_g, mu, w0, w_up, a0, aa_up, g_up,
               k_k, k_a, r_k, ln_g, ln_b, need_ctx):
    def features(hh):
        z = hh @ w_in
        return (gla_features(z[..., :GLA_IN], a_up, a_bias),
                rwkv_features(z[..., GLA_IN:], mu, w0, w_up, a0, aa_up, g_up, k_k, k_a))

    gla_c, rw_c = features(hc)
    gla_x, rw_x = features(h)
    B = h.shape[0]
    s0_gla = jnp.zeros((B, GLA_HEADS, GLA_DK, GLA_DV), F32)
    s0_rw = jnp.zeros((B, RW_HEADS, RW_DH, RW_DH), F32)

    def run_gla(f, s_f, s_b):
        q, k, v, _, la_f, la_b = f
        return bidir(gla_chunked, (q, k, v, la_f), (q, k, v, la_b), s_f, s_b)

    def run_rw(f, s_f, s_b):
        r, wf, wb, k, v, av, bv, _ = f
        return bidir(rwkv7_scan, (r, wf, k, v, av, bv), (r, wb, k, v, av, bv), s_f, s_b)

    og_c, sg_f, sg_b = run_gla(gla_c, s0_gla, s0_gla)
    or_c, sr_f, sr_b = run_rw(rw_c, s0_rw, s0_rw)
    og_x, _, _ = run_gla(gla_x, sg_f, sg_b)
    or_x, _, _ = run_rw(rw_x, sr_f, sr_b)

    def merge(og, gf, orw, rf):
        Bm, T = og.shape[:2]
        r, _, _, k, v, _, _, g_rw = rf
        y_gla = rmsnorm(og, gla_g) * jax.nn.silu(gf[3]).reshape(Bm, T, GLA_HEADS, GLA_DV)
        of = orw.astype(F32)
        mean = jnp.mean(of, axis=-1, keepdims=True)
        var = jnp.mean(jnp.square(of - mean), axis=-1, keepdims=True)
        y_rw = ((of - mean) * lax.rsqrt(var + RW_GN_EPS)).astype(orw.dtype).reshape(Bm, T, RW_W) * ln_g + ln_b
        bonus = jnp.sum(r * k * r_k, axis=-1, keepdims=True) * v
        y_rw = (y_rw + bonus.reshape(Bm, T, RW_W)) * g_rw
        return jnp.concatenate([y_gla.reshape(Bm, T, GLA_V), y_rw], axis=-1) @ w_out

    y = merge(og_x, gla_x, or_x, rw_x)
    yc = merge(og_c, gla_c, or_c, rw_c) if need_ctx else None
    return y, yc


def axial_rope_tables(T):
    t = jnp.arange(T)
    pos = jnp.stack([t // GRID_W, t % GRID_W], axis=-1).astype(F32)
    half = HEAD_DIM // 2
    inv = ROPE_THETA ** (-jnp.arange(0, half, 2, dtype=F32) / half)
    ang = pos[:, :, None] * inv
    return jnp.cos(ang), jnp.sin(ang)


def apply_rope(x, cos, sin):
    B, T, H, dh = x.shape
    xf = x.astype(F32).reshape(B, T, H, 2, 2, dh // 4)
    x1, x2 = xf[..., 0, :], xf[..., 1, :]
    c, s = cos[None, :, None], sin[None, :, None]
    out = jnp.stack([x1 * c - x2 * s, x1 * s + x2 * c], axis=-2)
    return out.reshape(B, T, H, dh).astype(x.dtype)


def blocked_attention(q, k, v):
    B, T, Hq, dh = q.shape
    Hkv = k.shape[2]
    G = Hq // Hkv
    nb = T // Q_BLOCK
    qb = (q * dh ** -0.5).reshape(B, nb, Q_BLOCK, Hkv, G, dh).transpose(1, 0, 2, 3, 4, 5)

    def blk(qi):
        s = jnp.einsum('bqkgd,bskd->bkgqs', qi, k).astype(F32)
        p = jax.nn.softmax(s, axis=-1).astype(v.dtype)
        return jnp.einsum('bkgqs,bskd->bqkgd', p, v)

    o = lax.map(blk, qb)
    return o.transpose(1, 0, 2, 3, 4, 5).reshape(B, T, Hq * dh)


def neighbourhood_attention(q, k, v, k_ctx, v_ctx, rpb):
    B, T, H, dh = q.shape
    rows = T // GRID_W
    kh = min(NA_KH, rows)
    kw = min(NA_KW, GRID_W)
    qr = (q * dh ** -0.5).reshape(B, rows, GRID_W, H, dh).transpose(1, 0, 2, 3, 4)
    kg = k.reshape(B, rows, GRID_W, H, dh)
    vg = v.reshape(B, rows, GRID_W, H, dh)
    col = jnp.arange(GRID_W)
    start = jnp.clip(col - kw // 2, 0, GRID_W - kw)
    in_win = (col[None, :] >= start[:, None]) & (col[None, :] < start[:, None] + kw)
    dc = jnp.clip(col[None, :] - col[:, None], -(NA_KW - 1), NA_KW - 1) + NA_KW - 1
    bias_cols = rpb[:, :, dc]

    def row(args):
        r, q_r = args
        rs = jnp.clip(r - kh // 2, 0, rows - kh)
        k_r = lax.dynamic_slice_in_dim(kg, rs, kh, axis=1)
        v_r = lax.dynamic_slice_in_dim(vg, rs, kh, axis=1)
        bias = bias_cols[:, rs + jnp.arange(kh) - r + NA_KH - 1]
        s_nb = jnp.einsum('bchd,bkwhd->bhckw', q_r, k_r).astype(F32) + bias.transpose(0, 2, 1, 3)[None].astype(F32)
        s_nb = jnp.where(in_win[:, None, :], s_nb, -jnp.inf)
        s_cx = jnp.einsum('bchd,blhd->bhcl', q_r, k_ctx).astype(F32)
        p = jax.nn.softmax(jnp.concatenate([s_nb.reshape(B, H, GRID_W, kh * GRID_W), s_cx], axis=-1), axis=-1)
        p = p.astype(v.dtype)
        p_nb = p[..., :kh * GRID_W].reshape(B, H, GRID_W, kh, GRID_W)
        return (jnp.einsum('bhckw,bkwhd->bchd', p_nb, v_r)
                + jnp.einsum('bhcl,blhd->bchd', p[..., kh * GRID_W:], v_ctx))

    o = lax.map(row, (jnp.arange(rows), qr))
    return o.transpose(1, 0, 2, 3, 4).reshape(B, T, H * dh)


def odd_mixer(h, hc, w_in, w_out, q_g, k_g, rpb, cos, sin, need_ctx):
    def features(hh):
        B, T = hh.shape[:2]
        qc, kc, vc, qd, kd, vd = _split(hh @ w_in, ODD_SIZES)
        return (rmsnorm(qc.reshape(B, T, C_HEADS, HEAD_DIM), q_g),
                rmsnorm(kc.reshape(B, T, C_KV, HEAD_DIM), k_g),
                vc.reshape(B, T, C_KV, HEAD_DIM),
                qd.reshape(B, T, NA_HEADS, HEAD_DIM),
                kd.reshape(B, T, NA_HEADS, HEAD_DIM),
                vd.reshape(B, T, NA_HEADS, HEAD_DIM))

    qc_c, kc_c, vc_c, qd_c, kd_c, vd_c = features(hc)
    qc, kc, vc, qd, kd, vd = features(h)
    qc, kc = apply_rope(qc, cos, sin), apply_rope(kc, cos, sin)
    y_gqa = blocked_attention(qc, jnp.concatenate([kc_c, kc], axis=1), jnp.concatenate([vc_c, vc], axis=1))
    y_na = neighbourhood_attention(qd, kd, vd, kd_c, vd_c, rpb)
    y = jnp.concatenate([y_gqa, y_na], axis=-1) @ w_out
    yc = None
    if need_ctx:
        yc = jnp.concatenate([blocked_attention(qc_c, kc_c, vc_c), blocked_attention(qd_c, kd_c, vd_c)], axis=-1) @ w_out
    return y, yc


def setup_inputs(seed: int = 0) -> dict:
    key = jax.random.key(seed)
    ks = iter(jax.random.split(key, 40))
    nrm = lambda shape, s: jax.random.normal(next(ks), shape, F32) * s
    uni = lambda shape, lo, hi: jax.random.uniform(next(ks), shape, F32, lo, hi)
    D = D_MODEL
    return {
        "x": nrm((BATCH, SEQ, D), 1.0),
        "c": nrm((BATCH, D), 1.0),
        "ctx": nrm((BATCH, CTX_LEN, D), 1.0),
        "c_ctx": nrm((D,), 1.0),
        "w_mod": nrm((DEPTH, D, 6 * D), 0.02),
        "b_mod": nrm((DEPTH, 6 * D), 0.02),
        "norm1_g": 1.0 + nrm((DEPTH, D), 0.02),
        "norm2_g": 1.0 + nrm((DEPTH, D), 0.02),
        "ffn_w13": nrm((DEPTH, D, 2 * D_FF), D ** -0.5),
        "ffn_w2": nrm((DEPTH, D_FF, D), D_FF ** -0.5),
        "ev_w_in": nrm((N_EVEN, D, EVEN_IN), D ** -0.5),
        "ev_w_out": nrm((N_EVEN, EVEN_MIX, D), EVEN_MIX ** -0.5),
        "gla_a_up": nrm((N_EVEN, 2, GLA_RANK, GLA_QK), GLA_RANK ** -0.5),
        "gla_a_bias": uni((N_EVEN, 2, GLA_QK), 1.0, 4.0),
        "gla_norm_g": 1.0 + nrm((N_EVEN, GLA_DV), 0.02),
        "rw_mu": uni((N_EVEN, 2, RW_IN), 0.0, 0.5),
        "rw_w0": uni((N_EVEN, 2, RW_W), -6.0, -1.0),
        "rw_w_up": nrm((N_EVEN, 2, RW_DECAY_RANK, RW_W), 0.5 * RW_DECAY_RANK ** -0.5),
        "rw_a0": nrm((N_EVEN, RW_W), 0.1),
        "rw_a_up": nrm((N_EVEN, RW_A_RANK, RW_W), 0.5 * RW_A_RANK ** -0.5),
        "rw_g_up": nrm((N_EVEN, RW_G_RANK, RW_W), RW_G_RANK ** -0.5),
        "rw_k_k": 0.85 + nrm((N_EVEN, RW_W), 0.02),
        "rw_k_a": 1.0 + nrm((N_EVEN, RW_W), 0.02),
        "rw_r_k": nrm((N_EVEN, RW_HEADS, RW_DH), 0.1),
        "rw_ln_g": 1.0 + nrm((N_EVEN, RW_W), 0.02),
        "rw_ln_b": nrm((N_EVEN, RW_W), 0.02),
        "od_w_in": nrm((N_ODD, D, ODD_IN), D ** -0.5),
        "od_w_out": nrm((N_ODD, ODD_MIX, D), ODD_MIX ** -0.5),
        "cq_norm_g": 1.0 + nrm((N_ODD, HEAD_DIM), 0.02),
        "ck_norm_g": 1.0 + nrm((N_ODD, HEAD_DIM), 0.02),
        "na_rpb": nrm((N_ODD, NA_HEADS, 2 * NA_KH - 1, 2 * NA_KW - 1), 0.1),
        "final_g": 1.0 + nrm((D,), 0.02),
    }


def reference(x, c, ctx, c_ctx, w_mod, b_mod, norm1_g, norm2_g, ffn_w13, ffn_w2,
              ev_w_in, ev_w_out, gla_a_up, gla_a_bias, gla_norm_g, rw_mu, rw_w0, rw_w_up,
              rw_a0, rw_a_up, rw_g_up, rw_k_k, rw_k_a, rw_r_k, rw_ln_g, rw_ln_b,
              od_w_in, od_w_out, cq_norm_g, ck_norm_g, na_rpb, final_g):
    cos, sin = axial_rope_tables(x.shape[1])
    s_lat = jax.nn.silu(c)
    s_ctx = jax.nn.silu(c_ctx)
    for i in range(DEPTH):
        need_ctx = i < DEPTH - 1
        sh1, sc1, g1, sh2, sc2, g2 = [m[:, None, :] for m in _split(s_lat @ w_mod[i] + b_mod[i], [D_MODEL] * 6)]
        sh1c, sc1c, g1c, sh2c, sc2c, g2c = _split(s_ctx @ w_mod[i] + b_mod[i], [D_MODEL] * 6)
        h = rmsnorm(x, norm1_g[i]) * (1 + sc1) + sh1
        hc = rmsnorm(ctx, norm1_g[i]) * (1 + sc1c) + sh1c
        j = i // 2
        if i % 2 == 0:
            y, yc = even_mixer(h, hc, ev_w_in[j], ev_w_out[j], gla_a_up[j], gla_a_bias[j], gla_norm_g[j],
                               rw_mu[j], rw_w0[j], rw_w_up[j], rw_a0[j], rw_a_up[j], rw_g_up[j],
                               rw_k_k[j], rw_k_a[j], rw_r_k[j], rw_ln_g[j], rw_ln_b[j], need_ctx)
        else:
            y, yc = odd_mixer(h, hc, od_w_in[j], od_w_out[j], cq_norm_g[j], ck_norm_g[j], na_rpb[j],
                              cos, sin, need_ctx)
        x = x + g1 * y
        h = rmsnorm(x, norm2_g[i]) * (1 + sc2) + sh2
        x = x + g2 * swiglu(h, ffn_w13[i], ffn_w2[i])
        if need_ctx:
            ctx = ctx + g1c * yc
            hc = rmsnorm(ctx, norm2_g[i]) * (1 + sc2c) + sh2c
            ctx = ctx + g2c * swiglu(hc, ffn_w13[i], ffn_w2[i])
    return rmsnorm(x, final_g)
```

```python
import numpy as np
import concourse.bass as bass
import concourse.mybir as mybir

F32 = mybir.dt.float32
BF16 = mybir.dt.bfloat16
ALU = mybir.AluOpType
AF = mybir.ActivationFunctionType
AX = mybir.AxisListType

SEG = 30000
NDMASEM = 24


class Buf:
    _n = 0

    def __init__(self, handle, shape, kind, blk=None):
        self.h = handle
        self.shape = list(shape)
        self.kind = kind
        Buf._n += 1
        self.id = Buf._n
        self.bdim = 1 if kind != 'dr' else 0
        if len(shape) <= self.bdim:
            self.bdim = 0
        n = shape[self.bdim]
        self.blk = blk if blk else 1
        self.nblk = (n + self.blk - 1) // self.blk
        if kind != 'dr' and len(shape) == 2 and blk is None:
            self.blk = n
            self.nblk = 1

    def full_ap(self):
        return self.h.ap() if hasattr(self.h, 'ap') and callable(getattr(self.h, 'ap')) else self.h[:]

    def __getitem__(self, key):
        if not isinstance(key, tuple):
            key = (key,)
        ap = self.h[key] if len(key) > 1 else self.h[key[0]]
        lo, hi = 0, self.shape[self.bdim]
        if len(key) > self.bdim:
            k = key[self.bdim]
            if isinstance(k, slice):
                lo = 0 if k.start is None else k.start
                hi = self.shape[self.bdim] if k.stop is None else k.stop
            else:
                lo, hi = int(k), int(k) + 1
        b0, b1 = lo // self.blk, (hi - 1) // self.blk + 1
        return View(ap, self, b0, b1)

    def all(self):
        return self[tuple(slice(None) for _ in self.shape)]


class View:
    def __init__(self, ap, buf, b0, b1):
        self.ap, self.buf, self.b0, self.b1 = ap, buf, b0, b1

    def blocks(self):
        return [(self.buf.id, b) for b in range(self.b0, self.b1)]

    def re(self, fn):
        return View(fn(self.ap), self.buf, self.b0, self.b1)

    def __getitem__(self, key):
        return View(self.ap[key], self.buf, self.b0, self.b1)


class Sched:
    ENGS = ['pe', 'act', 'dve', 'pool', 'sp']

    def __init__(self, nc):
        self.nc = nc
        self.q = {e: [] for e in self.ENGS}
        self.cnt = {e: 0 for e in self.ENGS}
        self.lastw = {}
        self.readers = {}
        self.seen = {e: {s: -1 for s in self.ENGS} for e in self.ENGS}
        self.seen_dma = {e: set() for e in self.ENGS}
        self.ndma = 0
        self.dma_sem_last = [None] * NDMASEM
        self.dma_sem_cnt = [0] * NDMASEM
        self.nsb = 0

    def sb(self, name, shape, dtype, blk=None):
        h = self.nc.alloc_sbuf_tensor(name, list(shape), dtype)
        return Buf(h, shape, 'sb', blk)

    def ps(self, name, shape, dtype=F32, blk=None):
        h = self.nc.alloc_psum_tensor(name, list(shape), dtype)
        return Buf(h, shape, 'ps', blk)

    def dram(self, name, shape, dtype, kind="Internal", blk=None):
        h = self.nc.dram_tensor(name, list(shape), dtype, kind=kind)
        return Buf(h, shape, 'dr', blk)

    def _deps(self, reads, writes):
        deps = []
        for v in reads:
            for b in v.blocks():
                t = self.lastw.get(b)
                if t is not None:
                    deps.append(t)
        for v in writes:
            for b in v.blocks():
                t = self.lastw.get(b)
                if t is not None:
                    deps.append(t)
                deps.extend(self.readers.get(b, ()))
        return deps

    def _record(self, tok, reads, writes):
        for v in reads:
            for b in v.blocks():
                lst = self.readers.setdefault(b, [])
                if tok[0] == 'c':
                    lst[:] = [t for t in lst if not (t[0] == 'c' and t[1] == tok[1])]
                lst.append(tok)
        for v in writes:
            for b in v.blocks():
                self.lastw[b] = tok
                self.readers[b] = []

    def _waits(self, eng, deps):
        waits = []
        best = {}
        for t in deps:
            if t[0] == 'c':
                _, src, idx = t
                if idx <= self.seen[eng][src]:
                    continue
                if idx > best.get(src, -1):
                    best[src] = idx
            else:
                _, did, sem, val = t
                if did in self.seen_dma[eng]:
                    continue
                self.seen_dma[eng].add(did)
                waits.append(('d', sem, val))
        for src, idx in best.items():
            self.seen[eng][src] = idx
            waits.append(('c', src, idx))
        return waits

    def op(self, eng, fn, reads=(), writes=()):
        reads = [v for v in reads if isinstance(v, View)]
        writes = [v for v in writes if isinstance(v, View)]
        deps = self._deps(reads, writes)
        waits = self._waits(eng, deps)
        idx = self.cnt[eng]
        self.cnt[eng] += 1
        tok = ('c', eng, idx)
        self.q[eng].append((fn, waits, ('c', eng, idx)))
        self._record(tok, reads, writes)
        return tok

    def dma(self, eng, out, in_, **kw):
        s = self.ndma % NDMASEM
        deps = self._deps([in_], [out])
        if self.dma_sem_last[s] is not None:
            deps.append(self.dma_sem_last[s])
        waits = self._waits(eng, deps)
        self.dma_sem_cnt[s] += 1
        tok = ('d', self.ndma, s, 16 * self.dma_sem_cnt[s])
        self.dma_sem_last[s] = tok
        self.ndma += 1
        oa, ia = out.ap, in_.ap

        def fn(e, oa=oa, ia=ia, kw=kw):
            return e.dma_start(out=oa, in_=ia, **kw)
        self.q[eng].append((fn, waits, tok))
        self._record(tok, [in_], [out])
        return tok

    def wait_all(self, eng):
        deps = []
        for e in self.ENGS:
            if self.cnt[e] > 0:
                deps.append(('c', e, self.cnt[e] - 1))
        for s in range(NDMASEM):
            if self.dma_sem_last[s] is not None:
                deps.append(self.dma_sem_last[s])
        waits = self._waits(eng, deps)
        self.q[eng].append((None, waits, None))

    def emit(self):
        nc = self.nc
        csem = {}
        for e in self.ENGS:
            nseg = (self.cnt[e] + SEG - 1) // SEG
            csem[e] = [nc.alloc_semaphore(f"c_{e}_{i}") for i in range(nseg)]
        dsem = [nc.alloc_semaphore(f"d_{i}") for i in range(NDMASEM)]
        engobj = {'pe': 'tensor', 'act': 'scalar', 'dve': 'vector', 'pool': 'gpsimd', 'sp': 'sync'}

        def run(ename):
            def body(engine):
                for fn, waits, sig in self.q[ename]:
                    for w in waits:
                        if w[0] == 'c':
                            _, src, idx = w
                            engine.wait_ge(csem[src][idx // SEG], idx % SEG + 1)
                        else:
                            _, sem, val = w
                            engine.wait_ge(dsem[sem], val)
                    if fn is None:
                        continue
                    ins = fn(engine)
                    if sig[0] == 'c':
                        _, src, idx = sig
                        ins.then_inc(csem[src][idx // SEG], 1)
                    else:
                        _, did, sem, val = sig
                        ins.then_inc(dsem[sem], 16)
            return body

        with nc.Block() as block:
            for e in self.ENGS:
                if not self.q[e]:
                    continue
                getattr(block, engobj[e])(run(e))

    def mm(self, out, lhsT, rhs, start=True, stop=True, **kw):
        o, l, r = out.ap, lhsT.ap, rhs.ap
        reads = [lhsT, rhs] + ([] if start else [])
        return self.op('pe', lambda e: e.matmul(o, l, r, start=start, stop=stop, **kw),
                       reads, [out])

    def tr(self, out, in_, ident):
        o, i, d = out.ap, in_.ap, ident.ap
        return self.op('pe', lambda e: e.transpose(o, i, d), [in_, ident], [out])

    def act(self, out, in_, func, bias=0.0, scale=1.0, accum_out=None, eng='act'):
        o, i = out.ap, in_.ap
        b = bias.ap if isinstance(bias, View) else bias
        s = scale.ap if isinstance(scale, View) else scale
        kw = {}
        wr = [out]
        if accum_out is not None:
            kw['accum_out'] = accum_out.ap
            wr.append(accum_out)
        return self.op('act', lambda e: e.activation(o, i, func, bias=b, scale=s, **kw),
                       [in_, bias, scale], wr)

    def tt(self, out, a, b, op, eng='dve'):
        o, x, y = out.ap, a.ap, b.ap
        return self.op(eng, lambda e: e.tensor_tensor(o, x, y, op), [a, b], [out])

    def ts(self, out, a, s1, s2, op0, op1=None, eng='dve', accum_out=None):
        o, x = out.ap, a.ap
        p1 = s1.ap if isinstance(s1, View) else s1
        p2 = s2.ap if isinstance(s2, View) else s2
        kw = {}
        wr = [out]
        if accum_out is not None:
            kw['accum_out'] = accum_out.ap
            wr.append(accum_out)
        if op1 is None:
            return self.op(eng, lambda e: e.tensor_scalar(o, x, p1, None, op0, **kw), [a, s1], wr)
        return self.op(eng, lambda e: e.tensor_scalar(o, x, p1, p2, op0, op1, **kw), [a, s1, s2], wr)

    def stt(self, out, a, s, b, op0, op1):
        o, x, y = out.ap, a.ap, b.ap
        p = s.ap if isinstance(s, View) else s
        return self.op('dve', lambda e: e.scalar_tensor_tensor(o, x, p, y, op0, op1), [a, s, b], [out])

    def copy(self, out, in_, eng='dve'):
        o, i = out.ap, in_.ap
        if eng == 'act':
            return self.op('act', lambda e: e.copy(o, i), [in_], [out])
        return self.op(eng, lambda e: e.tensor_copy(o, i), [in_], [out])

    def memset(self, out, val, eng='dve'):
        o = out.ap
        return self.op(eng, lambda e: e.memset(o, val), [], [out])

    def recip(self, out, in_):
        o, i = out.ap, in_.ap
        return self.op('dve', lambda e: e.reciprocal(o, i), [in_], [out])

    def reduce(self, out, in_, op, axis=AX.X):
        o, i = out.ap, in_.ap
        return self.op('dve', lambda e: e.tensor_reduce(o, i, axis, op), [in_], [out])

    def scan(self, out, d0, d1, init, op0, op1):
        o, a, b = out.ap, d0.ap, d1.ap
        ini = init.ap if isinstance(init, View) else init
        return self.op('dve', lambda e: e.tensor_tensor_scan(o, a, b, ini, op0, op1), [d0, d1, init], [out])


D = 1024
NT = 2304
CTX = 256
TW = 384
NTILE = NT // TW
CH = 128
NCH = NT // CH
DFF = 2816
EVEN_IN = 3296
ODD_IN = 2304
SLOT = 1152
NSLOT = 20
FT = 768
R_BMOD, R_NG, R_FG, R_GLAG = 0, 192, 256, 264
C_ID, C_TRF, C_TRB, C_ONE, C_STF, C_STB, C_BLK, C_EPS, C_1 = 0, 128, 256, 384, 512, 640, 768, 896, 897
KAPPA = float(np.exp(-0.5))
RW0 = 1568


def segs(c0, c1):
    out = []
    if c0 < CTX:
        out.append((c0, min(c1, CTX), True))
    if c1 > CTX:
        out.append((max(c0, CTX), c1, False))
    return out


class MK:
    def __init__(self, nc, nseq=4, layers=(0, 1, 2, 3), final=True, parts=('gla', 'rw', 'odd', 'ffn'), debug=False):
        self.nc = nc
        self.debug = debug
        self.dbgn = 0
        self.S = Sched(nc)
        self.nseq, self.layers, self.final, self.parts = nseq, layers, final, parts
        self.psn = 0
        self.stn = 0
        self.decl_inputs()
        self.alloc()
        self.prologue()
        for s in range(nseq):
            self.load_seq(s)
            for l in layers:
                self.layer(s, l)
            self.store_seq(s)
        self.S.wait_all('sp')
        self.S.emit()

    def decl_inputs(self):
        S, ns = self.S, self.nseq
        I = lambda n, sh, dt=F32: S.dram(n, sh, dt, kind="ExternalInput")
        self.x = I("x", [ns, 2048, D])
        self.ctx = I("ctx", [ns, CTX, D])
        self.cc = I("cc", [8, D])
        self.w_mod = I("w_mod", [4, D, 6 * D])
        self.vec32 = I("vec32", [384, 128])
        self.ffn_w13 = I("ffn_w13", [4, D, 2 * DFF])
        self.ffn_w2 = I("ffn_w2", [4, DFF, D])
        self.ev_w_in = I("ev_w_in", [2, D, EVEN_IN])
        self.ev_w_out = I("ev_w_out", [2, D, D])
        self.od_w_in = I("od_w_in", [2, D, ODD_IN])
        self.od_w_out = I("od_w_out", [2, D, D])
        self.gla_a_up = I("gla_a_up", [2, 2, 16, 256])
        self.rw_w_up = I("rw_w_up", [2, 2, 32, 512])
        self.rw_a_up = I("rw_a_up", [2, 32, 512])
        self.rw_g_up = I("rw_g_up", [2, 96, 512])
        self.vec64 = I("vec64", [64, 256])
        self.consts = I("consts", [128, 1792])
        self.rpbp = I("rpbp", [2, 120, 160])
        self.ropet = I("ropet", [2, 64, 2048])
        self.s_tab = [S.dram(f"s_tab_{j}", [64, 8, 960], BF16) for j in range(2)]
        self.rmask = I("rmask", [1, NT])
        self.vecrw = I("vecrw", [128, 256])
        self.rwp = S.dram("rwp", [5, 128, NT], BF16, kind=("ExternalOutput" if self.debug else "Internal"))
        self.out = S.dram("out", [ns, 2048, D], F32, kind="ExternalOutput")
        Wd = lambda n, nk, nc_: S.dram(n, [128, nk, nc_], BF16)
        self.s_w13 = [Wd(f"s_w13_{l}", 8, 2 * DFF) for l in range(4)]
        self.s_w2 = [Wd(f"s_w2_{l}", 22, D) for l in range(4)]
        self.s_evin = [Wd(f"s_evin_{j}", 8, EVEN_IN) for j in range(2)]
        self.s_evout = [Wd(f"s_evout_{j}", 8, D) for j in range(2)]
        self.s_odin = [Wd(f"s_odin_{j}", 8, ODD_IN) for j in range(2)]
        self.s_odout = [Wd(f"s_odout_{j}", 8, D) for j in range(2)]

    def alloc(self):
        S = self.S
        self.xT = S.sb("xT", [128, 8, NT], F32)
        self.hT = S.sb("hT", [128, 8, NT], BF16)
        self.ps = S.ps("ps", [128, 8, 512], F32)
        self.cst = S.sb("cst", [128, 328], F32, blk=328)
        self.cstb = S.sb("cstb", [128, 1792], BF16, blk=1792)
        self.modT = S.sb("modT", [128, 192, 5], F32, blk=192)
        self.modA = S.sb("modA", [128, 64, 5], F32, blk=64)
        self.vT = S.sb("vT", [128, 384], F32, blk=384)
        self.v64 = S.sb("v64", [64, 256], F32, blk=256)
        self.stage = [S.sb(f"stage{i}", [128, 3072], BF16, blk=3072) for i in range(3)]
        self.work = S.sb("work", [128, 10, TW], F32)
        self.scr = S.sb("scr", [128, NSLOT * SLOT], BF16, blk=SLOT)
        self.rm = S.sb("rm", [128, NT], BF16, blk=NT)
        self.rww = S.sb("rww", [128, 3, 512], BF16)
        self.vrw = S.sb("vrw", [128, 256], F32, blk=256)
        self.vd = S.sb("vd", [128, 128], F32, blk=128)

    def dump(self, name, view, shape, dt=F32):
        if not self.debug:
            return
        d = self.S.dram("dbg_" + name, list(shape), dt, kind="ExternalOutput")
        self.S.dma('sp', d.all(), view)

    nrot = 8

    def wv2(self, row, n, dt=F32, p0=0, p1=128):
        ap = self.work.h[p0:p1, row:row + 2, :].rearrange("p a t -> p (a t)")
        if dt == BF16:
            ap = ap.bitcast(BF16)
        return View(ap[:, 0:n], self.work, row, row + 2)

    def bank(self):
        b = self.psn % self.nrot
        self.psn += 1
        return b

    def sv(self, e0, n, dt=BF16, p0=0, p1=128):
        nb = n * (2 if dt == F32 else 1)
        ap = self.scr.h[p0:p1, e0:e0 + nb]
        if dt == F32:
            ap = ap.bitcast(F32)
        return View(ap, self.scr, e0 // SLOT, (e0 + nb - 1) // SLOT + 1)

    def wv(self, row, n=TW, dt=F32, p0=0, p1=128, c0=0):
        if dt == F32:
            ap = self.work.h[p0:p1, row, c0:c0 + n]
        else:
            ap = self.work.h[p0:p1, row, c0:c0 + (n + 1) // 2].bitcast(BF16)[:, 0:n]
        return View(ap, self.work, row, row + 1)

    def sub(self, v, key):
        return View(v.ap[key], v.buf, v.b0, v.b1)

    def ldw(self, src_view, nk, ncols, eng='sp'):
        st = self.stage[self.stn % 3]
        self.stn += 1
        dst = st[:, 0:nk * ncols].re(lambda a: a.rearrange("p (k c) -> p k c", k=nk))
        self.S.dma(eng, dst, src_view)
        return dst

    def ident(self, n=128):
        return self.cst[0:n, 0:n]

    def identb(self, n=128):
        return self.cstb[0:n, 0:n]

    def onesb(self, k=128, m=128):
        return self.cstb[0:k, C_ONE:C_ONE + m]

    def trib(self, d):
        return self.cstb[:, C_TRF + 128 * d:C_TRF + 128 * d + 128]

    def tris(self, d):
        return self.cstb[:, C_STF + 128 * d:C_STF + 128 * d + 128]

    def eps(self, p=128):
        return self.cst[0:p, 256:257]

    def one(self, p=128):
        return self.cst[0:p, 257:258]

    def gneps(self, p=128):
        return self.cst[0:p, 258:259]

    def blk32(self):
        return self.cst[:, 128:256]

    def blkb(self):
        return self.cstb[:, C_BLK:C_BLK + 128]

    def bmask(self, col, n):
        return self.cstb[:, col:col + 128].re(lambda a: a.rearrange("p (o t) -> p o t", o=1).broadcast_to([128, n, 128]))

    def cast_w(self, dst, src_view, ncols):
        for c0 in range(0, ncols, 1024):
            c1 = min(ncols, c0 + 1024)
            sv = src_view[:, c0:c1].re(lambda a: a.rearrange("(k p) c -> p k c", p=128))
            self.S.dma('pool', dst[:, :, c0:c1], sv)

    def prologue(self):
        S = self.S
        cf = self.sv(0, 1792, F32)
        S.dma('sp', cf, self.consts.all())
        S.copy(self.cstb.all(), cf)
        S.copy(self.cst[:, 0:128], self.sub(cf, (slice(None), slice(0, 128))))
        S.copy(self.cst[:, 128:256], self.sub(cf, (slice(None), slice(C_BLK, C_BLK + 128))))
        S.copy(self.cst[:, 256:264], self.sub(cf, (slice(None), slice(896, 904))))
        S.copy(self.cst[:, 264:328], self.sub(cf, (slice(None), slice(1664, 1728))))
        S.dma('sp', self.vrw.all(), self.vecrw.all())
        for jj in range(2):
            b0, b2 = jj * 128, jj * 64
            for (cp, cn, co, w) in ((0, 4, 0, 4), (8, 12, 4, 4), (16, 20, 8, 4), (24, 25, 12, 1), (26, 27, 13, 1)):
                S.stt(self.vd[:, b2 + co:b2 + co + w], self.vrw[:, b0 + cp:b0 + cp + w], -1.0, self.vrw[:, b0 + cn:b0 + cn + w], ALU.mult, ALU.subtract)
                S.ts(self.vd[:, b2 + co:b2 + co + w], self.vd[:, b2 + co:b2 + co + w], 1.0, None, ALU.add)
            S.ts(self.vd[:, b2 + 16:b2 + 20], self.vrw[:, b0 + 44:b0 + 48], -1.0, 1.0, ALU.mult, ALU.add)
        S.dma('sp', self.v64.all(), self.vec64.all())
        S.ts(self.v64[:, 0:8], self.v64[:, 0:8], -1.0, None, ALU.mult)
        S.ts(self.v64[:, 128:136], self.v64[:, 128:136], -1.0, None, ALU.mult)
        rmf = self.sv(4 * SLOT, NT, F32)
        S.dma('sp', rmf, self.rmask.all().re(lambda a: a.broadcast_to([128, NT])))
        S.copy(self.rm.all(), rmf)
        for q in range(3):
            tmp = self.wv(q, 128)
            S.dma('sp', tmp, self.vec32[q * 128:(q + 1) * 128, :])
            b = self.bank()
            S.tr(self.ps[:, b, 0:128], tmp, self.ident())
            S.copy(self.vT[:, q * 128:(q + 1) * 128], self.ps[:, b, 0:128])
        used = sorted(set(self.layers))
        for l in used:
            j = l // 2
            if l % 2 == 0:
                self.cast_w(self.s_evin[j], self.ev_w_in[j], EVEN_IN)
                self.cast_w(self.s_evout[j], self.ev_w_out[j], D)
            else:
                self.cast_w(self.s_odin[j], self.od_w_in[j], ODD_IN)
                self.cast_w(self.s_odout[j], self.od_w_out[j], D)
            self.cast_w(self.s_w13[l], self.ffn_w13[l], 2 * DFF)
            self.cast_w(self.s_w2[l], self.ffn_w2[l], D)
        for l in used:
            if l % 2 == 1:
                jj = l // 2
                for half in range(2):
                    tabf = self.sv(0, 60 * 64, F32, 0, 64)
                    tabb = self.sv(8 * SLOT, 60 * 64, BF16, 0, 64)
                    src = View(bass.AP(self.rpbp.h, jj * 120 * 160 + half * 60 * 160 + 16, [[1, 64], [160, 60], [1, 64]]),
                               self.rpbp, 0, 2)
                    S.dma('sp', tabf.re(lambda a: a.rearrange("p (n q) -> p n q", n=60)), src)
                    S.tt(tabb.re(lambda a: a.rearrange("p (n q) -> p n q", n=60)), tabf.re(lambda a: a.rearrange("p (n q) -> p n q", n=60)),
                         self.cst[0:64, 264:328].re(lambda a: a.rearrange("p (o q) -> p o q", o=1).broadcast_to([64, 60, 64])), ALU.add)
                    S.dma('sp', self.s_tab[jj][:, half * 4:half * 4 + 4, :].re(lambda a: a.rearrange("p h q -> p (h q)")), tabb)
        ccv = self.sv(16 * SLOT, D, F32, 0, 8)
        S.dma('sp', ccv, self.cc.all())
        S.act(ccv, ccv, AF.Silu)
        scT = self.wv(4, 64)
        for k in range(8):
            b = self.bank()
            S.tr(self.ps[:, b, 0:8], self.sub(ccv, (slice(None), slice(k * 128, (k + 1) * 128))), self.ident(8))
            S.copy(self.sub(scT, (slice(None), slice(k * 8, (k + 1) * 8))), self.ps[:, b, 0:8])
        for l in used:
            for piece in range(12):
                wv = self.sv((piece % 2) * 8 * SLOT, 8 * 512, F32).re(lambda a: a.rearrange("p (k c) -> p k c", k=8))
                S.dma('sp' if piece % 2 else 'act', wv, self.w_mod[l][:, piece * 512:(piece + 1) * 512].re(
                    lambda a: a.rearrange("(k p) c -> p k c", p=128)))
                for mm in range(4):
                    wm = piece * 4 + mm
                    b = self.bank()
                    for k in range(8):
                        S.mm(self.ps[:, b, 0:8], self.sub(wv, (slice(None), k, slice(mm * 128, (mm + 1) * 128))),
                             self.sub(scT, (slice(None), slice(k * 8, (k + 1) * 8))), start=(k == 0), stop=(k == 7))
                    r = R_BMOD + l * 48 + wm
                    S.ts(self.modT[:, l * 48 + wm, :], self.ps[:, b, 0:5], self.vT[:, r:r + 1], None, ALU.add)
            for n in range(2):
                for m in range(8):
                    r = R_NG + n * 32 + l * 8 + m
                    S.ts(self.modA[:, (l * 2 + n) * 8 + m, :], self.modT[:, l * 48 + (3 * n + 1) * 8 + m, :],
                         1.0, self.vT[:, r:r + 1], ALU.add, ALU.mult)

    def mod(self, l, w, m, col):
        return self.modT[:, l * 48 + w * 8 + m, col:col + 1]

    def mA(self, l, n, m, col):
        return self.modA[:, (l * 2 + n) * 8 + m, col:col + 1]

    def io_tile(self, t):
        return self.sv((t % 2) * 4 * SLOT, D, F32)

    def load_seq(self, s):
        S = self.S
        for t in range(NCH):
            src = self.ctx[s][t * 128:(t + 1) * 128, :] if t < 2 else self.x[s][(t - 2) * 128:(t - 1) * 128, :]
            tl = self.io_tile(t)
            S.dma('sp' if t % 2 == 0 else 'act', tl, src)
            for k in range(8):
                b = self.bank()
                S.tr(self.ps[:, b, 0:128], self.sub(tl, (slice(None), slice(k * 128, (k + 1) * 128))), self.ident())
                S.copy(self.xT[:, k, t * 128:(t + 1) * 128], self.ps[:, b, 0:128], eng='dve' if k % 2 else 'act')

    def store_seq(self, s):
        S = self.S
        if self.final:
            self.norm(None, None, s, final=True)
        for t in range(2, NCH):
            tl = self.io_tile(t)
            for k in range(8):
                b = self.bank()
                S.tr(self.ps[:, b, 0:128], self.xT[:, k, t * 128:(t + 1) * 128], self.ident())
                S.copy(self.sub(tl, (slice(None), slice(k * 128, (k + 1) * 128))), self.ps[:, b, 0:128],
                       eng='dve' if k % 2 else 'act')
            S.dma('sp' if t % 2 == 0 else 'act', self.out[s][(t - 2) * 128:(t - 1) * 128, :], tl)

    def norm(self, l, n, s, final=False):
        S = self.S
        for t in range(NTILE):
            c0, c1 = t * TW, (t + 1) * TW
            sq = [self.wv(k // 2, TW, BF16, c0=(k % 2) * (TW // 2)) for k in range(8)]
            for k in range(8):
                S.act(sq[k], self.xT[:, k, c0:c1], AF.Square)
            b = self.bank()
            for k in range(8):
                S.mm(self.ps[:, b, 0:TW], self.onesb(), sq[k], start=(k == 0), stop=(k == 7))
            rstd = self.wv(4)
            S.act(rstd, self.ps[:, b, 0:TW], AF.Sqrt, bias=self.eps(), scale=1.0 / D)
            S.recip(rstd, rstd)
            for k in range(8):
                if final:
                    S.stt(self.xT[:, k, c0:c1], self.xT[:, k, c0:c1], self.vT[:, R_FG + k:R_FG + k + 1], rstd,
                          ALU.mult, ALU.mult)
                    continue
                tmp = self.wv(5 + (k % 2))
                S.tt(tmp, self.xT[:, k, c0:c1], rstd, ALU.mult)
                for (a, e, isc) in segs(c0, c1):
                    col = 4 if isc else s
                    S.act(self.hT[:, k, a:e], self.sub(tmp, (slice(None), slice(a - c0, e - c0))), AF.Identity,
                          bias=self.mod(l, 3 * n, k, col), scale=self.mA(l, n, k, col))

    def resid(self, l, gate_w, m, c0, c1, ps_view, s):
        for (a, e, isc) in segs(c0, c1):
            col = 4 if isc else s
            self.S.stt(self.xT[:, m, a:e], self.sub(ps_view, (slice(None), slice(a - c0, e - c0))),
                       self.mod(l, gate_w, m, col), self.xT[:, m, a:e], ALU.mult, ALU.add)

    def ffn(self, l, s):
        S = self.S
        gj = lambda j, t: self.sv(j * FT + t * TW, TW)
        for blk in range(NT // FT):
            base = blk * FT
            for j in range(22):
                wa = self.ldw(self.s_w13[l][:, :, j * 128:(j + 1) * 128], 8, 128, eng='sp')
                wb = self.ldw(self.s_w13[l][:, :, DFF + j * 128:DFF + (j + 1) * 128], 8, 128, eng='act')
                for t in range(FT // TW):
                    c0 = base + t * TW
                    ba, bb = self.bank(), self.bank()
                    for k in range(8):
                        S.mm(self.ps[:, ba, 0:TW], wa[:, k, :], self.hT[:, k, c0:c0 + TW], start=(k == 0), stop=(k == 7))
                    for k in range(8):
                        S.mm(self.ps[:, bb, 0:TW], wb[:, k, :], self.hT[:, k, c0:c0 + TW], start=(k == 0), stop=(k == 7))
                    sl = self.wv(5 + (j * 2 + t) % 2)
                    S.act(sl, self.ps[:, ba, 0:TW], AF.Silu)
                    S.tt(gj(j, t), sl, self.ps[:, bb, 0:TW], ALU.mult)
            for m in range(8):
                w2 = self.ldw(self.s_w2[l][:, :, m * 128:(m + 1) * 128], 22, 128, eng='sp' if m % 2 else 'act')
                for t in range(FT // TW):
                    c0 = base + t * TW
                    b = self.bank()
                    for j in range(22):
                        S.mm(self.ps[:, b, 0:TW], w2[:, j, :], gj(j, t), start=(j == 0), stop=(j == 21))
                    self.resid(l, 5, m, c0, c0 + TW, self.ps[:, b, 0:TW], s)

    def layer(self, s, l):
        self.norm(l, 0, s)
        if l % 2 == 0:
            if 'gla' in self.parts:
                self.gla(l, s)
            if 'rw' in self.parts:
                self.rwkv(l, s)
        elif 'odd' in self.parts:
            self.odd(l, s)
        if 'ffn' in self.parts:
            self.norm(l, 1, s)
            self.ffn(l, s)

    def proj(self, wsrc, c0, M, cb, p0=0):
        S = self.S
        w = self.ldw(wsrc[:, :, c0:c0 + M], 8, M)
        for t in range(NTILE):
            b = self.bank()
            for k in range(8):
                S.mm(self.ps[p0:p0 + M, b, 0:TW], w[:, k, :], self.hT[:, k, t * TW:(t + 1) * TW], start=(k == 0), stop=(k == 7))
            cb(t, self.ps[p0:p0 + M, b, 0:TW])

    def gla(self, l, s):
        S = self.S
        j = l // 2
        win, wout = self.s_evin[j], self.s_evout[j]
        tsl = lambda t: (slice(None), slice(t * TW, (t + 1) * TW))
        adT = self.sv(0, NT, BF16, 0, 48)
        vtok = self.sv(2 * SLOT, NT)
        qf = self.sv(4 * SLOT, NT, BF16, 0, 64)
        kf = self.sv(6 * SLOT, NT, BF16, 0, 64)
        cw = self.sv(8 * SLOT, NT, F32, 0, 64)
        qb = self.sv(12 * SLOT, NT, BF16, 0, 64)
        kb = self.sv(14 * SLOT, NT, BF16, 0, 64)
        oacc = self.sv(16 * SLOT, NT, F32)
        aup = self.wv(5, 256, BF16, 0, 48)
        aupf = self.wv(6, 256, F32, 0, 48)
        etot = self.wv(7, NCH, F32, 0, 64)
        Sst = self.wv(7, 128, F32, 0, 64, c0=64)
        Sbf = self.wv(7, 128, BF16, 0, 64, c0=192)
        kbtok = self.wv(7, 64, BF16, 0, 128, c0=256)
        attm = self.wv(8, 128, BF16)
        for d in range(2):
            S.dma('sp', self.sub(aupf, (slice(32 * d, 32 * d + 16), slice(None))), self.gla_a_up[j][d])
            S.copy(self.sub(aup, (slice(32 * d, 32 * d + 16), slice(None))),
                   self.sub(aupf, (slice(32 * d, 32 * d + 16), slice(None))))
            self.proj(win, 1536 + 16 * d, 16,
                      lambda t, ps, d=d: S.copy(self.sub(adT, (slice(32 * d, 32 * d + 16), slice(t * TW, (t + 1) * TW))), ps),
                      p0=32 * d)
        for h in range(4):
            self.proj(win, h * 64, 64, lambda t, ps: S.copy(self.sub(qf, tsl(t)), ps, eng='act'))
            self.proj(win, 256 + h * 64, 64, lambda t, ps: S.copy(self.sub(kf, tsl(t)), ps))
            wv = self.ldw(win[:, :, 512 + h * 128:512 + (h + 1) * 128], 8, 128)
            for c in range(NCH):
                b = self.bank()
                for k in range(8):
                    S.mm(self.ps[:, b, 0:128], self.hT[:, k, c * 128:(c + 1) * 128], wv[:, k, :], start=(k == 0), stop=(k == 7))
                S.copy(self.sub(vtok, (slice(None), slice(c * 128, (c + 1) * 128))), self.ps[:, b, 0:128],
                       eng='act' if c % 2 else 'dve')
            for d in range(2):
                rv = (lambda a: a) if d == 0 else (lambda a: a[:, ::-1])
                for t in range(NTILE):
                    b = self.bank()
                    S.mm(self.ps[0:64, b, 0:TW], self.sub(aup, (slice(32 * d, 32 * d + 16), slice(h * 64, (h + 1) * 64))),
                         self.sub(adT, (slice(32 * d, 32 * d + 16), slice(t * TW, (t + 1) * TW))))
                    cwt = self.sub(cw, tsl(t))
                    col = j * 128 + d * 4 + h
                    S.act(cwt, self.ps[0:64, b, 0:TW], AF.Exp, bias=self.v64[:, col:col + 1], scale=-1.0)
                    S.act(cwt, cwt, AF.Ln, bias=self.one(64), scale=1.0)
                if d == 0:
                    S.scan(cw, self.rm[0:64, :], cw, 0.0, ALU.mult, ALU.add)
                else:
                    S.scan(cw.re(rv), self.rm[0:64, :], cw.re(rv), 0.0, ALU.mult, ALU.add)
                S.act(etot, self.sub(cw, (slice(None), slice(127 if d == 0 else 0, NT, 128))), AF.Exp, scale=-1.0 / 16)
                for t in range(NTILE):
                    e1, e2 = self.wv(0, TW, F32, 0, 64), self.wv(1, TW, F32, 0, 64)
                    cwt = self.sub(cw, tsl(t))
                    S.act(e1, cwt, AF.Exp, scale=-1.0 / 16)
                    S.stt(self.sub(qb, tsl(t)), self.sub(qf, tsl(t)), 0.125, e1, ALU.mult, ALU.mult)
                    S.act(e2, cwt, AF.Exp, scale=1.0 / 16)
                    S.tt(self.sub(kb, tsl(t)), self.sub(kf, tsl(t)), e2, ALU.mult)
                S.memset(Sst, 0.0)
                S.memset(Sbf, 0.0)
                order = list(range(NCH)) if d == 0 else [1, 0] + list(range(NCH - 1, 1, -1))
                for c in order:
                    cs = (slice(None), slice(c * 128, (c + 1) * 128))
                    kbc, qbc, vc = self.sub(kb, cs), self.sub(qb, cs), self.sub(vtok, cs)
                    b1 = self.bank()
                    tp = View(self.ps.h[:, b1, 0:32].bitcast(BF16), self.ps, b1, b1 + 1)
                    S.tr(tp, kbc, self.identb(64))
                    S.copy(kbtok, tp, eng='act')
                    b2 = self.bank()
                    S.mm(self.ps[:, b2, 0:128], kbc, qbc)
                    S.tt(attm, self.ps[:, b2, 0:128], self.trib(d), ALU.mult)
                    b3 = self.bank()
                    S.mm(self.ps[:, b3, 0:128], vc, attm, start=True, stop=False)
                    S.mm(self.ps[:, b3, 0:128], Sbf, qbc, start=False, stop=True)
                    oc = self.sub(oacc, cs)
                    if d == 0:
                        S.copy(oc, self.ps[:, b3, 0:128], eng='act')
                    else:
                        S.tt(oc, oc, self.ps[:, b3, 0:128], ALU.add)
                    b4 = self.bank()
                    S.mm(self.ps[0:64, b4, 0:128], kbtok, vc)
                    S.tt(Sst, Sst, self.ps[0:64, b4, 0:128], ALU.add)
                    S.ts(Sst, Sst, self.sub(etot, (slice(None), slice(c, c + 1))), None, ALU.mult)
                    S.copy(Sbf, Sst, eng='act')
            wg = self.ldw(win[:, :, 1024 + h * 128:1024 + (h + 1) * 128], 8, 128)
            wo = self.ldw(wout[:, h:h + 1, :], 1, D)
            for t in range(NTILE):
                c0, c1 = t * TW, (t + 1) * TW
                ot = self.sub(oacc, tsl(t))
                sqb = self.wv(0, TW, BF16)
                S.act(sqb, ot, AF.Square)
                b = self.bank()
                S.mm(self.ps[:, b, 0:TW], self.onesb(), sqb)
                rstd = self.wv(1)
                S.act(rstd, self.ps[:, b, 0:TW], AF.Sqrt, bias=self.eps(), scale=1.0 / 128)
                S.recip(rstd, rstd)
                bg = self.bank()
                for k in range(8):
                    S.mm(self.ps[:, bg, 0:TW], wg[:, k, :], self.hT[:, k, c0:c1], start=(k == 0), stop=(k == 7))
                sg = self.wv(2)
                S.act(sg, self.ps[:, bg, 0:TW], AF.Silu)
                y1 = self.wv(3)
                S.stt(y1, ot, self.vT[:, R_GLAG + j:R_GLAG + j + 1], rstd, ALU.mult, ALU.mult)
                yb = self.wv(4, TW, BF16)
                S.tt(yb, y1, sg, ALU.mult)
                for m in range(8):
                    b = self.bank()
                    S.mm(self.ps[:, b, 0:TW], wo[:, 0, m * 128:(m + 1) * 128], yb)
                    self.resid(l, 2, m, c0, c1, self.ps[:, b, 0:TW], s)

    def tshift(self, zp, dst_fn, P, cmc, mpc, mnc):
        S = self.S
        i = 0
        for t in range(NTILE):
            for (a, e, isc) in segs(t * TW, (t + 1) * TW):
                z0 = a + (1 if isc else 2)
                n = e - a
                tmp = self.wv(5 + i % 2, n, F32, 0, P)
                i += 1
                zs = lambda o: self.sub(zp, (slice(0, P), slice(z0 + o, z0 + o + n)))
                S.ts(tmp, zs(0), cmc, None, ALU.mult)
                S.stt(tmp, zs(-1), mpc, tmp, ALU.mult, ALU.add)
                S.stt(dst_fn(a, e), zs(1), mnc, tmp, ALU.mult, ALU.add)

    def rwkv(self, l, s):
        S = self.S
        j = l // 2
        win, wout = self.s_evin[j], self.s_evout[j]
        vb, vb2 = j * 128, j * 64
        vc = lambda c, P=128: self.vrw[0:P, vb + c:vb + c + 1]
        vdc = lambda c, P=128: self.vd[0:P, vb2 + c:vb2 + c + 1]
        tsl = lambda t: (slice(None), slice(t * TW, (t + 1) * TW))
        LR1 = self.sv(0, NT, BF16, 0, 96)
        LR2 = self.sv(2 * SLOT, NT, BF16, 0, 96)
        oacc = self.sv(4 * SLOT, NT, F32)
        zp = self.sv(8 * SLOT, NT + 3, F32)
        rs_ = self.sv(13 * SLOT, NT)
        ks_ = self.sv(15 * SLOT, NT)
        vs_ = self.sv(17 * SLOT, NT)
        S.dma('pool', self.rww[0:64, 0, :], self.rw_w_up[j].re(lambda a: a.rearrange("d r c -> (d r) c")))
        S.dma('pool', self.rww[64:96, 1, :], self.rw_a_up[j])
        S.dma('pool', self.rww[0:96, 2, :], self.rw_g_up[j])
        for c in (0, 257, NT + 2):
            S.memset(self.sub(zp, (slice(None), slice(c, c + 1))), 0.0)

        def to_zp(P):
            def cb(t, ps):
                for (a, e, isc) in segs(t * TW, (t + 1) * TW):
                    o = 1 if isc else 2
                    S.copy(self.sub(zp, (slice(0, P), slice(a + o, e + o))), self.sub(ps, (slice(None), slice(a - t * TW, e - t * TW))),
                           eng='act' if t % 2 else 'dve')
            return cb
        self.proj(win, RW0 + 1536, 96, to_zp(96))
        self.tshift(zp, lambda a, e: self.sub(LR1, (slice(None), slice(a, e))), 96, vdc(12, 96), vc(24, 96), vc(25, 96))
        S.act(self.sub(LR1, (slice(0, 64), slice(None))), self.sub(LR1, (slice(0, 64), slice(None))), AF.Tanh)
        self.proj(win, RW0 + 1632, 96, to_zp(96))
        self.tshift(zp, lambda a, e: self.sub(LR2, (slice(None), slice(a, e))), 96, vdc(13, 96), vc(26, 96), vc(27, 96))
        S.act(LR2, LR2, AF.Sigmoid)
        self.dump('LR1', LR1, [96, NT], BF16)
        self.dump('LR2', LR2, [96, NT], BF16)
        for p in range(1 if self.debug else 4):
            for c in (0, 257, NT + 2):
                S.memset(self.sub(zp, (slice(None), slice(c, c + 1))), 0.0)
            for q, dst in enumerate((rs_, ks_, vs_)):
                self.proj(win, RW0 + q * 512 + p * 128, 128, to_zp(128))
                self.tshift(zp, lambda a, e, dst=dst: self.sub(dst, (slice(None), slice(a, e))), 128,
                            vdc(q * 4 + p), vc(q * 8 + p), vc(q * 8 + 4 + p))
            for t in range(NTILE):
                c0, c1 = t * TW, (t + 1) * TW
                ba = self.bank()
                for hh in range(2):
                    S.mm(self.ps[64 * hh:64 * hh + 64, ba, 0:TW], self.rww[64:96, 1, p * 128 + hh * 64:p * 128 + hh * 64 + 64],
                         self.sub(LR1, (slice(64, 96), slice(c0, c1))))
                a_ = self.wv(0)
                S.act(a_, self.ps[:, ba, 0:TW], AF.Sigmoid, bias=vc(36 + p))
                kk = self.wv(1)
                S.ts(kk, self.sub(ks_, tsl(t)), vc(40 + p), None, ALU.mult)
                sqb = self.wv(2, TW, BF16)
                S.tt(sqb, kk, kk, ALU.mult)
                bs = self.bank()
                S.mm(self.ps[:, bs, 0:TW], self.blkb(), sqb)
                nr = self.wv(3)
                S.act(nr, self.ps[:, bs, 0:TW], AF.Sqrt)
                S.ts(nr, nr, 1e-12, None, ALU.max)
                S.recip(nr, nr)
                S.tt(kk, kk, nr, ALU.mult)
                al = self.wv(2, TW, BF16)
                S.ts(al, kk, -1.0, None, ALU.mult)
                S.dma('sp', self.rwp[3][:, c0:c1], al)
                be = self.wv(3, TW, BF16)
                S.tt(be, kk, a_, ALU.mult)
                S.dma('act', self.rwp[4][:, c0:c1], be)
                S.ts(a_, a_, vc(44 + p), vdc(16 + p), ALU.mult, ALU.add)
                k2 = self.wv(4, TW, BF16)
                S.tt(k2, self.sub(ks_, tsl(t)), a_, ALU.mult)
                S.dma('sp', self.rwp[1][:, c0:c1], k2)
            S.dma('sp', self.rwp[0], rs_)
            S.dma('act', self.rwp[2], vs_)
            G0 = 8 * SLOT
            gin = [self.sv(G0 + i * 256, 256) for i in range(5)]
            dec = [self.sv(G0 + 2 * SLOT + i * 256, 256) for i in range(4)]
            tok = [self.sv(G0 + 3 * SLOT + i * 256, 256) for i in range(3)]
            amat = [self.sv(G0 + (4 + i // 2) * SLOT + (i % 2) * 512, 512) for i in range(4)]
            pm = [self.sv(G0 + (6 + i // 2) * SLOT + (i % 2) * 512, 512) for i in range(4)]
            rhsb = self.sv(G0 + 8 * SLOT, 128)
            ub = self.sv(G0 + 8 * SLOT + 128, 128)
            Mst = self.wv(4, 64, F32)
            Mbf = self.wv(4, 64, BF16, c0=64)
            gl = self.wv(4, 2, F32, c0=128)
            a4 = lambda v: v.re(lambda a: a.rearrange("p (h c t) -> p h c t", h=2, c=2))
            a3 = lambda v: v.re(lambda a: a.rearrange("p (c t) -> p c t", c=2))
            for d in range(2):
                S.memset(Mst, 0.0)
                S.memset(Mbf, 0.0)
                gorder = list(range(9)) if d == 0 else [0] + list(range(8, 0, -1))
                for gi in gorder:
                    g0 = gi * 256
                    gs = (slice(None), slice(g0, g0 + 256))
                    for i in range(5):
                        S.dma('sp' if i % 2 else 'act', gin[i], self.rwp[i][:, g0:g0 + 256])
                    bsg = self.bank()
                    for hh in range(2):
                        S.mm(self.ps[64 * hh:64 * hh + 64, bsg, 0:256],
                             self.rww[32 * d:32 * d + 32, 0, p * 128 + hh * 64:p * 128 + hh * 64 + 64],
                             self.sub(LR1, (slice(32 * d, 32 * d + 32), slice(g0, g0 + 256))))
                    cw = self.wv(0, 256)
                    S.act(cw, self.ps[:, bsg, 0:256], AF.Sigmoid, bias=vc(28 + d * 4 + p))
                    rmg = self.rm[:, 0:256]
                    if d == 0:
                        S.scan(cw, rmg, cw, 0.0, ALU.mult, ALU.add)
                    else:
                        S.scan(cw.re(lambda a: a[:, ::-1]), rmg, cw.re(lambda a: a[:, ::-1]), 0.0, ALU.mult, ALU.add)
                    E1, E2, E3 = self.wv(1, 256), self.wv(2, 256), self.wv(3, 256)
                    S.act(E1, cw, AF.Exp, scale=-KAPPA)
                    S.act(E2, cw, AF.Exp, scale=KAPPA)
                    S.memset(E3, 1.0)
                    if d == 0:
                        S.act(a3(E3)[:, :, 1:128], a3(cw)[:, :, 0:127], AF.Exp, scale=-KAPPA)
                    else:
                        S.act(a3(E3)[:, :, 0:127], a3(cw)[:, :, 1:128], AF.Exp, scale=-KAPPA)
                    S.copy(gl, a3(E1)[:, :, 127 if d == 0 else 0])
                    rt, at, bt, kt = dec
                    S.tt(rt, gin[0], E1, ALU.mult)
                    S.tt(at, gin[3], E3, ALU.mult)
                    S.tt(bt, gin[4], E2, ALU.mult)
                    S.tt(kt, gin[1], E2, ALU.mult)
                    for i, src in enumerate((bt, kt, gin[2])):
                        bt_ = self.bank()
                        tp = View(self.ps.h[:, bt_, 0:128].bitcast(BF16), self.ps, bt_, bt_ + 1)
                        for c in range(2):
                            S.tr(self.sub(tp, (slice(None), slice(c * 128, (c + 1) * 128))),
                                 self.sub(src, (slice(None), slice(c * 128, (c + 1) * 128))), self.identb())
                        S.copy(tok[i], tp, eng='act' if i % 2 else 'dve')
                    btok, ktok, vtok = tok

                    def amm(dst, lf, rf, maskcol, extra_ident=False):
                        b = self.bank()
                        for hh in range(2):
                            for c in range(2):
                                hs = slice(64 * hh, 64 * hh + 64)
                                cs = slice(c * 128, (c + 1) * 128)
                                S.mm(self.ps[:, b, (hh * 2 + c) * 128:(hh * 2 + c + 1) * 128],
                                     self.sub(lf, (hs, cs)), self.sub(rf, (hs, cs)))
                        S.tt(a4(dst).re(lambda a: a.rearrange("p h c t -> p (h c) t")),
                             self.ps[:, b, 0:512].re(lambda a: a.rearrange("p (n t) -> p n t", n=4)),
                             self.bmask(maskcol, 4), ALU.mult)
                    incl = C_TRF if d == 0 else C_TRB
                    strict = C_STF if d == 0 else C_STB
                    strictT = C_STB if d == 0 else C_STF
                    ArbT, ArkT, AakT, TT = amat
                    amm(ArbT, bt, rt, incl)
                    amm(ArkT, kt, rt, incl)
                    amm(AakT, kt, at, strict)
                    P, PT = pm[0], pm[1]
                    amm(PT, bt, at, strict)
                    amm(P, at, bt, strictT)


                    def bmm(lf, rf):
                        b = self.bank()
                        for n in range(4):
                            ns = (slice(None), slice(n * 128, (n + 1) * 128))
                            S.mm(self.ps[:, b, n * 128:(n + 1) * 128], self.sub(lf, ns), self.sub(rf, ns))
                        return self.ps[:, b, 0:512]
                    f4 = lambda v: v.re(lambda a: a.rearrange("p (n t) -> p n t", n=4))
                    Q0, Q0T = pm[2], pm[3]
                    S.tt(f4(Q0), f4(P), self.bmask(1024, 4), ALU.mult)
                    S.tt(f4(Q0T), f4(PT), self.bmask(1024, 4), ALU.mult)
                    NbT = [self.sv(G0 + 9 * SLOT + i * 512, 512) for i in range(4)]
                    for i in range(4):
                        S.tt(f4(NbT[i]), f4(PT), self.bmask(1152 + 128 * i, 4), ALU.mult)
                    Dm = self.sv(G0 + 11 * SLOT, 512)
                    S.tt(f4(Dm), f4(Q0), self.bmask(C_ID, 4), ALU.add)
                    S.tt(f4(TT), f4(Q0T), self.bmask(C_ID, 4), ALU.add)
                    Q1, Q1T = pm[0], pm[1]
                    S.copy(Q1, bmm(Q0T, Q0), eng='act')
                    S.copy(Q1T, bmm(Q0, Q0T))
                    S.tt(Dm, Dm, bmm(TT, Q1), ALU.add)
                    S.tt(TT, TT, bmm(Q1, TT), ALU.add)
                    Q2 = pm[2]
                    S.copy(Q2, bmm(Q1T, Q1), eng='act')
                    S.tt(Dm, Dm, bmm(TT, Q2), ALU.add)
                    S.tt(TT, TT, bmm(Q2, TT), ALU.add)
                    X = pm[0]
                    for i in range(4):
                        S.copy(X, bmm(NbT[i], Dm), eng='act')
                        if i < 3:
                            S.tt(Dm, Dm, bmm(TT, X), ALU.add)
                        S.tt(TT, TT, bmm(X, TT), ALU.add)
                    if d == 0 and gi == 0 and p == 0:
                        self.dump('cw', cw, [128, 256]); self.dump('E1', E1, [128, 256]); self.dump('E3', E3, [128, 256])
                        self.dump('TT', TT, [128, 512], BF16); self.dump('ArbT', ArbT, [128, 512], BF16); self.dump('AakT', AakT, [128, 512], BF16)
                        self.dump('P0', pm[0], [128, 512], BF16); self.dump('btok', btok, [128, 256], BF16)
                    corder = (0, 1) if d == 0 else (1, 0)
                    for c in corder:
                        cs = slice(c * 128, (c + 1) * 128)
                        b1 = self.bank()
                        for hh in range(2):
                            hs = slice(64 * hh, 64 * hh + 64)
                            n = hh * 2 + c
                            S.mm(self.ps[:, b1, hh * 64:hh * 64 + 64], self.sub(at, (hs, cs)), self.sub(Mbf, (hs, slice(None))),
                                 start=True, stop=False)
                            S.mm(self.ps[:, b1, hh * 64:hh * 64 + 64], self.sub(AakT, (slice(None), slice(n * 128, (n + 1) * 128))),
                                 self.sub(vtok, (slice(None), slice(c * 128 + hh * 64, c * 128 + hh * 64 + 64))), start=False, stop=True)
                        S.copy(rhsb, self.ps[:, b1, 0:128], eng='act')
                        b2 = self.bank()
                        for hh in range(2):
                            n = hh * 2 + c
                            S.mm(self.ps[:, b2, hh * 64:hh * 64 + 64], self.sub(TT, (slice(None), slice(n * 128, (n + 1) * 128))),
                                 self.sub(rhsb, (slice(None), slice(hh * 64, hh * 64 + 64))))
                        S.copy(ub, self.ps[:, b2, 0:128])
                        b3 = self.bank()
                        for hh in range(2):
                            hs = slice(64 * hh, 64 * hh + 64)
                            n = hh * 2 + c
                            us = self.sub(ub, (slice(None), slice(hh * 64, hh * 64 + 64)))
                            vv = self.sub(vtok, (slice(None), slice(c * 128 + hh * 64, c * 128 + hh * 64 + 64)))
                            S.mm(self.ps[hs, b3, 0:128], self.sub(Mbf, (hs, slice(None))), self.sub(rt, (hs, cs)), start=True, stop=False)
                            S.mm(self.ps[hs, b3, 0:128], us, self.sub(ArbT, (slice(None), slice(n * 128, (n + 1) * 128))), start=False, stop=False)
                            S.mm(self.ps[hs, b3, 0:128], vv, self.sub(ArkT, (slice(None), slice(n * 128, (n + 1) * 128))), start=False, stop=True)
                        oc = self.sub(oacc, (slice(None), slice(g0 + c * 128, g0 + (c + 1) * 128)))
                        if d == 0:
                            S.copy(oc, self.ps[:, b3, 0:128], eng='act')
                        else:
                            S.tt(oc, oc, self.ps[:, b3, 0:128], ALU.add)
                        b4 = self.bank()
                        for hh in range(2):
                            hs = slice(64 * hh, 64 * hh + 64)
                            us = self.sub(ub, (slice(None), slice(hh * 64, hh * 64 + 64)))
                            vv = self.sub(vtok, (slice(None), slice(c * 128 + hh * 64, c * 128 + hh * 64 + 64)))
                            S.mm(self.ps[hs, b4, 0:64], self.sub(btok, (slice(None), slice(c * 128 + hh * 64, c * 128 + hh * 64 + 64))), us,
                                 start=True, stop=False)
                            S.mm(self.ps[hs, b4, 0:64], self.sub(ktok, (slice(None), slice(c * 128 + hh * 64, c * 128 + hh * 64 + 64))), vv,
                                 start=False, stop=True)
                        S.tt(Mst, Mst, self.ps[:, b4, 0:64], ALU.add)
                        S.ts(Mst, Mst, self.sub(gl, (slice(None), slice(c, c + 1))), None, ALU.mult)
                        S.copy(Mbf, Mst, eng='act')
            self.dump(f'oacc{p}', oacc, [128, NT])
            wo = self.ldw(wout[:, 4 + p:5 + p, :], 1, D)
            for t in range(NTILE):
                c0, c1 = t * TW, (t + 1) * TW
                rr, kk2, vv = self.wv(5, TW, BF16), self.wv(5, TW, BF16, c0=192), self.wv(6, TW, BF16)
                S.dma('sp', rr, self.rwp[0][:, c0:c1])
                S.dma('act', kk2, self.rwp[1][:, c0:c1])
                S.dma('sp', vv, self.rwp[2][:, c0:c1])
                ot = self.sub(oacc, tsl(t))
                bm = self.bank()
                S.mm(self.ps[:, bm, 0:TW], self.blk32(), ot)
                cen = self.wv(0)
                S.stt(cen, self.ps[:, bm, 0:TW], -1.0 / 64, ot, ALU.mult, ALU.add)
                sq = self.wv(1)
                S.act(sq, cen, AF.Square)
                bv = self.bank()
                S.mm(self.ps[:, bv, 0:TW], self.blk32(), sq)
                rstd = self.wv(1)
                S.act(rstd, self.ps[:, bv, 0:TW], AF.Sqrt, bias=self.gneps(), scale=1.0 / 64)
                S.recip(rstd, rstd)
                S.tt(cen, cen, rstd, ALU.mult)
                S.ts(cen, cen, vc(52 + p), vc(56 + p), ALU.mult, ALU.add)
                rk = self.wv(2, TW, BF16)
                S.stt(rk, rr, vc(48 + p), kk2, ALU.mult, ALU.mult)
                bb = self.bank()
                S.mm(self.ps[:, bb, 0:TW], self.blkb(), rk)
                bon = self.wv(3)
                S.tt(bon, vv, self.ps[:, bb, 0:TW], ALU.mult)
                S.tt(cen, cen, bon, ALU.add)
                bg = self.bank()
                for hh in range(2):
                    S.mm(self.ps[64 * hh:64 * hh + 64, bg, 0:TW], self.rww[0:96, 2, p * 128 + hh * 64:p * 128 + hh * 64 + 64],
                         self.sub(LR2, (slice(None), slice(c0, c1))))
                yb = self.wv(4, TW, BF16)
                S.tt(yb, cen, self.ps[:, bg, 0:TW], ALU.mult)
                for m in range(8):
                    b = self.bank()
                    S.mm(self.ps[:, b, 0:TW], wo[:, 0, m * 128:(m + 1) * 128], yb)
                    self.resid(l, 2, m, c0, c1, self.ps[:, b, 0:TW], s)

    def odd(self, l, s):
        S = self.S
        j = l // 2
        win, wout = self.s_odin[j], self.s_odout[j]
        self.nrot = 4
        qa = self.sv(0, NT, BF16, 0, 65)
        ka = self.sv(2 * SLOT, NT, BF16, 0, 65)
        vtG = self.sv(4 * SLOT, 18 * 128)
        cosT = self.sv(6 * SLOT, 2048, F32, 0, 64)
        sinT = self.sv(10 * SLOT, 2048, F32, 0, 64)
        vrow = self.sv(14 * SLOT, 2048, BF16, 0, 64)
        vctx = self.sv(16 * SLOT, 128)
        tab = self.sv(17 * SLOT, 960, BF16, 0, 64)
        qn = self.sv(18 * SLOT, NT, BF16, 0, 64)
        negK = self.wv(9, 1, F32, 64, 65, c0=380)
        kmx = self.wv(9, 8, F32, 64, 65, c0=368)
        perm = self.cstb[0:64, 1728:1792]
        accn = [0]
        S.dma('sp', cosT, self.ropet[0])
        S.dma('act', sinT, self.ropet[1])

        def prep(dst, col0, gcol, rope, scale, is_q):
            w = self.ldw(win[:, :, col0:col0 + 64], 8, 64)
            for t in range(NTILE):
                c0, c1 = t * TW, (t + 1) * TW
                b = self.bank()
                for k in range(8):
                    S.mm(self.ps[0:64, b, 0:TW], w[:, k, :], self.hT[:, k, c0:c1], start=(k == 0), stop=(k == 7))
                pq = self.ps[0:64, b, 0:TW]
                qnt = self.sub(qn, (slice(None), slice(c0, c1)))
                if gcol is not None:
                    sq = self.wv(0, TW, BF16, 0, 64)
                    S.act(sq, pq, AF.Square)
                    b2 = self.bank()
                    S.mm(self.ps[0:64, b2, 0:TW], self.onesb(64, 64), sq)
                    rstd = self.wv(1, TW, F32, 0, 64)
                    S.act(rstd, self.ps[0:64, b2, 0:TW], AF.Sqrt, bias=self.eps(64), scale=1.0 / 64)
                    S.recip(rstd, rstd)
                    S.stt(qnt, pq, self.v64[:, gcol:gcol + 1], rstd, ALU.mult, ALU.mult)
                else:
                    S.copy(qnt, pq, eng='act')
                for (a, e, isc) in segs(c0, c1):
                    d_ = self.sub(dst, (slice(0, 64), slice(a, e)))
                    q_ = self.sub(qn, (slice(None), slice(a, e)))
                    if isc or not rope:
                        S.ts(d_, q_, scale, None, ALU.mult)
                    else:
                        n = e - a
                        b3 = self.bank()
                        S.mm(self.ps[0:64, b3, 0:n], perm, q_)
                        t1 = self.wv(2, n, F32, 0, 64)
                        t2 = self.wv(3, n, F32, 0, 64)
                        S.tt(t1, q_, self.sub(cosT, (slice(None), slice(a - CTX, e - CTX))), ALU.mult)
                        S.tt(t2, self.ps[0:64, b3, 0:n], self.sub(sinT, (slice(None), slice(a - CTX, e - CTX))), ALU.mult)
                        S.tt(t1, t1, t2, ALU.add)
                        S.ts(d_, t1, scale, None, ALU.mult)
                sq2 = self.wv(4, TW, BF16, 0, 64)
                dt_ = self.sub(dst, (slice(0, 64), slice(c0, c1)))
                S.tt(sq2, dt_, dt_, ALU.mult)
                b4 = self.bank()
                S.mm(self.ps[0:65, b4, 0:TW], self.onesb(64, 65), sq2)
                if is_q:
                    nq = self.wv(5, TW, F32, 64, 65)
                    S.act(nq, self.ps[64:65, b4, 0:TW], AF.Sqrt)
                    S.ts(self.sub(dst, (slice(64, 65), slice(c0, c1))), nq, negK, None, ALU.mult)
                else:
                    S.reduce(self.sub(kmx, (slice(None), slice(t, t + 1))), self.ps[64:65, b4, 0:TW], ALU.max)
            if not is_q:
                S.reduce(negK, self.sub(kmx, (slice(None), slice(0, NTILE))), ALU.max)
                S.act(negK, negK, AF.Sqrt)
                S.ts(negK, negK, -1.0, None, ALU.mult)
                S.memset(self.sub(dst, (slice(64, 65), slice(None))), 1.0)

        def accbanks():
            a = accn[0] % 2
            accn[0] += 1
            return 4 + 2 * a, 5 + 2 * a

        def finish(num, den, W):
            rden = self.wv2(2, W, F32, 0, 64)
            S.recip(rden, den)
            y = self.wv(4, W, BF16, 0, 64)
            S.tt(y, num, rden, ALU.mult)
            return y

        def dense(vfun, ktiles, q0, W):
            bn, bd = accbanks()
            num, den = self.ps[0:64, bn, 0:W], self.ps[0:64, bd, 0:W]
            last = len(ktiles) - 1
            accP = self.wv2(6, W, F32)

            def sc(i):
                kt = ktiles[i]
                b = self.bank()
                S.mm(self.ps[:, b, 0:W], self.sub(ka, (slice(0, 65), slice(kt * 128, (kt + 1) * 128))),
                     self.sub(qa, (slice(0, 65), slice(q0, q0 + W))))
                return b
            bcur = sc(0)
            for i, kt in enumerate(ktiles):
                bnext = sc(i + 1) if i < last else None
                pT = self.wv(i % 2, W, BF16)
                S.act(pT, self.ps[:, bcur, 0:W], AF.Exp)
                S.mm(num, vfun(kt), pT, start=(i == 0), stop=(i == last))
                if i == 0:
                    S.copy(accP, pT, eng='pool')
                else:
                    S.tt(accP, accP, pT, ALU.add, eng='pool')
                bcur = bnext
            S.mm(den, self.cst[0:64, 128:192], self.sub(accP, (slice(0, 64), slice(None))), start=True, stop=False)
            S.mm(den, self.cst[64:128, 192:256], self.sub(accP, (slice(64, 128), slice(None))), start=False, stop=True)
            return finish(num, den, W)

        def ld_wo(hrow):
            st = self.stage[self.stn % 3]
            self.stn += 1
            dst = st[0:64, 0:D]
            S.dma('sp', dst, wout[(hrow % 2) * 64:(hrow % 2) * 64 + 64, hrow // 2, :])
            return dst

        def apply_wout(y, W, q0, wo):
            for m in range(8):
                b = self.bank()
                S.mm(self.ps[:, b, 0:W], self.sub(wo, (slice(None), slice(m * 128, (m + 1) * 128))), y)
                self.resid(l, 2, m, q0, q0 + W, self.ps[:, b, 0:W], s)

        wv = self.ldw(win[:, :, 640:768], 8, 128)
        for c in range(NCH):
            b = self.bank()
            for k in range(8):
                S.mm(self.ps[:, b, 0:128], self.hT[:, k, c * 128:(c + 1) * 128], wv[:, k, :], start=(k == 0), stop=(k == 7))
            S.copy(self.sub(vtG, (slice(None), slice(c * 128, (c + 1) * 128))), self.ps[:, b, 0:128], eng='act' if c % 2 else 'dve')
        for kv in range(2):
            prep(ka, 512 + kv * 64, 142 + j, True, 1.0, False)
            for hq in range(4):
                h = kv * 4 + hq
                prep(qa, h * 64, 140 + j, True, 0.125, True)
                wo = ld_wo(h)
                vf = lambda kt, kv=kv: self.sub(vtG, (slice(None), slice(kt * 128 + kv * 64, kt * 128 + kv * 64 + 64)))
                y = dense(vf, [0, 1], 0, CTX)
                apply_wout(y, CTX, 0, wo)
                for qb_ in range(4):
                    q0 = CTX + qb_ * 512
                    y = dense(vf, list(range(NCH)), q0, 512)
                    apply_wout(y, 512, q0, wo)
        tok = lambda r: CTX + r * 64
        tab3 = tab.re(lambda a: a.rearrange("p (d q) -> p d q", d=15))
        for h in range(8):
            prep(ka, 1280 + h * 64, None, False, 1.0, False)
            prep(qa, 768 + h * 64, None, False, 0.125, True)
            wv = self.ldw(win[:, :, 1792 + h * 64:1792 + (h + 1) * 64], 8, 64)
            for r in range(32):
                b = self.bank()
                for k in range(8):
                    S.mm(self.ps[0:64, b, 0:64], self.hT[:, k, tok(r):tok(r) + 64], wv[:, k, :], start=(k == 0), stop=(k == 7))
                S.copy(self.sub(vrow, (slice(None), slice(r * 64, (r + 1) * 64))), self.ps[0:64, b, 0:64], eng='act' if r % 2 else 'dve')
            for ct in range(2):
                b = self.bank()
                for k in range(8):
                    S.mm(self.ps[:, b, 0:64], self.hT[:, k, ct * 128:(ct + 1) * 128], wv[:, k, :], start=(k == 0), stop=(k == 7))
                S.copy(self.sub(vctx, (slice(None), slice(ct * 64, (ct + 1) * 64))), self.ps[:, b, 0:64])
            S.dma('act', tab, self.s_tab[j][:, h, :])
            wo = ld_wo(8 + h)
            y = dense(lambda kt: self.sub(vctx, (slice(None), slice(kt * 64, (kt + 1) * 64))), [0, 1], 0, CTX)
            apply_wout(y, CTX, 0, wo)
            for r0 in range(0, 32, 8):
                bn, bd = accbanks()

                def scores(r):
                    rs = min(max(r - 4, 0), 24)
                    qv = self.sub(qa, (slice(0, 65), slice(tok(r), tok(r) + 64)))
                    b = self.bank()
                    for i in range(8):
                        S.mm(self.ps[0:64, b, i * 64:(i + 1) * 64], self.sub(ka, (slice(0, 65), slice(tok(rs + i), tok(rs + i) + 64))), qv)
                    b2 = self.bank()
                    for ct in range(2):
                        S.mm(self.ps[:, b2, ct * 64:(ct + 1) * 64], self.sub(ka, (slice(0, 65), slice(ct * 128, (ct + 1) * 128))), qv)
                    return b, b2
                nxt = scores(r0)
                for r in range(r0, r0 + 8):
                    b, b2 = nxt
                    rs = min(max(r - 4, 0), 24)
                    dr0 = rs - r + 7
                    sb = self.wv2(6, 512, F32, 0, 64)
                    S.tt(sb.re(lambda a: a.rearrange("p (i q) -> p i q", i=8)),
                         self.ps[0:64, b, 0:512].re(lambda a: a.rearrange("p (i q) -> p i q", i=8)),
                         View(tab3.ap[:, dr0:dr0 + 8, ::-1], tab.buf, tab.b0, tab.b1), ALU.add)
                    pT = self.wv(r % 2, 512, BF16, 0, 64)
                    S.act(pT, sb, AF.Exp)
                    pTc = self.wv(8 + r % 2, 128, BF16)
                    S.act(pTc, self.ps[:, b2, 0:128], AF.Exp)
                    if r + 1 < r0 + 8:
                        nxt = scores(r + 1)
                    cs = slice((r - r0) * 64, (r - r0 + 1) * 64)
                    out = self.ps[0:64, bn, cs]
                    for i in range(8):
                        S.mm(out, self.sub(vrow, (slice(None), slice((rs + i) * 64, (rs + i + 1) * 64))),
                             self.sub(pT, (slice(None), slice(i * 64, (i + 1) * 64))), start=(i == 0), stop=False)
                    for ct in range(2):
                        S.mm(out, self.sub(vctx, (slice(None), slice(ct * 64, (ct + 1) * 64))),
                             self.sub(pTc, (slice(None), slice(ct * 64, (ct + 1) * 64))), start=False, stop=(ct == 1))
                    ps8 = self.wv(5, 64, F32, 0, 64, c0=(r % 2) * 192)
                    S.reduce(ps8, pT.re(lambda a: a.rearrange("p (i q) -> p q i", i=8)), ALU.add)
                    pc = self.wv(5, 64, F32, 0, 128, c0=(r % 2) * 192 + 64)
                    S.tt(pc, self.sub(pTc, (slice(None), slice(0, 64))), self.sub(pTc, (slice(None), slice(64, 128))), ALU.add)
                    dout = self.ps[0:64, bd, cs]
                    S.mm(dout, self.cst[0:64, 128:192], ps8, start=True, stop=False)
                    S.mm(dout, self.cst[0:64, 128:192], self.sub(pc, (slice(0, 64), slice(None))), start=False, stop=False)
                    S.mm(dout, self.cst[64:128, 192:256], self.sub(pc, (slice(64, 128), slice(None))), start=False, stop=True)
                y = finish(self.ps[0:64, bn, 0:512], self.ps[0:64, bd, 0:512], 512)
                apply_wout(y, 512, tok(r0), wo)
        self.nrot = 8


def make_consts():
    c = np.zeros((128, 1792), np.float32)
    j = np.arange(128)[:, None]; i = np.arange(128)[None, :]
    c[:, C_ID:C_ID + 128] = (j == i)
    c[:, C_TRF:C_TRF + 128] = (j <= i)
    c[:, C_TRB:C_TRB + 128] = (j >= i)
    c[:, C_ONE:C_ONE + 128] = 1.0
    c[:, C_STF:C_STF + 128] = (j < i)
    c[:, C_STB:C_STB + 128] = (j > i)
    c[:, C_BLK:C_BLK + 128] = ((j // 64) == (i // 64))
    c[:, 1024:1152] = ((j // 8) == (i // 8))
    for n_, b_ in enumerate((8, 16, 32, 64)):
        c[:, 1152 + 128 * n_:1280 + 128 * n_] = ((j // (2 * b_)) == (i // (2 * b_))) & ((j // b_) != (i // b_))
    qc = 63 - np.arange(64)[None, :]; kc = np.arange(64)[:, None]
    start = np.clip(qc - 8, 0, 48)
    c[0:64, 1664:1728] = np.where((kc >= start) & (kc < start + 16), 0.0, -30000.0)
    for dst in range(64):
        a_, b_, f_ = dst // 32, (dst % 32) // 16, dst % 16
        if b_ == 0:
            c[dst + 16, 1728 + dst] = -1.0
        else:
            c[dst - 16, 1728 + dst] = 1.0
    c[:, 896] = 1e-6
    c[:, 897] = 1.0
    c[:, 898] = 64e-5
    c[:, 899] = 1e-12
    return c

def pack_common(inp):
    f = lambda k: np.asarray(inp[k], np.float32)
    v32 = np.zeros((384, 128), np.float32)
    v32[R_BMOD:R_BMOD + 192] = f("b_mod").reshape(192, 128)
    v32[R_NG:R_NG + 32] = f("norm1_g").reshape(32, 128)
    v32[R_NG + 32:R_NG + 64] = f("norm2_g").reshape(32, 128)
    v32[R_FG:R_FG + 8] = f("final_g").reshape(8, 128)
    v32[R_GLAG:R_GLAG + 2] = f("gla_norm_g")
    v64 = np.zeros((64, 256), np.float32)
    ab = f("gla_a_bias")
    for j in range(2):
        for d in range(2):
            for h in range(4):
                v64[:, j * 128 + d * 4 + h] = ab[j, d, h * 64:(h + 1) * 64]
    vrw = np.zeros((128, 256), np.float32)
    mu = f("rw_mu")
    for j in range(2):
        b = j * 128
        for p in range(4):
            for q in range(3):
                vrw[:, b + q * 8 + p] = mu[j, 0, q * 512 + p * 128:q * 512 + (p + 1) * 128]
                vrw[:, b + q * 8 + 4 + p] = mu[j, 1, q * 512 + p * 128:q * 512 + (p + 1) * 128]
            for d in range(2):
                vrw[:, b + 28 + d * 4 + p] = f("rw_w0")[j, d, p * 128:(p + 1) * 128]
            vrw[:, b + 36 + p] = f("rw_a0")[j, p * 128:(p + 1) * 128]
            vrw[:, b + 40 + p] = f("rw_k_k")[j, p * 128:(p + 1) * 128]
            vrw[:, b + 44 + p] = f("rw_k_a")[j, p * 128:(p + 1) * 128]
            vrw[:, b + 48 + p] = f("rw_r_k")[j].reshape(512)[p * 128:(p + 1) * 128]
            vrw[:, b + 52 + p] = f("rw_ln_g")[j, p * 128:(p + 1) * 128]
            vrw[:, b + 56 + p] = f("rw_ln_b")[j, p * 128:(p + 1) * 128]
        vrw[0:96, b + 24] = mu[j, 0, 1536:1632]; vrw[0:96, b + 25] = mu[j, 1, 1536:1632]
        vrw[0:96, b + 26] = mu[j, 0, 1632:1728]; vrw[0:96, b + 27] = mu[j, 1, 1632:1728]
    v64[:, 140:142] = f("cq_norm_g").T
    v64[:, 142:144] = f("ck_norm_g").T
    rp = np.zeros((2, 120, 160), np.float32)
    rp[:, :, 64:95] = f("na_rpb").reshape(2, 120, 31)
    t_ = np.arange(2048)
    pos = np.stack([t_ // 64, t_ % 64], axis=-1).astype(np.float32)
    inv = (np.float32(10000.0) ** (-np.arange(0, 32, 2, dtype=np.float32) / np.float32(32))).astype(np.float32)
    ang = pos[:, :, None] * inv
    rope = np.zeros((2, 64, 2048), np.float32)
    for d_ in range(64):
        a_, f_ = d_ // 32, d_ % 16
        rope[0, d_] = np.cos(ang[:, a_, f_]); rope[1, d_] = np.sin(ang[:, a_, f_])
    rmask = np.ones((1, NT), np.float32); rmask[0, ::128] = 0.0
    com = {"w_mod": f("w_mod"), "vec32": v32, "ffn_w13": f("ffn_w13"), "ffn_w2": f("ffn_w2"),
           "ev_w_in": f("ev_w_in"), "ev_w_out": f("ev_w_out"), "od_w_in": f("od_w_in"), "od_w_out": f("od_w_out"),
           "gla_a_up": f("gla_a_up"), "rw_w_up": f("rw_w_up"), "rw_a_up": f("rw_a_up"), "rw_g_up": f("rw_g_up"),
           "vec64": v64, "vecrw": vrw, "rpbp": rp, "ropet": rope, "consts": make_consts(), "rmask": rmask}
    return com

def pack_core(inp, com, seqs):
    f = lambda k: np.asarray(inp[k], np.float32)
    m = dict(com)
    m["x"] = np.ascontiguousarray(f("x")[seqs])
    m["ctx"] = np.ascontiguousarray(f("ctx")[seqs])
    cc = np.zeros((8, D), np.float32)
    cc[:len(seqs)] = f("c")[seqs]
    cc[4] = f("c_ctx")
    m["cc"] = cc
    return m


PARTS = ('gla', 'rw', 'odd', 'ffn')


def kernel(**inputs):
    from concourse.bass_utils import run_bass_kernel_spmd
    ncores, nseq = 8, 4
    nc = bass.Bass("TRN2", target_bir_lowering=False)
    MK(nc, nseq=nseq, layers=(0, 1, 2, 3), final=True, parts=PARTS)
    com = pack_common(inputs)
    in_maps = [pack_core(inputs, com, list(range(c * nseq, (c + 1) * nseq))) for c in range(ncores)]
    res = run_bass_kernel_spmd(nc, in_maps, core_ids=list(range(ncores)))
    return np.concatenate([np.asarray(r["out"], np.float32) for r in res.results], axis=0)
```

```python
import numpy as np
import concourse.bass as bass
import concourse.mybir as mybir

F32 = mybir.dt.float32
BF16 = mybir.dt.bfloat16
ALU = mybir.AluOpType
AF = mybir.ActivationFunctionType
AX = mybir.AxisListType

SEG = 30000
NDMASEM = 24


class Buf:
    _n = 0

    def __init__(self, handle, shape, kind, blk=None):
        self.h = handle
        self.shape = list(shape)
        self.kind = kind
        Buf._n += 1
        self.id = Buf._n
        self.bdim = 1 if kind != 'dr' else 0
        if len(shape) <= self.bdim:
            self.bdim = 0
        n = shape[self.bdim]
        self.blk = blk if blk else 1
        self.nblk = (n + self.blk - 1) // self.blk
        if kind != 'dr' and len(shape) == 2 and blk is None:
            self.blk = n
            self.nblk = 1

    def full_ap(self):
        return self.h.ap() if hasattr(self.h, 'ap') and callable(getattr(self.h, 'ap')) else self.h[:]

    def __getitem__(self, key):
        if not isinstance(key, tuple):
            key = (key,)
        ap = self.h[key] if len(key) > 1 else self.h[key[0]]
        lo, hi = 0, self.shape[self.bdim]
        if len(key) > self.bdim:
            k = key[self.bdim]
            if isinstance(k, slice):
                lo = 0 if k.start is None else k.start
                hi = self.shape[self.bdim] if k.stop is None else k.stop
            else:
                lo, hi = int(k), int(k) + 1
        b0, b1 = lo // self.blk, (hi - 1) // self.blk + 1
        return View(ap, self, b0, b1)

    def all(self):
        return self[tuple(slice(None) for _ in self.shape)]


class View:
    def __init__(self, ap, buf, b0, b1):
        self.ap, self.buf, self.b0, self.b1 = ap, buf, b0, b1

    def blocks(self):
        return [(self.buf.id, b) for b in range(self.b0, self.b1)]

    def re(self, fn):
        return View(fn(self.ap), self.buf, self.b0, self.b1)

    def __getitem__(self, key):
        return View(self.ap[key], self.buf, self.b0, self.b1)


class Sched:
    ENGS = ['pe', 'act', 'dve', 'pool', 'sp']

    def __init__(self, nc):
        self.nc = nc
        self.q = {e: [] for e in self.ENGS}
        self.cnt = {e: 0 for e in self.ENGS}
        self.lastw = {}
        self.readers = {}
        self.seen = {e: {s: -1 for s in self.ENGS} for e in self.ENGS}
        self.seen_dma = {e: set() for e in self.ENGS}
        self.ndma = 0
        self.dma_sem_last = [None] * NDMASEM
        self.dma_sem_cnt = [0] * NDMASEM
        self.nsb = 0

    def sb(self, name, shape, dtype, blk=None):
        h = self.nc.alloc_sbuf_tensor(name, list(shape), dtype)
        return Buf(h, shape, 'sb', blk)

    def ps(self, name, shape, dtype=F32, blk=None):
        h = self.nc.alloc_psum_tensor(name, list(shape), dtype)
        return Buf(h, shape, 'ps', blk)

    def dram(self, name, shape, dtype, kind="Internal", blk=None):
        h = self.nc.dram_tensor(name, list(shape), dtype, kind=kind)
        return Buf(h, shape, 'dr', blk)

    def _deps(self, reads, writes):
        deps = []
        for v in reads:
            for b in v.blocks():
                t = self.lastw.get(b)
                if t is not None:
                    deps.append(t)
        for v in writes:
            for b in v.blocks():
                t = self.lastw.get(b)
                if t is not None:
                    deps.append(t)
                deps.extend(self.readers.get(b, ()))
        return deps

    def _record(self, tok, reads, writes):
        for v in reads:
            for b in v.blocks():
                lst = self.readers.setdefault(b, [])
                if tok[0] == 'c':
                    lst[:] = [t for t in lst if not (t[0] == 'c' and t[1] == tok[1])]
                lst.append(tok)
        for v in writes:
            for b in v.blocks():
                self.lastw[b] = tok
                self.readers[b] = []

    def _waits(self, eng, deps):
        waits = []
        best = {}
        for t in deps:
            if t[0] == 'c':
                _, src, idx = t
                if idx <= self.seen[eng][src]:
                    continue
                if idx > best.get(src, -1):
                    best[src] = idx
            else:
                _, did, sem, val = t
                if did in self.seen_dma[eng]:
                    continue
                self.seen_dma[eng].add(did)
                waits.append(('d', sem, val))
        for src, idx in best.items():
            self.seen[eng][src] = idx
            waits.append(('c', src, idx))
        return waits

    def op(self, eng, fn, reads=(), writes=()):
        reads = [v for v in reads if isinstance(v, View)]
        writes = [v for v in writes if isinstance(v, View)]
        deps = self._deps(reads, writes)
        waits = self._waits(eng, deps)
        idx = self.cnt[eng]
        self.cnt[eng] += 1
        tok = ('c', eng, idx)
        self.q[eng].append((fn, waits, ('c', eng, idx)))
        self._record(tok, reads, writes)
        return tok

    def dma(self, eng, out, in_, **kw):
        s = self.ndma % NDMASEM
        deps = self._deps([in_], [out])
        if self.dma_sem_last[s] is not None:
            deps.append(self.dma_sem_last[s])
        waits = self._waits(eng, deps)
        self.dma_sem_cnt[s] += 1
        tok = ('d', self.ndma, s, 16 * self.dma_sem_cnt[s])
        self.dma_sem_last[s] = tok
        self.ndma += 1
        oa, ia = out.ap, in_.ap

        def fn(e, oa=oa, ia=ia, kw=kw):
            return e.dma_start(out=oa, in_=ia, **kw)
        self.q[eng].append((fn, waits, tok))
        self._record(tok, [in_], [out])
        return tok

    def wait_all(self, eng):
        deps = []
        for e in self.ENGS:
            if self.cnt[e] > 0:
                deps.append(('c', e, self.cnt[e] - 1))
        for s in range(NDMASEM):
            if self.dma_sem_last[s] is not None:
                deps.append(self.dma_sem_last[s])
        waits = self._waits(eng, deps)
        self.q[eng].append((None, waits, None))

    def emit(self):
        nc = self.nc
        csem = {}
        for e in self.ENGS:
            nseg = (self.cnt[e] + SEG - 1) // SEG
            csem[e] = [nc.alloc_semaphore(f"c_{e}_{i}") for i in range(nseg)]
        dsem = [nc.alloc_semaphore(f"d_{i}") for i in range(NDMASEM)]
        engobj = {'pe': 'tensor', 'act': 'scalar', 'dve': 'vector', 'pool': 'gpsimd', 'sp': 'sync'}

        def run(ename):
            def body(engine):
                for fn, waits, sig in self.q[ename]:
                    for w in waits:
                        if w[0] == 'c':
                            _, src, idx = w
                            engine.wait_ge(csem[src][idx // SEG], idx % SEG + 1)
                        else:
                            _, sem, val = w
                            engine.wait_ge(dsem[sem], val)
                    if fn is None:
                        continue
                    ins = fn(engine)
                    if sig[0] == 'c':
                        _, src, idx = sig
                        ins.then_inc(csem[src][idx // SEG], 1)
                    else:
                        _, did, sem, val = sig
                        ins.then_inc(dsem[sem], 16)
            return body

        with nc.Block() as block:
            for e in self.ENGS:
                if not self.q[e]:
                    continue
                getattr(block, engobj[e])(run(e))

    def mm(self, out, lhsT, rhs, start=True, stop=True, **kw):
        o, l, r = out.ap, lhsT.ap, rhs.ap
        reads = [lhsT, rhs] + ([] if start else [])
        return self.op('pe', lambda e: e.matmul(o, l, r, start=start, stop=stop, **kw),
                       reads, [out])

    def tr(self, out, in_, ident):
        o, i, d = out.ap, in_.ap, ident.ap
        return self.op('pe', lambda e: e.transpose(o, i, d), [in_, ident], [out])

    def act(self, out, in_, func, bias=0.0, scale=1.0, accum_out=None, eng='act'):
        o, i = out.ap, in_.ap
        b = bias.ap if isinstance(bias, View) else bias
        s = scale.ap if isinstance(scale, View) else scale
        kw = {}
        wr = [out]
        if accum_out is not None:
            kw['accum_out'] = accum_out.ap
            wr.append(accum_out)
        return self.op('act', lambda e: e.activation(o, i, func, bias=b, scale=s, **kw),
                       [in_, bias, scale], wr)

    def tt(self, out, a, b, op, eng='dve'):
        o, x, y = out.ap, a.ap, b.ap
        return self.op(eng, lambda e: e.tensor_tensor(o, x, y, op), [a, b], [out])

    def ts(self, out, a, s1, s2, op0, op1=None, eng='dve', accum_out=None):
        o, x = out.ap, a.ap
        p1 = s1.ap if isinstance(s1, View) else s1
        p2 = s2.ap if isinstance(s2, View) else s2
        kw = {}
        wr = [out]
        if accum_out is not None:
            kw['accum_out'] = accum_out.ap
            wr.append(accum_out)
        if op1 is None:
            return self.op(eng, lambda e: e.tensor_scalar(o, x, p1, None, op0, **kw), [a, s1], wr)
        return self.op(eng, lambda e: e.tensor_scalar(o, x, p1, p2, op0, op1, **kw), [a, s1, s2], wr)

    def stt(self, out, a, s, b, op0, op1):
        o, x, y = out.ap, a.ap, b.ap
        p = s.ap if isinstance(s, View) else s
        return self.op('dve', lambda e: e.scalar_tensor_tensor(o, x, p, y, op0, op1), [a, s, b], [out])

    def copy(self, out, in_, eng='dve'):
        o, i = out.ap, in_.ap
        if eng == 'act':
            return self.op('act', lambda e: e.copy(o, i), [in_], [out])
        return self.op(eng, lambda e: e.tensor_copy(o, i), [in_], [out])

    def memset(self, out, val, eng='dve'):
        o = out.ap
        return self.op(eng, lambda e: e.memset(o, val), [], [out])

    def recip(self, out, in_):
        o, i = out.ap, in_.ap
        return self.op('dve', lambda e: e.reciprocal(o, i), [in_], [out])

    def reduce(self, out, in_, op, axis=AX.X):
        o, i = out.ap, in_.ap
        return self.op('dve', lambda e: e.tensor_reduce(o, i, axis, op), [in_], [out])

    def scan(self, out, d0, d1, init, op0, op1):
        o, a, b = out.ap, d0.ap, d1.ap
        ini = init.ap if isinstance(init, View) else init
        return self.op('dve', lambda e: e.tensor_tensor_scan(o, a, b, ini, op0, op1), [d0, d1, init], [out])


D = 1024
NT = 2304
CTX = 256
TW = 384
NTILE = NT // TW
CH = 128
NCH = NT // CH
DFF = 2816
EVEN_IN = 3296
ODD_IN = 2304
SLOT = 1152
NSLOT = 20
FT = 768
R_BMOD, R_NG, R_FG, R_GLAG = 0, 192, 256, 264
C_ID, C_TRF, C_TRB, C_ONE, C_STF, C_STB, C_BLK, C_EPS, C_1 = 0, 128, 256, 384, 512, 640, 768, 896, 897
KAPPA = float(np.exp(-0.5))
RW0 = 1568


def segs(c0, c1):
    out = []
    if c0 < CTX:
        out.append((c0, min(c1, CTX), True))
    if c1 > CTX:
        out.append((max(c0, CTX), c1, False))
    return out


class MK:
    def __init__(self, nc, nseq=4, layers=(0, 1, 2, 3), final=True, parts=('gla', 'rw', 'odd', 'ffn'), debug=False):
        self.nc = nc
        self.debug = debug
        self.dbgn = 0
        self.S = Sched(nc)
        self.nseq, self.layers, self.final, self.parts = nseq, layers, final, parts
        self.psn = 0
        self.stn = 0
        self.decl_inputs()
        self.alloc()
        self.prologue()
        for s in range(nseq):
            self.load_seq(s)
            for l in layers:
                self.layer(s, l)
            self.store_seq(s)
        self.S.wait_all('sp')
        self.S.emit()

    def decl_inputs(self):
        S, ns = self.S, self.nseq
        I = lambda n, sh, dt=F32: S.dram(n, sh, dt, kind="ExternalInput")
        self.x = I("x", [ns, 2048, D])
        self.ctx = I("ctx", [ns, CTX, D])
        self.cc = I("cc", [8, D])
        self.w_mod = I("w_mod", [4, D, 6 * D])
        self.vec32 = I("vec32", [384, 128])
        self.ffn_w13 = I("ffn_w13", [4, D, 2 * DFF])
        self.ffn_w2 = I("ffn_w2", [4, DFF, D])
        self.ev_w_in = I("ev_w_in", [2, D, EVEN_IN])
        self.ev_w_out = I("ev_w_out", [2, D, D])
        self.od_w_in = I("od_w_in", [2, D, ODD_IN])
        self.od_w_out = I("od_w_out", [2, D, D])
        self.gla_a_up = I("gla_a_up", [2, 2, 16, 256])
        self.rw_w_up = I("rw_w_up", [2, 2, 32, 512])
        self.rw_a_up = I("rw_a_up", [2, 32, 512])
        self.rw_g_up = I("rw_g_up", [2, 96, 512])
        self.vec64 = I("vec64", [64, 256])
        self.consts = I("consts", [128, 1792])
        self.rpbp = I("rpbp", [2, 120, 160])
        self.ropet = I("ropet", [2, 64, 2048])
        self.s_tab = [S.dram(f"s_tab_{j}", [64, 8, 960], BF16) for j in range(2)]
        self.rmask = I("rmask", [1, NT])
        self.vecrw = I("vecrw", [128, 256])
        self.rwp = S.dram("rwp", [5, 128, NT], BF16, kind=("ExternalOutput" if self.debug else "Internal"))
        self.out = S.dram("out", [ns, 2048, D], F32, kind="ExternalOutput")
        Wd = lambda n, nk, nc_: S.dram(n, [128, nk, nc_], BF16)
        self.s_w13 = [Wd(f"s_w13_{l}", 8, 2 * DFF) for l in range(4)]
        self.s_w2 = [Wd(f"s_w2_{l}", 22, D) for l in range(4)]
        self.s_evin = [Wd(f"s_evin_{j}", 8, EVEN_IN) for j in range(2)]
        self.s_evout = [Wd(f"s_evout_{j}", 8, D) for j in range(2)]
        self.s_odin = [Wd(f"s_odin_{j}", 8, ODD_IN) for j in range(2)]
        self.s_odout = [Wd(f"s_odout_{j}", 8, D) for j in range(2)]

    def alloc(self):
        S = self.S
        self.xT = S.sb("xT", [128, 8, NT], F32)
        self.hT = S.sb("hT", [128, 8, NT], BF16)
        self.ps = S.ps("ps", [128, 8, 512], F32)
        self.cst = S.sb("cst", [128, 328], F32, blk=328)
        self.cstb = S.sb("cstb", [128, 1792], BF16, blk=1792)
        self.modT = S.sb("modT", [128, 192, 5], F32, blk=192)
        self.modA = S.sb("modA", [128, 64, 5], F32, blk=64)
        self.vT = S.sb("vT", [128, 384], F32, blk=384)
        self.v64 = S.sb("v64", [64, 256], F32, blk=256)
        self.stage = [S.sb(f"stage{i}", [128, 3072], BF16, blk=3072) for i in range(3)]
        self.work = S.sb("work", [128, 10, TW], F32)
        self.scr = S.sb("scr", [128, NSLOT * SLOT], BF16, blk=SLOT)
        self.rm = S.sb("rm", [128, NT], BF16, blk=NT)
        self.rww = S.sb("rww", [128, 3, 512], BF16)
        self.vrw = S.sb("vrw", [128, 256], F32, blk=256)
        self.vd = S.sb("vd", [128, 128], F32, blk=128)

    def dump(self, name, view, shape, dt=F32):
        if not self.debug:
            return
        d = self.S.dram("dbg_" + name, list(shape), dt, kind="ExternalOutput")
        self.S.dma('sp', d.all(), view)

    nrot = 8

    def wv2(self, row, n, dt=F32, p0=0, p1=128):
        ap = self.work.h[p0:p1, row:row + 2, :].rearrange("p a t -> p (a t)")
        if dt == BF16:
            ap = ap.bitcast(BF16)
        return View(ap[:, 0:n], self.work, row, row + 2)

    def bank(self):
        b = self.psn % self.nrot
        self.psn += 1
        return b

    def sv(self, e0, n, dt=BF16, p0=0, p1=128):
        nb = n * (2 if dt == F32 else 1)
        ap = self.scr.h[p0:p1, e0:e0 + nb]
        if dt == F32:
            ap = ap.bitcast(F32)
        return View(ap, self.scr, e0 // SLOT, (e0 + nb - 1) // SLOT + 1)

    def wv(self, row, n=TW, dt=F32, p0=0, p1=128, c0=0):
        if dt == F32:
            ap = self.work.h[p0:p1, row, c0:c0 + n]
        else:
            ap = self.work.h[p0:p1, row, c0:c0 + (n + 1) // 2].bitcast(BF16)[:, 0:n]
        return View(ap, self.work, row, row + 1)

    def sub(self, v, key):
        return View(v.ap[key], v.buf, v.b0, v.b1)

    def ldw(self, src_view, nk, ncols, eng='sp'):
        st = self.stage[self.stn % 3]
        self.stn += 1
        dst = st[:, 0:nk * ncols].re(lambda a: a.rearrange("p (k c) -> p k c", k=nk))
        self.S.dma(eng, dst, src_view)
        return dst

    def ident(self, n=128):
        return self.cst[0:n, 0:n]

    def identb(self, n=128):
        return self.cstb[0:n, 0:n]

    def onesb(self, k=128, m=128):
        return self.cstb[0:k, C_ONE:C_ONE + m]

    def trib(self, d):
        return self.cstb[:, C_TRF + 128 * d:C_TRF + 128 * d + 128]

    def tris(self, d):
        return self.cstb[:, C_STF + 128 * d:C_STF + 128 * d + 128]

    def eps(self, p=128):
        return self.cst[0:p, 256:257]

    def one(self, p=128):
        return self.cst[0:p, 257:258]

    def gneps(self, p=128):
        return self.cst[0:p, 258:259]

    def blk32(self):
        return self.cst[:, 128:256]

    def blkb(self):
        return self.cstb[:, C_BLK:C_BLK + 128]

    def bmask(self, col, n):
        return self.cstb[:, col:col + 128].re(lambda a: a.rearrange("p (o t) -> p o t", o=1).broadcast_to([128, n, 128]))

    def cast_w(self, dst, src_view, ncols):
        for c0 in range(0, ncols, 1024):
            c1 = min(ncols, c0 + 1024)
            sv = src_view[:, c0:c1].re(lambda a: a.rearrange("(k p) c -> p k c", p=128))
            self.S.dma('pool', dst[:, :, c0:c1], sv)

    def prologue(self):
        S = self.S
        cf = self.sv(0, 1792, F32)
        S.dma('sp', cf, self.consts.all())
        S.copy(self.cstb.all(), cf)
        S.copy(self.cst[:, 0:128], self.sub(cf, (slice(None), slice(0, 128))))
        S.copy(self.cst[:, 128:256], self.sub(cf, (slice(None), slice(C_BLK, C_BLK + 128))))
        S.copy(self.cst[:, 256:264], self.sub(cf, (slice(None), slice(896, 904))))
        S.copy(self.cst[:, 264:328], self.sub(cf, (slice(None), slice(1664, 1728))))
        S.dma('sp', self.vrw.all(), self.vecrw.all())
        for jj in range(2):
            b0, b2 = jj * 128, jj * 64
            for (cp, cn, co, w) in ((0, 4, 0, 4), (8, 12, 4, 4), (16, 20, 8, 4), (24, 25, 12, 1), (26, 27, 13, 1)):
                S.stt(self.vd[:, b2 + co:b2 + co + w], self.vrw[:, b0 + cp:b0 + cp + w], -1.0, self.vrw[:, b0 + cn:b0 + cn + w], ALU.mult, ALU.subtract)
                S.ts(self.vd[:, b2 + co:b2 + co + w], self.vd[:, b2 + co:b2 + co + w], 1.0, None, ALU.add)
            S.ts(self.vd[:, b2 + 16:b2 + 20], self.vrw[:, b0 + 44:b0 + 48], -1.0, 1.0, ALU.mult, ALU.add)
        S.dma('sp', self.v64.all(), self.vec64.all())
        S.ts(self.v64[:, 0:8], self.v64[:, 0:8], -1.0, None, ALU.mult)
        S.ts(self.v64[:, 128:136], self.v64[:, 128:136], -1.0, None, ALU.mult)
        rmf = self.sv(4 * SLOT, NT, F32)
        S.dma('sp', rmf, self.rmask.all().re(lambda a: a.broadcast_to([128, NT])))
        S.copy(self.rm.all(), rmf)
        for q in range(3):
            tmp = self.wv(q, 128)
            S.dma('sp', tmp, self.vec32[q * 128:(q + 1) * 128, :])
            b = self.bank()
            S.tr(self.ps[:, b, 0:128], tmp, self.ident())
            S.copy(self.vT[:, q * 128:(q + 1) * 128], self.ps[:, b, 0:128])
        used = sorted(set(self.layers))
        for l in used:
            j = l // 2
            if l % 2 == 0:
                self.cast_w(self.s_evin[j], self.ev_w_in[j], EVEN_IN)
                self.cast_w(self.s_evout[j], self.ev_w_out[j], D)
            else:
                self.cast_w(self.s_odin[j], self.od_w_in[j], ODD_IN)
                self.cast_w(self.s_odout[j], self.od_w_out[j], D)
            self.cast_w(self.s_w13[l], self.ffn_w13[l], 2 * DFF)
            self.cast_w(self.s_w2[l], self.ffn_w2[l], D)
        for l in used:
            if l % 2 == 1:
                jj = l // 2
                for half in range(2):
                    tabf = self.sv(0, 60 * 64, F32, 0, 64)
                    tabb = self.sv(8 * SLOT, 60 * 64, BF16, 0, 64)
                    src = View(bass.AP(self.rpbp.h, jj * 120 * 160 + half * 60 * 160 + 16, [[1, 64], [160, 60], [1, 64]]),
                               self.rpbp, 0, 2)
                    S.dma('sp', tabf.re(lambda a: a.rearrange("p (n q) -> p n q", n=60)), src)
                    S.tt(tabb.re(lambda a: a.rearrange("p (n q) -> p n q", n=60)), tabf.re(lambda a: a.rearrange("p (n q) -> p n q", n=60)),
                         self.cst[0:64, 264:328].re(lambda a: a.rearrange("p (o q) -> p o q", o=1).broadcast_to([64, 60, 64])), ALU.add)
                    S.dma('sp', self.s_tab[jj][:, half * 4:half * 4 + 4, :].re(lambda a: a.rearrange("p h q -> p (h q)")), tabb)
        ccv = self.sv(16 * SLOT, D, F32, 0, 8)
        S.dma('sp', ccv, self.cc.all())
        S.act(ccv, ccv, AF.Silu)
        scT = self.wv(4, 64)
        for k in range(8):
            b = self.bank()
            S.tr(self.ps[:, b, 0:8], self.sub(ccv, (slice(None), slice(k * 128, (k + 1) * 128))), self.ident(8))
            S.copy(self.sub(scT, (slice(None), slice(k * 8, (k + 1) * 8))), self.ps[:, b, 0:8])
        for l in used:
            for piece in range(12):
                wv = self.sv((piece % 2) * 8 * SLOT, 8 * 512, F32).re(lambda a: a.rearrange("p (k c) -> p k c", k=8))
                S.dma('sp' if piece % 2 else 'act', wv, self.w_mod[l][:, piece * 512:(piece + 1) * 512].re(
                    lambda a: a.rearrange("(k p) c -> p k c", p=128)))
                for mm in range(4):
                    wm = piece * 4 + mm
                    b = self.bank()
                    for k in range(8):
                        S.mm(self.ps[:, b, 0:8], self.sub(wv, (slice(None), k, slice(mm * 128, (mm + 1) * 128))),
                             self.sub(scT, (slice(None), slice(k * 8, (k + 1) * 8))), start=(k == 0), stop=(k == 7))
                    r = R_BMOD + l * 48 + wm
                    S.ts(self.modT[:, l * 48 + wm, :], self.ps[:, b, 0:5], self.vT[:, r:r + 1], None, ALU.add)
            for n in range(2):
                for m in range(8):
                    r = R_NG + n * 32 + l * 8 + m
                    S.ts(self.modA[:, (l * 2 + n) * 8 + m, :], self.modT[:, l * 48 + (3 * n + 1) * 8 + m, :],
                         1.0, self.vT[:, r:r + 1], ALU.add, ALU.mult)

    def mod(self, l, w, m, col):
        return self.modT[:, l * 48 + w * 8 + m, col:col + 1]

    def mA(self, l, n, m, col):
        return self.modA[:, (l * 2 + n) * 8 + m, col:col + 1]

    def io_tile(self, t):
        return self.sv((t % 2) * 4 * SLOT, D, F32)

    def load_seq(self, s):
        S = self.S
        for t in range(NCH):
            src = self.ctx[s][t * 128:(t + 1) * 128, :] if t < 2 else self.x[s][(t - 2) * 128:(t - 1) * 128, :]
            tl = self.io_tile(t)
            S.dma('sp' if t % 2 == 0 else 'act', tl, src)
            for k in range(8):
                b = self.bank()
                S.tr(self.ps[:, b, 0:128], self.sub(tl, (slice(None), slice(k * 128, (k + 1) * 128))), self.ident())
                S.copy(self.xT[:, k, t * 128:(t + 1) * 128], self.ps[:, b, 0:128], eng='dve' if k % 2 else 'act')

    def store_seq(self, s):
        S = self.S
        if self.final:
            self.norm(None, None, s, final=True)
        for t in range(2, NCH):
            tl = self.io_tile(t)
            for k in range(8):
                b = self.bank()
                S.tr(self.ps[:, b, 0:128], self.xT[:, k, t * 128:(t + 1) * 128], self.ident())
                S.copy(self.sub(tl, (slice(None), slice(k * 128, (k + 1) * 128))), self.ps[:, b, 0:128],
                       eng='dve' if k % 2 else 'act')
            S.dma('sp' if t % 2 == 0 else 'act', self.out[s][(t - 2) * 128:(t - 1) * 128, :], tl)

    def norm(self, l, n, s, final=False):
        S = self.S
        for t in range(NTILE):
            c0, c1 = t * TW, (t + 1) * TW
            sq = [self.wv(k // 2, TW, BF16, c0=(k % 2) * (TW // 2)) for k in range(8)]
            for k in range(8):
                S.act(sq[k], self.xT[:, k, c0:c1], AF.Square)
            b = self.bank()
            for k in range(8):
                S.mm(self.ps[:, b, 0:TW], self.onesb(), sq[k], start=(k == 0), stop=(k == 7))
            rstd = self.wv(4)
            S.act(rstd, self.ps[:, b, 0:TW], AF.Sqrt, bias=self.eps(), scale=1.0 / D)
            S.recip(rstd, rstd)
            for k in range(8):
                if final:
                    S.stt(self.xT[:, k, c0:c1], self.xT[:, k, c0:c1], self.vT[:, R_FG + k:R_FG + k + 1], rstd,
                          ALU.mult, ALU.mult)
                    continue
                tmp = self.wv(5 + (k % 2))
                S.tt(tmp, self.xT[:, k, c0:c1], rstd, ALU.mult)
                for (a, e, isc) in segs(c0, c1):
                    col = 4 if isc else s
                    S.act(self.hT[:, k, a:e], self.sub(tmp, (slice(None), slice(a - c0, e - c0))), AF.Identity,
                          bias=self.mod(l, 3 * n, k, col), scale=self.mA(l, n, k, col))

    def resid(self, l, gate_w, m, c0, c1, ps_view, s):
        for (a, e, isc) in segs(c0, c1):
            col = 4 if isc else s
            self.S.stt(self.xT[:, m, a:e], self.sub(ps_view, (slice(None), slice(a - c0, e - c0))),
                       self.mod(l, gate_w, m, col), self.xT[:, m, a:e], ALU.mult, ALU.add)

    def ffn(self, l, s):
        S = self.S
        gj = lambda j, t: self.sv(j * FT + t * TW, TW)
        for blk in range(NT // FT):
            base = blk * FT
            for j in range(22):
                wa = self.ldw(self.s_w13[l][:, :, j * 128:(j + 1) * 128], 8, 128, eng='sp')
                wb = self.ldw(self.s_w13[l][:, :, DFF + j * 128:DFF + (j + 1) * 128], 8, 128, eng='act')
                for t in range(FT // TW):
                    c0 = base + t * TW
                    ba, bb = self.bank(), self.bank()
                    for k in range(8):
                        S.mm(self.ps[:, ba, 0:TW], wa[:, k, :], self.hT[:, k, c0:c0 + TW], start=(k == 0), stop=(k == 7))
                    for k in range(8):
                        S.mm(self.ps[:, bb, 0:TW], wb[:, k, :], self.hT[:, k, c0:c0 + TW], start=(k == 0), stop=(k == 7))
                    sl = self.wv(5 + (j * 2 + t) % 2)
                    S.act(sl, self.ps[:, ba, 0:TW], AF.Silu)
                    S.tt(gj(j, t), sl, self.ps[:, bb, 0:TW], ALU.mult)
            for m in range(8):
                w2 = self.ldw(self.s_w2[l][:, :, m * 128:(m + 1) * 128], 22, 128, eng='sp' if m % 2 else 'act')
                for t in range(FT // TW):
                    c0 = base + t * TW
                    b = self.bank()
                    for j in range(22):
                        S.mm(self.ps[:, b, 0:TW], w2[:, j, :], gj(j, t), start=(j == 0), stop=(j == 21))
                    self.resid(l, 5, m, c0, c0 + TW, self.ps[:, b, 0:TW], s)

    def layer(self, s, l):
        self.norm(l, 0, s)
        if l % 2 == 0:
            if 'gla' in self.parts:
                self.gla(l, s)
            if 'rw' in self.parts:
                self.rwkv(l, s)
        elif 'odd' in self.parts:
            self.odd(l, s)
        if 'ffn' in self.parts:
            self.norm(l, 1, s)
            self.ffn(l, s)

    def proj(self, wsrc, c0, M, cb, p0=0):
        S = self.S
        w = self.ldw(wsrc[:, :, c0:c0 + M], 8, M)
        for t in range(NTILE):
            b = self.bank()
            for k in range(8):
                S.mm(self.ps[p0:p0 + M, b, 0:TW], w[:, k, :], self.hT[:, k, t * TW:(t + 1) * TW], start=(k == 0), stop=(k == 7))
            cb(t, self.ps[p0:p0 + M, b, 0:TW])

    def gla(self, l, s):
        S = self.S
        j = l // 2
        win, wout = self.s_evin[j], self.s_evout[j]
        tsl = lambda t: (slice(None), slice(t * TW, (t + 1) * TW))
        adT = self.sv(0, NT, BF16, 0, 48)
        vtok = self.sv(2 * SLOT, NT)
        qf = self.sv(4 * SLOT, NT, BF16, 0, 64)
        kf = self.sv(6 * SLOT, NT, BF16, 0, 64)
        cw = self.sv(8 * SLOT, NT, F32, 0, 64)
        qb = self.sv(12 * SLOT, NT, BF16, 0, 64)
        kb = self.sv(14 * SLOT, NT, BF16, 0, 64)
        oacc = self.sv(16 * SLOT, NT, F32)
        aup = self.wv(5, 256, BF16, 0, 48)
        aupf = self.wv(6, 256, F32, 0, 48)
        etot = self.wv(7, NCH, F32, 0, 64)
        Sst = self.wv(7, 128, F32, 0, 64, c0=64)
        Sbf = self.wv(7, 128, BF16, 0, 64, c0=192)
        kbtok = self.wv(7, 64, BF16, 0, 128, c0=256)
        attm = self.wv(8, 128, BF16)
        for d in range(2):
            S.dma('sp', self.sub(aupf, (slice(32 * d, 32 * d + 16), slice(None))), self.gla_a_up[j][d])
            S.copy(self.sub(aup, (slice(32 * d, 32 * d + 16), slice(None))),
                   self.sub(aupf, (slice(32 * d, 32 * d + 16), slice(None))))
            self.proj(win, 1536 + 16 * d, 16,
                      lambda t, ps, d=d: S.copy(self.sub(adT, (slice(32 * d, 32 * d + 16), slice(t * TW, (t + 1) * TW))), ps),
                      p0=32 * d)
        for h in range(4):
            self.proj(win, h * 64, 64, lambda t, ps: S.copy(self.sub(qf, tsl(t)), ps, eng='act'))
            self.proj(win, 256 + h * 64, 64, lambda t, ps: S.copy(self.sub(kf, tsl(t)), ps))
            wv = self.ldw(win[:, :, 512 + h * 128:512 + (h + 1) * 128], 8, 128)
            for c in range(NCH):
                b = self.bank()
                for k in range(8):
                    S.mm(self.ps[:, b, 0:128], self.hT[:, k, c * 128:(c + 1) * 128], wv[:, k, :], start=(k == 0), stop=(k == 7))
                S.copy(self.sub(vtok, (slice(None), slice(c * 128, (c + 1) * 128))), self.ps[:, b, 0:128],
                       eng='act' if c % 2 else 'dve')
            for d in range(2):
                rv = (lambda a: a) if d == 0 else (lambda a: a[:, ::-1])
                for t in range(NTILE):
                    b = self.bank()
                    S.mm(self.ps[0:64, b, 0:TW], self.sub(aup, (slice(32 * d, 32 * d + 16), slice(h * 64, (h + 1) * 64))),
                         self.sub(adT, (slice(32 * d, 32 * d + 16), slice(t * TW, (t + 1) * TW))))
                    cwt = self.sub(cw, tsl(t))
                    col = j * 128 + d * 4 + h
                    S.act(cwt, self.ps[0:64, b, 0:TW], AF.Exp, bias=self.v64[:, col:col + 1], scale=-1.0)
                    S.act(cwt, cwt, AF.Ln, bias=self.one(64), scale=1.0)
                if d == 0:
                    S.scan(cw, self.rm[0:64, :], cw, 0.0, ALU.mult, ALU.add)
                else:
                    S.scan(cw.re(rv), self.rm[0:64, :], cw.re(rv), 0.0, ALU.mult, ALU.add)
                S.act(etot, self.sub(cw, (slice(None), slice(127 if d == 0 else 0, NT, 128))), AF.Exp, scale=-1.0 / 16)
                for t in range(NTILE):
                    e1, e2 = self.wv(0, TW, F32, 0, 64), self.wv(1, TW, F32, 0, 64)
                    cwt = self.sub(cw, tsl(t))
                    S.act(e1, cwt, AF.Exp, scale=-1.0 / 16)
                    S.stt(self.sub(qb, tsl(t)), self.sub(qf, tsl(t)), 0.125, e1, ALU.mult, ALU.mult)
                    S.act(e2, cwt, AF.Exp, scale=1.0 / 16)
                    S.tt(self.sub(kb, tsl(t)), self.sub(kf, tsl(t)), e2, ALU.mult)
                S.memset(Sst, 0.0)
                S.memset(Sbf, 0.0)
                order = list(range(NCH)) if d == 0 else [1, 0] + list(range(NCH - 1, 1, -1))
                for c in order:
                    cs = (slice(None), slice(c * 128, (c + 1) * 128))
                    kbc, qbc, vc = self.sub(kb, cs), self.sub(qb, cs), self.sub(vtok, cs)
                    b1 = self.bank()
                    tp = View(self.ps.h[:, b1, 0:32].bitcast(BF16), self.ps, b1, b1 + 1)
                    S.tr(tp, kbc, self.identb(64))
                    S.copy(kbtok, tp, eng='act')
                    b2 = self.bank()
                    S.mm(self.ps[:, b2, 0:128], kbc, qbc)
                    S.tt(attm, self.ps[:, b2, 0:128], self.trib(d), ALU.mult)
                    b3 = self.bank()
                    S.mm(self.ps[:, b3, 0:128], vc, attm, start=True, stop=False)
                    S.mm(self.ps[:, b3, 0:128], Sbf, qbc, start=False, stop=True)
                    oc = self.sub(oacc, cs)
                    if d == 0:
                        S.copy(oc, self.ps[:, b3, 0:128], eng='act')
                    else:
                        S.tt(oc, oc, self.ps[:, b3, 0:128], ALU.add)
                    b4 = self.bank()
                    S.mm(self.ps[0:64, b4, 0:128], kbtok, vc)
                    S.tt(Sst, Sst, self.ps[0:64, b4, 0:128], ALU.add)
                    S.ts(Sst, Sst, self.sub(etot, (slice(None), slice(c, c + 1))), None, ALU.mult)
                    S.copy(Sbf, Sst, eng='act')
            wg = self.ldw(win[:, :, 1024 + h * 128:1024 + (h + 1) * 128], 8, 128)
            wo = self.ldw(wout[:, h:h + 1, :], 1, D)
            for t in range(NTILE):
                c0, c1 = t * TW, (t + 1) * TW
                ot = self.sub(oacc, tsl(t))
                sqb = self.wv(0, TW, BF16)
                S.act(sqb, ot, AF.Square)
                b = self.bank()
                S.mm(self.ps[:, b, 0:TW], self.onesb(), sqb)
                rstd = self.wv(1)
                S.act(rstd, self.ps[:, b, 0:TW], AF.Sqrt, bias=self.eps(), scale=1.0 / 128)
                S.recip(rstd, rstd)
                bg = self.bank()
                for k in range(8):
                    S.mm(self.ps[:, bg, 0:TW], wg[:, k, :], self.hT[:, k, c0:c1], start=(k == 0), stop=(k == 7))
                sg = self.wv(2)
                S.act(sg, self.ps[:, bg, 0:TW], AF.Silu)
                y1 = self.wv(3)
                S.stt(y1, ot, self.vT[:, R_GLAG + j:R_GLAG + j + 1], rstd, ALU.mult, ALU.mult)
                yb = self.wv(4, TW, BF16)
                S.tt(yb, y1, sg, ALU.mult)
                for m in range(8):
                    b = self.bank()
                    S.mm(self.ps[:, b, 0:TW], wo[:, 0, m * 128:(m + 1) * 128], yb)
                    self.resid(l, 2, m, c0, c1, self.ps[:, b, 0:TW], s)

    def tshift(self, zp, dst_fn, P, cmc, mpc, mnc):
        S = self.S
        i = 0
        for t in range(NTILE):
            for (a, e, isc) in segs(t * TW, (t + 1) * TW):
                z0 = a + (1 if isc else 2)
                n = e - a
                tmp = self.wv(5 + i % 2, n, F32, 0, P)
                i += 1
                zs = lambda o: self.sub(zp, (slice(0, P), slice(z0 + o, z0 + o + n)))
                S.ts(tmp, zs(0), cmc, None, ALU.mult)
                S.stt(tmp, zs(-1), mpc, tmp, ALU.mult, ALU.add)
                S.stt(dst_fn(a, e), zs(1), mnc, tmp, ALU.mult, ALU.add)

    def rwkv(self, l, s):
        S = self.S
        j = l // 2
        win, wout = self.s_evin[j], self.s_evout[j]
        vb, vb2 = j * 128, j * 64
        vc = lambda c, P=128: self.vrw[0:P, vb + c:vb + c + 1]
        vdc = lambda c, P=128: self.vd[0:P, vb2 + c:vb2 + c + 1]
        tsl = lambda t: (slice(None), slice(t * TW, (t + 1) * TW))
        LR1 = self.sv(0, NT, BF16, 0, 96)
        LR2 = self.sv(2 * SLOT, NT, BF16, 0, 96)
        oacc = self.sv(4 * SLOT, NT, F32)
        zp = self.sv(8 * SLOT, NT + 3, F32)
        rs_ = self.sv(13 * SLOT, NT)
        ks_ = self.sv(15 * SLOT, NT)
        vs_ = self.sv(17 * SLOT, NT)
        S.dma('pool', self.rww[0:64, 0, :], self.rw_w_up[j].re(lambda a: a.rearrange("d r c -> (d r) c")))
        S.dma('pool', self.rww[64:96, 1, :], self.rw_a_up[j])
        S.dma('pool', self.rww[0:96, 2, :], self.rw_g_up[j])
        for c in (0, 257, NT + 2):
            S.memset(self.sub(zp, (slice(None), slice(c, c + 1))), 0.0)

        def to_zp(P):
            def cb(t, ps):
                for (a, e, isc) in segs(t * TW, (t + 1) * TW):
                    o = 1 if isc else 2
                    S.copy(self.sub(zp, (slice(0, P), slice(a + o, e + o))), self.sub(ps, (slice(None), slice(a - t * TW, e - t * TW))),
                           eng='act' if t % 2 else 'dve')
            return cb
        self.proj(win, RW0 + 1536, 96, to_zp(96))
        self.tshift(zp, lambda a, e: self.sub(LR1, (slice(None), slice(a, e))), 96, vdc(12, 96), vc(24, 96), vc(25, 96))
        S.act(self.sub(LR1, (slice(0, 64), slice(None))), self.sub(LR1, (slice(0, 64), slice(None))), AF.Tanh)
        self.proj(win, RW0 + 1632, 96, to_zp(96))
        self.tshift(zp, lambda a, e: self.sub(LR2, (slice(None), slice(a, e))), 96, vdc(13, 96), vc(26, 96), vc(27, 96))
        S.act(LR2, LR2, AF.Sigmoid)
        self.dump('LR1', LR1, [96, NT], BF16)
        self.dump('LR2', LR2, [96, NT], BF16)
        for p in range(1 if self.debug else 4):
            for c in (0, 257, NT + 2):
                S.memset(self.sub(zp, (slice(None), slice(c, c + 1))), 0.0)
            for q, dst in enumerate((rs_, ks_, vs_)):
                self.proj(win, RW0 + q * 512 + p * 128, 128, to_zp(128))
                self.tshift(zp, lambda a, e, dst=dst: self.sub(dst, (slice(None), slice(a, e))), 128,
                            vdc(q * 4 + p), vc(q * 8 + p), vc(q * 8 + 4 + p))
            for t in range(NTILE):
                c0, c1 = t * TW, (t + 1) * TW
                ba = self.bank()
                for hh in range(2):
                    S.mm(self.ps[64 * hh:64 * hh + 64, ba, 0:TW], self.rww[64:96, 1, p * 128 + hh * 64:p * 128 + hh * 64 + 64],
                         self.sub(LR1, (slice(64, 96), slice(c0, c1))))
                a_ = self.wv(0)
                S.act(a_, self.ps[:, ba, 0:TW], AF.Sigmoid, bias=vc(36 + p))
                kk = self.wv(1)
                S.ts(kk, self.sub(ks_, tsl(t)), vc(40 + p), None, ALU.mult)
                sqb = self.wv(2, TW, BF16)
                S.tt(sqb, kk, kk, ALU.mult)
                bs = self.bank()
                S.mm(self.ps[:, bs, 0:TW], self.blkb(), sqb)
                nr = self.wv(3)
                S.act(nr, self.ps[:, bs, 0:TW], AF.Sqrt)
                S.ts(nr, nr, 1e-12, None, ALU.max)
                S.recip(nr, nr)
                S.tt(kk, kk, nr, ALU.mult)
                al = self.wv(2, TW, BF16)
                S.ts(al, kk, -1.0, None, ALU.mult)
                S.dma('sp', self.rwp[3][:, c0:c1], al)
                be = self.wv(3, TW, BF16)
                S.tt(be, kk, a_, ALU.mult)
                S.dma('act', self.rwp[4][:, c0:c1], be)
                S.ts(a_, a_, vc(44 + p), vdc(16 + p), ALU.mult, ALU.add)
                k2 = self.wv(4, TW, BF16)
                S.tt(k2, self.sub(ks_, tsl(t)), a_, ALU.mult)
                S.dma('sp', self.rwp[1][:, c0:c1], k2)
            S.dma('sp', self.rwp[0], rs_)
            S.dma('act', self.rwp[2], vs_)
            G0 = 8 * SLOT
            gin = [self.sv(G0 + i * 256, 256) for i in range(5)]
            dec = [self.sv(G0 + 2 * SLOT + i * 256, 256) for i in range(4)]
            tok = [self.sv(G0 + 3 * SLOT + i * 256, 256) for i in range(3)]
            amat = [self.sv(G0 + (4 + i // 2) * SLOT + (i % 2) * 512, 512) for i in range(4)]
            pm = [self.sv(G0 + (6 + i // 2) * SLOT + (i % 2) * 512, 512) for i in range(4)]
            rhsb = self.sv(G0 + 8 * SLOT, 128)
            ub = self.sv(G0 + 8 * SLOT + 128, 128)
            Mst = self.wv(4, 64, F32)
            Mbf = self.wv(4, 64, BF16, c0=64)
            gl = self.wv(4, 2, F32, c0=128)
            a4 = lambda v: v.re(lambda a: a.rearrange("p (h c t) -> p h c t", h=2, c=2))
            a3 = lambda v: v.re(lambda a: a.rearrange("p (c t) -> p c t", c=2))
            for d in range(2):
                S.memset(Mst, 0.0)
                S.memset(Mbf, 0.0)
                gorder = list(range(9)) if d == 0 else [0] + list(range(8, 0, -1))
                for gi in gorder:
                    g0 = gi * 256
                    gs = (slice(None), slice(g0, g0 + 256))
                    for i in range(5):
                        S.dma('sp' if i % 2 else 'act', gin[i], self.rwp[i][:, g0:g0 + 256])
                    bsg = self.bank()
                    for hh in range(2):
                        S.mm(self.ps[64 * hh:64 * hh + 64, bsg, 0:256],
                             self.rww[32 * d:32 * d + 32, 0, p * 128 + hh * 64:p * 128 + hh * 64 + 64],
                             self.sub(LR1, (slice(32 * d, 32 * d + 32), slice(g0, g0 + 256))))
                    cw = self.wv(0, 256)
                    S.act(cw, self.ps[:, bsg, 0:256], AF.Sigmoid, bias=vc(28 + d * 4 + p))
                    rmg = self.rm[:, 0:256]
                    if d == 0:
                        S.scan(cw, rmg, cw, 0.0, ALU.mult, ALU.add)
                    else:
                        S.scan(cw.re(lambda a: a[:, ::-1]), rmg, cw.re(lambda a: a[:, ::-1]), 0.0, ALU.mult, ALU.add)
                    E1, E2, E3 = self.wv(1, 256), self.wv(2, 256), self.wv(3, 256)
                    S.act(E1, cw, AF.Exp, scale=-KAPPA)
                    S.act(E2, cw, AF.Exp, scale=KAPPA)
                    S.memset(E3, 1.0)
                    if d == 0:
                        S.act(a3(E3)[:, :, 1:128], a3(cw)[:, :, 0:127], AF.Exp, scale=-KAPPA)
                    else:
                        S.act(a3(E3)[:, :, 0:127], a3(cw)[:, :, 1:128], AF.Exp, scale=-KAPPA)
                    S.copy(gl, a3(E1)[:, :, 127 if d == 0 else 0])
                    rt, at, bt, kt = dec
                    S.tt(rt, gin[0], E1, ALU.mult)
                    S.tt(at, gin[3], E3, ALU.mult)
                    S.tt(bt, gin[4], E2, ALU.mult)
                    S.tt(kt, gin[1], E2, ALU.mult)
                    for i, src in enumerate((bt, kt, gin[2])):
                        bt_ = self.bank()
                        tp = View(self.ps.h[:, bt_, 0:128].bitcast(BF16), self.ps, bt_, bt_ + 1)
                        for c in range(2):
                            S.tr(self.sub(tp, (slice(None), slice(c * 128, (c + 1) * 128))),
                                 self.sub(src, (slice(None), slice(c * 128, (c + 1) * 128))), self.identb())
                        S.copy(tok[i], tp, eng='act' if i % 2 else 'dve')
                    btok, ktok, vtok = tok

                    def amm(dst, lf, rf, maskcol, extra_ident=False):
                        b = self.bank()
                        for hh in range(2):
                            for c in range(2):
                                hs = slice(64 * hh, 64 * hh + 64)
                                cs = slice(c * 128, (c + 1) * 128)
                                S.mm(self.ps[:, b, (hh * 2 + c) * 128:(hh * 2 + c + 1) * 128],
                                     self.sub(lf, (hs, cs)), self.sub(rf, (hs, cs)))
                        S.tt(a4(dst).re(lambda a: a.rearrange("p h c t -> p (h c) t")),
                             self.ps[:, b, 0:512].re(lambda a: a.rearrange("p (n t) -> p n t", n=4)),
                             self.bmask(maskcol, 4), ALU.mult)
                    incl = C_TRF if d == 0 else C_TRB
                    strict = C_STF if d == 0 else C_STB
                    strictT = C_STB if d == 0 else C_STF
                    ArbT, ArkT, AakT, TT = amat
                    amm(ArbT, bt, rt, incl)
                    amm(ArkT, kt, rt, incl)
                    amm(AakT, kt, at, strict)
                    P, PT = pm[0], pm[1]
                    amm(PT, bt, at, strict)
                    amm(P, at, bt, strictT)


                    def bmm(lf, rf):
                        b = self.bank()
                        for n in range(4):
                            ns = (slice(None), slice(n * 128, (n + 1) * 128))
                            S.mm(self.ps[:, b, n * 128:(n + 1) * 128], self.sub(lf, ns), self.sub(rf, ns))
                        return self.ps[:, b, 0:512]
                    f4 = lambda v: v.re(lambda a: a.rearrange("p (n t) -> p n t", n=4))
                    Q0, Q0T = pm[2], pm[3]
                    S.tt(f4(Q0), f4(P), self.bmask(1024, 4), ALU.mult)
                    S.tt(f4(Q0T), f4(PT), self.bmask(1024, 4), ALU.mult)
                    NbT = [self.sv(G0 + 9 * SLOT + i * 512, 512) for i in range(4)]
                    for i in range(4):
                        S.tt(f4(NbT[i]), f4(PT), self.bmask(1152 + 128 * i, 4), ALU.mult)
                    Dm = self.sv(G0 + 11 * SLOT, 512)
                    S.tt(f4(Dm), f4(Q0), self.bmask(C_ID, 4), ALU.add)
                    S.tt(f4(TT), f4(Q0T), self.bmask(C_ID, 4), ALU.add)
                    Q1, Q1T = pm[0], pm[1]
                    S.copy(Q1, bmm(Q0T, Q0), eng='act')
                    S.copy(Q1T, bmm(Q0, Q0T))
                    S.tt(Dm, Dm, bmm(TT, Q1), ALU.add)
                    S.tt(TT, TT, bmm(Q1, TT), ALU.add)
                    Q2 = pm[2]
                    S.copy(Q2, bmm(Q1T, Q1), eng='act')
                    S.tt(Dm, Dm, bmm(TT, Q2), ALU.add)
                    S.tt(TT, TT, bmm(Q2, TT), ALU.add)
                    X = pm[0]
                    for i in range(4):
                        S.copy(X, bmm(NbT[i], Dm), eng='act')
                        if i < 3:
                            S.tt(Dm, Dm, bmm(TT, X), ALU.add)
                        S.tt(TT, TT, bmm(X, TT), ALU.add)
                    if d == 0 and gi == 0 and p == 0:
                        self.dump('cw', cw, [128, 256]); self.dump('E1', E1, [128, 256]); self.dump('E3', E3, [128, 256])
                        self.dump('TT', TT, [128, 512], BF16); self.dump('ArbT', ArbT, [128, 512], BF16); self.dump('AakT', AakT, [128, 512], BF16)
                        self.dump('P0', pm[0], [128, 512], BF16); self.dump('btok', btok, [128, 256], BF16)
                    corder = (0, 1) if d == 0 else (1, 0)
                    for c in corder:
                        cs = slice(c * 128, (c + 1) * 128)
                        b1 = self.bank()
                        for hh in range(2):
                            hs = slice(64 * hh, 64 * hh + 64)
                            n = hh * 2 + c
                            S.mm(self.ps[:, b1, hh * 64:hh * 64 + 64], self.sub(at, (hs, cs)), self.sub(Mbf, (hs, slice(None))),
                                 start=True, stop=False)
                            S.mm(self.ps[:, b1, hh * 64:hh * 64 + 64], self.sub(AakT, (slice(None), slice(n * 128, (n + 1) * 128))),
                                 self.sub(vtok, (slice(None), slice(c * 128 + hh * 64, c * 128 + hh * 64 + 64))), start=False, stop=True)
                        S.copy(rhsb, self.ps[:, b1, 0:128], eng='act')
                        b2 = self.bank()
                        for hh in range(2):
                            n = hh * 2 + c
                            S.mm(self.ps[:, b2, hh * 64:hh * 64 + 64], self.sub(TT, (slice(None), slice(n * 128, (n + 1) * 128))),
                                 self.sub(rhsb, (slice(None), slice(hh * 64, hh * 64 + 64))))
                        S.copy(ub, self.ps[:, b2, 0:128])
                        b3 = self.bank()
                        for hh in range(2):
                            hs = slice(64 * hh, 64 * hh + 64)
                            n = hh * 2 + c
                            us = self.sub(ub, (slice(None), slice(hh * 64, hh * 64 + 64)))
                            vv = self.sub(vtok, (slice(None), slice(c * 128 + hh * 64, c * 128 + hh * 64 + 64)))
                            S.mm(self.ps[hs, b3, 0:128], self.sub(Mbf, (hs, slice(None))), self.sub(rt, (hs, cs)), start=True, stop=False)
                            S.mm(self.ps[hs, b3, 0:128], us, self.sub(ArbT, (slice(None), slice(n * 128, (n + 1) * 128))), start=False, stop=False)
                            S.mm(self.ps[hs, b3, 0:128], vv, self.sub(ArkT, (slice(None), slice(n * 128, (n + 1) * 128))), start=False, stop=True)
                        oc = self.sub(oacc, (slice(None), slice(g0 + c * 128, g0 + (c + 1) * 128)))
                        if d == 0:
                            S.copy(oc, self.ps[:, b3, 0:128], eng='act')
                        else:
                            S.tt(oc, oc, self.ps[:, b3, 0:128], ALU.add)
                        b4 = self.bank()
                        for hh in range(2):
                            hs = slice(64 * hh, 64 * hh + 64)
                            us = self.sub(ub, (slice(None), slice(hh * 64, hh * 64 + 64)))
                            vv = self.sub(vtok, (slice(None), slice(c * 128 + hh * 64, c * 128 + hh * 64 + 64)))
                            S.mm(self.ps[hs, b4, 0:64], self.sub(btok, (slice(None), slice(c * 128 + hh * 64, c * 128 + hh * 64 + 64))), us,
                                 start=True, stop=False)
                            S.mm(self.ps[hs, b4, 0:64], self.sub(ktok, (slice(None), slice(c * 128 + hh * 64, c * 128 + hh * 64 + 64))), vv,
                                 start=False, stop=True)
                        S.tt(Mst, Mst, self.ps[:, b4, 0:64], ALU.add)
                        S.ts(Mst, Mst, self.sub(gl, (slice(None), slice(c, c + 1))), None, ALU.mult)
                        S.copy(Mbf, Mst, eng='act')
            self.dump(f'oacc{p}', oacc, [128, NT])
            wo = self.ldw(wout[:, 4 + p:5 + p, :], 1, D)
            for t in range(NTILE):
                c0, c1 = t * TW, (t + 1) * TW
                rr, kk2, vv = self.wv(5, TW, BF16), self.wv(5, TW, BF16, c0=192), self.wv(6, TW, BF16)
                S.dma('sp', rr, self.rwp[0][:, c0:c1])
                S.dma('act', kk2, self.rwp[1][:, c0:c1])
                S.dma('sp', vv, self.rwp[2][:, c0:c1])
                ot = self.sub(oacc, tsl(t))
                bm = self.bank()
                S.mm(self.ps[:, bm, 0:TW], self.blk32(), ot)
                cen = self.wv(0)
                S.stt(cen, self.ps[:, bm, 0:TW], -1.0 / 64, ot, ALU.mult, ALU.add)
                sq = self.wv(1)
                S.act(sq, cen, AF.Square)
                bv = self.bank()
                S.mm(self.ps[:, bv, 0:TW], self.blk32(), sq)
                rstd = self.wv(1)
                S.act(rstd, self.ps[:, bv, 0:TW], AF.Sqrt, bias=self.gneps(), scale=1.0 / 64)
                S.recip(rstd, rstd)
                S.tt(cen, cen, rstd, ALU.mult)
                S.ts(cen, cen, vc(52 + p), vc(56 + p), ALU.mult, ALU.add)
                rk = self.wv(2, TW, BF16)
                S.stt(rk, rr, vc(48 + p), kk2, ALU.mult, ALU.mult)
                bb = self.bank()
                S.mm(self.ps[:, bb, 0:TW], self.blkb(), rk)
                bon = self.wv(3)
                S.tt(bon, vv, self.ps[:, bb, 0:TW], ALU.mult)
                S.tt(cen, cen, bon, ALU.add)
                bg = self.bank()
                for hh in range(2):
                    S.mm(self.ps[64 * hh:64 * hh + 64, bg, 0:TW], self.rww[0:96, 2, p * 128 + hh * 64:p * 128 + hh * 64 + 64],
                         self.sub(LR2, (slice(None), slice(c0, c1))))
                yb = self.wv(4, TW, BF16)
                S.tt(yb, cen, self.ps[:, bg, 0:TW], ALU.mult)
                for m in range(8):
                    b = self.bank()
                    S.mm(self.ps[:, b, 0:TW], wo[:, 0, m * 128:(m + 1) * 128], yb)
                    self.resid(l, 2, m, c0, c1, self.ps[:, b, 0:TW], s)

    def odd(self, l, s):
        S = self.S
        j = l // 2
        win, wout = self.s_odin[j], self.s_odout[j]
        self.nrot = 4
        qa = self.sv(0, NT, BF16, 0, 65)
        ka = self.sv(2 * SLOT, NT, BF16, 0, 65)
        vtG = self.sv(4 * SLOT, 18 * 128)
        cosT = self.sv(6 * SLOT, 2048, F32, 0, 64)
        sinT = self.sv(10 * SLOT, 2048, F32, 0, 64)
        vrow = self.sv(14 * SLOT, 2048, BF16, 0, 64)
        vctx = self.sv(16 * SLOT, 128)
        tab = self.sv(17 * SLOT, 960, BF16, 0, 64)
        qn = self.sv(18 * SLOT, NT, BF16, 0, 64)
        negK = self.wv(9, 1, F32, 64, 65, c0=380)
        kmx = self.wv(9, 8, F32, 64, 65, c0=368)
        perm = self.cstb[0:64, 1728:1792]
        accn = [0]
        S.dma('sp', cosT, self.ropet[0])
        S.dma('act', sinT, self.ropet[1])

        def prep(dst, col0, gcol, rope, scale, is_q):
            w = self.ldw(win[:, :, col0:col0 + 64], 8, 64)
            for t in range(NTILE):
                c0, c1 = t * TW, (t + 1) * TW
                b = self.bank()
                for k in range(8):
                    S.mm(self.ps[0:64, b, 0:TW], w[:, k, :], self.hT[:, k, c0:c1], start=(k == 0), stop=(k == 7))
                pq = self.ps[0:64, b, 0:TW]
                qnt = self.sub(qn, (slice(None), slice(c0, c1)))
                if gcol is not None:
                    sq = self.wv(0, TW, BF16, 0, 64)
                    S.act(sq, pq, AF.Square)
                    b2 = self.bank()
                    S.mm(self.ps[0:64, b2, 0:TW], self.onesb(64, 64), sq)
                    rstd = self.wv(1, TW, F32, 0, 64)
                    S.act(rstd, self.ps[0:64, b2, 0:TW], AF.Sqrt, bias=self.eps(64), scale=1.0 / 64)
                    S.recip(rstd, rstd)
                    S.stt(qnt, pq, self.v64[:, gcol:gcol + 1], rstd, ALU.mult, ALU.mult)
                else:
                    S.copy(qnt, pq, eng='act')
                for (a, e, isc) in segs(c0, c1):
                    d_ = self.sub(dst, (slice(0, 64), slice(a, e)))
                    q_ = self.sub(qn, (slice(None), slice(a, e)))
                    if isc or not rope:
                        S.ts(d_, q_, scale, None, ALU.mult)
                    else:
                        n = e - a
                        b3 = self.bank()
                        S.mm(self.ps[0:64, b3, 0:n], perm, q_)
                        t1 = self.wv(2, n, F32, 0, 64)
                        t2 = self.wv(3, n, F32, 0, 64)
                        S.tt(t1, q_, self.sub(cosT, (slice(None), slice(a - CTX, e - CTX))), ALU.mult)
                        S.tt(t2, self.ps[0:64, b3, 0:n], self.sub(sinT, (slice(None), slice(a - CTX, e - CTX))), ALU.mult)
                        S.tt(t1, t1, t2, ALU.add)
                        S.ts(d_, t1, scale, None, ALU.mult)
                sq2 = self.wv(4, TW, BF16, 0, 64)
                dt_ = self.sub(dst, (slice(0, 64), slice(c0, c1)))
                S.tt(sq2, dt_, dt_, ALU.mult)
                b4 = self.bank()
                S.mm(self.ps[0:65, b4, 0:TW], self.onesb(64, 65), sq2)
                if is_q:
                    nq = self.wv(5, TW, F32, 64, 65)
                    S.act(nq, self.ps[64:65, b4, 0:TW], AF.Sqrt)
                    S.ts(self.sub(dst, (slice(64, 65), slice(c0, c1))), nq, negK, None, ALU.mult)
                else:
                    S.reduce(self.sub(kmx, (slice(None), slice(t, t + 1))), self.ps[64:65, b4, 0:TW], ALU.max)
            if not is_q:
                S.reduce(negK, self.sub(kmx, (slice(None), slice(0, NTILE))), ALU.max)
                S.act(negK, negK, AF.Sqrt)
                S.ts(negK, negK, -1.0, None, ALU.mult)
                S.memset(self.sub(dst, (slice(64, 65), slice(None))), 1.0)

        def accbanks():
            a = accn[0] % 2
            accn[0] += 1
            return 4 + 2 * a, 5 + 2 * a

        def finish(num, den, W):
            rden = self.wv2(2, W, F32, 0, 64)
            S.recip(rden, den)
            y = self.wv(4, W, BF16, 0, 64)
            S.tt(y, num, rden, ALU.mult)
            return y

        def dense(vfun, ktiles, q0, W):
            bn, bd = accbanks()
            num, den = self.ps[0:64, bn, 0:W], self.ps[0:64, bd, 0:W]
            last = len(ktiles) - 1
            accP = self.wv2(6, W, F32)

            def sc(i):
                kt = ktiles[i]
                b = self.bank()
                S.mm(self.ps[:, b, 0:W], self.sub(ka, (slice(0, 65), slice(kt * 128, (kt + 1) * 128))),
                     self.sub(qa, (slice(0, 65), slice(q0, q0 + W))))
                return b
            bcur = sc(0)
            for i, kt in enumerate(ktiles):
                bnext = sc(i + 1) if i < last else None
                pT = self.wv(i % 2, W, BF16)
                S.act(pT, self.ps[:, bcur, 0:W], AF.Exp)
                S.mm(num, vfun(kt), pT, start=(i == 0), stop=(i == last))
                if i == 0:
                    S.copy(accP, pT, eng='pool')
                else:
                    S.tt(accP, accP, pT, ALU.add, eng='pool')
                bcur = bnext
            S.mm(den, self.cst[0:64, 128:192], self.sub(accP, (slice(0, 64), slice(None))), start=True, stop=False)
            S.mm(den, self.cst[64:128, 192:256], self.sub(accP, (slice(64, 128), slice(None))), start=False, stop=True)
            return finish(num, den, W)

        def ld_wo(hrow):
            st = self.stage[self.stn % 3]
            self.stn += 1
            dst = st[0:64, 0:D]
            S.dma('sp', dst, wout[(hrow % 2) * 64:(hrow % 2) * 64 + 64, hrow // 2, :])
            return dst

        def apply_wout(y, W, q0, wo):
            for m in range(8):
                b = self.bank()
                S.mm(self.ps[:, b, 0:W], self.sub(wo, (slice(None), slice(m * 128, (m + 1) * 128))), y)
                self.resid(l, 2, m, q0, q0 + W, self.ps[:, b, 0:W], s)

        wv = self.ldw(win[:, :, 640:768], 8, 128)
        for c in range(NCH):
            b = self.bank()
            for k in range(8):
                S.mm(self.ps[:, b, 0:128], self.hT[:, k, c * 128:(c + 1) * 128], wv[:, k, :], start=(k == 0), stop=(k == 7))
            S.copy(self.sub(vtG, (slice(None), slice(c * 128, (c + 1) * 128))), self.ps[:, b, 0:128], eng='act' if c % 2 else 'dve')
        for kv in range(2):
            prep(ka, 512 + kv * 64, 142 + j, True, 1.0, False)
            for hq in range(4):
                h = kv * 4 + hq
                prep(qa, h * 64, 140 + j, True, 0.125, True)
                wo = ld_wo(h)
                vf = lambda kt, kv=kv: self.sub(vtG, (slice(None), slice(kt * 128 + kv * 64, kt * 128 + kv * 64 + 64)))
                y = dense(vf, [0, 1], 0, CTX)
                apply_wout(y, CTX, 0, wo)
                for qb_ in range(4):
                    q0 = CTX + qb_ * 512
                    y = dense(vf, list(range(NCH)), q0, 512)
                    apply_wout(y, 512, q0, wo)
        tok = lambda r: CTX + r * 64
        tab3 = tab.re(lambda a: a.rearrange("p (d q) -> p d q", d=15))
        vrowA = self.sv(4 * SLOT, 32 * 256, BF16, 0, 64)
        vctxA = self.sv(14 * SLOT, 512)
        for hg in range(2):
            wv4 = self.ldw(win[:, :, 1792 + hg * 256:1792 + (hg + 1) * 256], 8, 256)
            for r in range(32):
                b = self.bank()
                for k in range(8):
                    S.mm(self.ps[0:64, b, 0:256], self.hT[:, k, tok(r):tok(r) + 64], wv4[:, k, :], start=(k == 0), stop=(k == 7))
                S.copy(self.sub(vrowA, (slice(None), slice(r * 256, (r + 1) * 256))), self.ps[0:64, b, 0:256], eng='act' if r % 2 else 'dve')
            for ct in range(2):
                b = self.bank()
                for k in range(8):
                    S.mm(self.ps[:, b, 0:256], self.hT[:, k, ct * 128:(ct + 1) * 128], wv4[:, k, :], start=(k == 0), stop=(k == 7))
                S.copy(self.sub(vctxA, (slice(None), slice(ct * 256, (ct + 1) * 256))), self.ps[:, b, 0:256])
            for hh in range(4):
                h = hg * 4 + hh
                prep(ka, 1280 + h * 64, None, False, 1.0, False)
                prep(qa, 768 + h * 64, None, False, 0.125, True)
                S.dma('act', tab, self.s_tab[j][:, h, :])
                wo = ld_wo(8 + h)
                vcx = lambda kt, hh=hh: self.sub(vctxA, (slice(None), slice(kt * 256 + hh * 64, kt * 256 + hh * 64 + 64)))
                y = dense(vcx, [0, 1], 0, CTX)
                apply_wout(y, CTX, 0, wo)
                for r0 in range(0, 32, 8):
                    bn, bd = accbanks()
                    num, den = self.ps[0:64, bn, 0:512], self.ps[0:64, bd, 0:512]
                    qblk = self.sub(qa, (slice(0, 65), slice(tok(r0), tok(r0) + 512)))
                    pTc = self.wv2(8, 1024, BF16)
                    for ct in range(2):
                        bc = self.bank()
                        S.mm(self.ps[:, bc, 0:512], self.sub(ka, (slice(0, 65), slice(ct * 128, (ct + 1) * 128))), qblk)
                        S.act(self.sub(pTc, (slice(None), slice(ct * 512, (ct + 1) * 512))), self.ps[:, bc, 0:512], AF.Exp)
                    for ct in range(2):
                        S.mm(num, vcx(ct), self.sub(pTc, (slice(None), slice(ct * 512, (ct + 1) * 512))), start=(ct == 0), stop=False)
                    pc = self.wv2(2, 512, F32)
                    S.tt(pc, self.sub(pTc, (slice(None), slice(0, 512))), self.sub(pTc, (slice(None), slice(512, 1024))), ALU.add, eng='pool')
                    S.mm(den, self.cst[0:64, 128:192], self.sub(pc, (slice(0, 64), slice(None))), start=True, stop=False)
                    S.mm(den, self.cst[64:128, 192:256], self.sub(pc, (slice(64, 128), slice(None))), start=False, stop=False)

                    def scores(r):
                        rs = min(max(r - 4, 0), 24)
                        qv = self.sub(qa, (slice(0, 65), slice(tok(r), tok(r) + 64)))
                        b = self.bank()
                        for i in range(8):
                            S.mm(self.ps[0:64, b, i * 64:(i + 1) * 64], self.sub(ka, (slice(0, 65), slice(tok(rs + i), tok(rs + i) + 64))), qv)
                        return b
                    nxt = scores(r0)
                    for r in range(r0, r0 + 8):
                        b = nxt
                        rs = min(max(r - 4, 0), 24)
                        dr0 = rs - r + 7
                        lastrow = (r == r0 + 7)
                        sb = self.wv2(6, 512, F32, 0, 64)
                        S.tt(sb.re(lambda a: a.rearrange("p (i q) -> p i q", i=8)),
                             self.ps[0:64, b, 0:512].re(lambda a: a.rearrange("p (i q) -> p i q", i=8)),
                             View(tab3.ap[:, dr0:dr0 + 8, ::-1], tab.buf, tab.b0, tab.b1), ALU.add)
                        pT = self.wv(r % 2, 512, BF16, 0, 64)
                        S.act(pT, sb, AF.Exp)
                        if not lastrow:
                            nxt = scores(r + 1)
                        cs = slice((r - r0) * 64, (r - r0 + 1) * 64)
                        out = self.ps[0:64, bn, cs]
                        for i in range(8):
                            S.mm(out, self.sub(vrowA, (slice(None), slice((rs + i) * 256 + hh * 64, (rs + i) * 256 + hh * 64 + 64))),
                                 self.sub(pT, (slice(None), slice(i * 64, (i + 1) * 64))), start=False, stop=(lastrow and i == 7))
                        ps8 = self.wv(5, 64, F32, 0, 64, c0=(r % 2) * 192)
                        S.reduce(ps8, pT.re(lambda a: a.rearrange("p (i q) -> p q i", i=8)), ALU.add)
                        S.mm(self.ps[0:64, bd, cs], self.cst[0:64, 128:192], ps8, start=False, stop=lastrow)
                    y = finish(num, den, 512)
                    apply_wout(y, 512, tok(r0), wo)
        self.nrot = 8


def make_consts():
    c = np.zeros((128, 1792), np.float32)
    j = np.arange(128)[:, None]; i = np.arange(128)[None, :]
    c[:, C_ID:C_ID + 128] = (j == i)
    c[:, C_TRF:C_TRF + 128] = (j <= i)
    c[:, C_TRB:C_TRB + 128] = (j >= i)
    c[:, C_ONE:C_ONE + 128] = 1.0
    c[:, C_STF:C_STF + 128] = (j < i)
    c[:, C_STB:C_STB + 128] = (j > i)
    c[:, C_BLK:C_BLK + 128] = ((j // 64) == (i // 64))
    c[:, 1024:1152] = ((j // 8) == (i // 8))
    for n_, b_ in enumerate((8, 16, 32, 64)):
        c[:, 1152 + 128 * n_:1280 + 128 * n_] = ((j // (2 * b_)) == (i // (2 * b_))) & ((j // b_) != (i // b_))
    qc = 63 - np.arange(64)[None, :]; kc = np.arange(64)[:, None]
    start = np.clip(qc - 8, 0, 48)
    c[0:64, 1664:1728] = np.where((kc >= start) & (kc < start + 16), 0.0, -30000.0)
    for dst in range(64):
        a_, b_, f_ = dst // 32, (dst % 32) // 16, dst % 16
        if b_ == 0:
            c[dst + 16, 1728 + dst] = -1.0
        else:
            c[dst - 16, 1728 + dst] = 1.0
    c[:, 896] = 1e-6
    c[:, 897] = 1.0
    c[:, 898] = 64e-5
    c[:, 899] = 1e-12
    return c

def pack_common(inp):
    f = lambda k: np.asarray(inp[k], np.float32)
    v32 = np.zeros((384, 128), np.float32)
    v32[R_BMOD:R_BMOD + 192] = f("b_mod").reshape(192, 128)
    v32[R_NG:R_NG + 32] = f("norm1_g").reshape(32, 128)
    v32[R_NG + 32:R_NG + 64] = f("norm2_g").reshape(32, 128)
    v32[R_FG:R_FG + 8] = f("final_g").reshape(8, 128)
    v32[R_GLAG:R_GLAG + 2] = f("gla_norm_g")
    v64 = np.zeros((64, 256), np.float32)
    ab = f("gla_a_bias")
    for j in range(2):
        for d in range(2):
            for h in range(4):
                v64[:, j * 128 + d * 4 + h] = ab[j, d, h * 64:(h + 1) * 64]
    vrw = np.zeros((128, 256), np.float32)
    mu = f("rw_mu")
    for j in range(2):
        b = j * 128
        for p in range(4):
            for q in range(3):
                vrw[:, b + q * 8 + p] = mu[j, 0, q * 512 + p * 128:q * 512 + (p + 1) * 128]
                vrw[:, b + q * 8 + 4 + p] = mu[j, 1, q * 512 + p * 128:q * 512 + (p + 1) * 128]
            for d in range(2):
                vrw[:, b + 28 + d * 4 + p] = f("rw_w0")[j, d, p * 128:(p + 1) * 128]
            vrw[:, b + 36 + p] = f("rw_a0")[j, p * 128:(p + 1) * 128]
            vrw[:, b + 40 + p] = f("rw_k_k")[j, p * 128:(p + 1) * 128]
            vrw[:, b + 44 + p] = f("rw_k_a")[j, p * 128:(p + 1) * 128]
            vrw[:, b + 48 + p] = f("rw_r_k")[j].reshape(512)[p * 128:(p + 1) * 128]
            vrw[:, b + 52 + p] = f("rw_ln_g")[j, p * 128:(p + 1) * 128]
            vrw[:, b + 56 + p] = f("rw_ln_b")[j, p * 128:(p + 1) * 128]
        vrw[0:96, b + 24] = mu[j, 0, 1536:1632]; vrw[0:96, b + 25] = mu[j, 1, 1536:1632]
        vrw[0:96, b + 26] = mu[j, 0, 1632:1728]; vrw[0:96, b + 27] = mu[j, 1, 1632:1728]
    v64[:, 140:142] = f("cq_norm_g").T
    v64[:, 142:144] = f("ck_norm_g").T
    rp = np.zeros((2, 120, 160), np.float32)
    rp[:, :, 64:95] = f("na_rpb").reshape(2, 120, 31)
    t_ = np.arange(2048)
    pos = np.stack([t_ // 64, t_ % 64], axis=-1).astype(np.float32)
    inv = (np.float32(10000.0) ** (-np.arange(0, 32, 2, dtype=np.float32) / np.float32(32))).astype(np.float32)
    ang = pos[:, :, None] * inv
    rope = np.zeros((2, 64, 2048), np.float32)
    for d_ in range(64):
        a_, f_ = d_ // 32, d_ % 16
        rope[0, d_] = np.cos(ang[:, a_, f_]); rope[1, d_] = np.sin(ang[:, a_, f_])
    rmask = np.ones((1, NT), np.float32); rmask[0, ::128] = 0.0
    com = {"w_mod": f("w_mod"), "vec32": v32, "ffn_w13": f("ffn_w13"), "ffn_w2": f("ffn_w2"),
           "ev_w_in": f("ev_w_in"), "ev_w_out": f("ev_w_out"), "od_w_in": f("od_w_in"), "od_w_out": f("od_w_out"),
           "gla_a_up": f("gla_a_up"), "rw_w_up": f("rw_w_up"), "rw_a_up": f("rw_a_up"), "rw_g_up": f("rw_g_up"),
           "vec64": v64, "vecrw": vrw, "rpbp": rp, "ropet": rope, "consts": make_consts(), "rmask": rmask}
    return com

def pack_core(inp, com, seqs):
    f = lambda k: np.asarray(inp[k], np.float32)
    m = dict(com)
    m["x"] = np.ascontiguousarray(f("x")[seqs])
    m["ctx"] = np.ascontiguousarray(f("ctx")[seqs])
    cc = np.zeros((8, D), np.float32)
    cc[:len(seqs)] = f("c")[seqs]
    cc[4] = f("c_ctx")
    m["cc"] = cc
    return m


PARTS = ('gla', 'rw', 'odd', 'ffn')


def kernel(**inputs):
    from concourse.bass_utils import run_bass_kernel_spmd
    ncores, nseq = 8, 4
    nc = bass.Bass("TRN2", target_bir_lowering=False)
    MK(nc, nseq=nseq, layers=(0, 1, 2, 3), final=True, parts=PARTS)
    com = pack_common(inputs)
    in_maps = [pack_core(inputs, com, list(range(c * nseq, (c + 1) * nseq))) for c in range(ncores)]
    res = run_bass_kernel_spmd(nc, in_maps, core_ids=list(range(ncores)))
    return np.concatenate([np.asarray(r["out"], np.float32) for r in res.results], axis=0)
```

```python
import numpy as np
import concourse.bass as bass
import concourse.mybir as mybir

F32 = mybir.dt.float32
BF16 = mybir.dt.bfloat16
ALU = mybir.AluOpType
AF = mybir.ActivationFunctionType
AX = mybir.AxisListType

SEG = 30000
NDMASEM = 24


class Buf:
    _n = 0

    def __init__(self, handle, shape, kind, blk=None):
        self.h = handle
        self.shape = list(shape)
        self.kind = kind
        Buf._n += 1
        self.id = Buf._n
        self.bdim = 1 if kind != 'dr' else 0
        if len(shape) <= self.bdim:
            self.bdim = 0
        n = shape[self.bdim]
        self.blk = blk if blk else 1
        self.nblk = (n + self.blk - 1) // self.blk
        if kind != 'dr' and len(shape) == 2 and blk is None:
            self.blk = n
            self.nblk = 1

    def full_ap(self):
        return self.h.ap() if hasattr(self.h, 'ap') and callable(getattr(self.h, 'ap')) else self.h[:]

    def __getitem__(self, key):
        if not isinstance(key, tuple):
            key = (key,)
        ap = self.h[key] if len(key) > 1 else self.h[key[0]]
        lo, hi = 0, self.shape[self.bdim]
        if len(key) > self.bdim:
            k = key[self.bdim]
            if isinstance(k, slice):
                lo = 0 if k.start is None else k.start
                hi = self.shape[self.bdim] if k.stop is None else k.stop
            else:
                lo, hi = int(k), int(k) + 1
        b0, b1 = lo // self.blk, (hi - 1) // self.blk + 1
        return View(ap, self, b0, b1)

    def all(self):
        return self[tuple(slice(None) for _ in self.shape)]


class View:
    def __init__(self, ap, buf, b0, b1):
        self.ap, self.buf, self.b0, self.b1 = ap, buf, b0, b1

    def blocks(self):
        return [(self.buf.id, b) for b in range(self.b0, self.b1)]

    def re(self, fn):
        return View(fn(self.ap), self.buf, self.b0, self.b1)

    def __getitem__(self, key):
        return View(self.ap[key], self.buf, self.b0, self.b1)


class Sched:
    ENGS = ['pe', 'act', 'dve', 'pool', 'sp']

    def __init__(self, nc):
        self.nc = nc
        self.q = {e: [] for e in self.ENGS}
        self.cnt = {e: 0 for e in self.ENGS}
        self.lastw = {}
        self.readers = {}
        self.seen = {e: {s: -1 for s in self.ENGS} for e in self.ENGS}
        self.seen_dma = {e: set() for e in self.ENGS}
        self.ndma = 0
        self.dma_sem_last = [None] * NDMASEM
        self.dma_sem_cnt = [0] * NDMASEM
        self.nsb = 0

    def sb(self, name, shape, dtype, blk=None):
        h = self.nc.alloc_sbuf_tensor(name, list(shape), dtype)
        return Buf(h, shape, 'sb', blk)

    def ps(self, name, shape, dtype=F32, blk=None):
        h = self.nc.alloc_psum_tensor(name, list(shape), dtype)
        return Buf(h, shape, 'ps', blk)

    def dram(self, name, shape, dtype, kind="Internal", blk=None):
        h = self.nc.dram_tensor(name, list(shape), dtype, kind=kind)
        return Buf(h, shape, 'dr', blk)

    def _deps(self, reads, writes):
        deps = []
        for v in reads:
            for b in v.blocks():
                t = self.lastw.get(b)
                if t is not None:
                    deps.append(t)
        for v in writes:
            for b in v.blocks():
                t = self.lastw.get(b)
                if t is not None:
                    deps.append(t)
                deps.extend(self.readers.get(b, ()))
        return deps

    def _record(self, tok, reads, writes):
        for v in reads:
            for b in v.blocks():
                lst = self.readers.setdefault(b, [])
                if tok[0] == 'c':
                    lst[:] = [t for t in lst if not (t[0] == 'c' and t[1] == tok[1])]
                lst.append(tok)
        for v in writes:
            for b in v.blocks():
                self.lastw[b] = tok
                self.readers[b] = []

    def _waits(self, eng, deps):
        waits = []
        best = {}
        for t in deps:
            if t[0] == 'c':
                _, src, idx = t
                if idx <= self.seen[eng][src]:
                    continue
                if idx > best.get(src, -1):
                    best[src] = idx
            else:
                _, did, sem, val = t
                if did in self.seen_dma[eng]:
                    continue
                self.seen_dma[eng].add(did)
                waits.append(('d', sem, val))
        for src, idx in best.items():
            self.seen[eng][src] = idx
            waits.append(('c', src, idx))
        return waits

    def op(self, eng, fn, reads=(), writes=()):
        reads = [v for v in reads if isinstance(v, View)]
        writes = [v for v in writes if isinstance(v, View)]
        deps = self._deps(reads, writes)
        waits = self._waits(eng, deps)
        idx = self.cnt[eng]
        self.cnt[eng] += 1
        tok = ('c', eng, idx)
        self.q[eng].append((fn, waits, ('c', eng, idx)))
        self._record(tok, reads, writes)
        return tok

    def dma(self, eng, out, in_, **kw):
        s = self.ndma % NDMASEM
        deps = self._deps([in_], [out])
        if self.dma_sem_last[s] is not None:
            deps.append(self.dma_sem_last[s])
        waits = self._waits(eng, deps)
        self.dma_sem_cnt[s] += 1
        tok = ('d', self.ndma, s, 16 * self.dma_sem_cnt[s])
        self.dma_sem_last[s] = tok
        self.ndma += 1
        oa, ia = out.ap, in_.ap

        def fn(e, oa=oa, ia=ia, kw=kw):
            return e.dma_start(out=oa, in_=ia, **kw)
        self.q[eng].append((fn, waits, tok))
        self._record(tok, [in_], [out])
        return tok

    def wait_all(self, eng):
        deps = []
        for e in self.ENGS:
            if self.cnt[e] > 0:
                deps.append(('c', e, self.cnt[e] - 1))
        for s in range(NDMASEM):
            if self.dma_sem_last[s] is not None:
                deps.append(self.dma_sem_last[s])
        waits = self._waits(eng, deps)
        self.q[eng].append((None, waits, None))

    def emit(self):
        nc = self.nc
        csem = {}
        for e in self.ENGS:
            nseg = (self.cnt[e] + SEG - 1) // SEG
            csem[e] = [nc.alloc_semaphore(f"c_{e}_{i}") for i in range(nseg)]
        dsem = [nc.alloc_semaphore(f"d_{i}") for i in range(NDMASEM)]
        engobj = {'pe': 'tensor', 'act': 'scalar', 'dve': 'vector', 'pool': 'gpsimd', 'sp': 'sync'}

        def run(ename):
            def body(engine):
                for fn, waits, sig in self.q[ename]:
                    for w in waits:
                        if w[0] == 'c':
                            _, src, idx = w
                            engine.wait_ge(csem[src][idx // SEG], idx % SEG + 1)
                        else:
                            _, sem, val = w
                            engine.wait_ge(dsem[sem], val)
                    if fn is None:
                        continue
                    ins = fn(engine)
                    if sig[0] == 'c':
                        _, src, idx = sig
                        ins.then_inc(csem[src][idx // SEG], 1)
                    else:
                        _, did, sem, val = sig
                        ins.then_inc(dsem[sem], 16)
            return body

        with nc.Block() as block:
            for e in self.ENGS:
                if not self.q[e]:
                    continue
                getattr(block, engobj[e])(run(e))

    def mm(self, out, lhsT, rhs, start=True, stop=True, **kw):
        o, l, r = out.ap, lhsT.ap, rhs.ap
        reads = [lhsT, rhs] + ([] if start else [])
        return self.op('pe', lambda e: e.matmul(o, l, r, start=start, stop=stop, **kw),
                       reads, [out])

    def tr(self, out, in_, ident):
        o, i, d = out.ap, in_.ap, ident.ap
        return self.op('pe', lambda e: e.transpose(o, i, d), [in_, ident], [out])

    def act(self, out, in_, func, bias=0.0, scale=1.0, accum_out=None, eng='act'):
        o, i = out.ap, in_.ap
        b = bias.ap if isinstance(bias, View) else bias
        s = scale.ap if isinstance(scale, View) else scale
        kw = {}
        wr = [out]
        if accum_out is not None:
            kw['accum_out'] = accum_out.ap
            wr.append(accum_out)
        return self.op('act', lambda e: e.activation(o, i, func, bias=b, scale=s, **kw),
                       [in_, bias, scale], wr)

    def tt(self, out, a, b, op, eng='dve'):
        o, x, y = out.ap, a.ap, b.ap
        return self.op(eng, lambda e: e.tensor_tensor(o, x, y, op), [a, b], [out])

    def ts(self, out, a, s1, s2, op0, op1=None, eng='dve', accum_out=None):
        o, x = out.ap, a.ap
        p1 = s1.ap if isinstance(s1, View) else s1
        p2 = s2.ap if isinstance(s2, View) else s2
        kw = {}
        wr = [out]
        if accum_out is not None:
            kw['accum_out'] = accum_out.ap
            wr.append(accum_out)
        if op1 is None:
            return self.op(eng, lambda e: e.tensor_scalar(o, x, p1, None, op0, **kw), [a, s1], wr)
        return self.op(eng, lambda e: e.tensor_scalar(o, x, p1, p2, op0, op1, **kw), [a, s1, s2], wr)

    def stt(self, out, a, s, b, op0, op1):
        o, x, y = out.ap, a.ap, b.ap
        p = s.ap if isinstance(s, View) else s
        return self.op('dve', lambda e: e.scalar_tensor_tensor(o, x, p, y, op0, op1), [a, s, b], [out])

    def copy(self, out, in_, eng='dve'):
        o, i = out.ap, in_.ap
        if eng == 'act':
            return self.op('act', lambda e: e.copy(o, i), [in_], [out])
        return self.op(eng, lambda e: e.tensor_copy(o, i), [in_], [out])

    def memset(self, out, val, eng='dve'):
        o = out.ap
        return self.op(eng, lambda e: e.memset(o, val), [], [out])

    def recip(self, out, in_):
        o, i = out.ap, in_.ap
        return self.op('dve', lambda e: e.reciprocal(o, i), [in_], [out])

    def reduce(self, out, in_, op, axis=AX.X):
        o, i = out.ap, in_.ap
        return self.op('dve', lambda e: e.tensor_reduce(o, i, axis, op), [in_], [out])

    def scan(self, out, d0, d1, init, op0, op1):
        o, a, b = out.ap, d0.ap, d1.ap
        ini = init.ap if isinstance(init, View) else init
        return self.op('dve', lambda e: e.tensor_tensor_scan(o, a, b, ini, op0, op1), [d0, d1, init], [out])


D = 1024
NT = 2304
CTX = 256
TW = 384
NTILE = NT // TW
CH = 128
NCH = NT // CH
DFF = 2816
EVEN_IN = 3296
ODD_IN = 2304
SLOT = 1152
NSLOT = 20
FT = 768
R_BMOD, R_NG, R_FG, R_GLAG = 0, 192, 256, 264
C_ID, C_TRF, C_TRB, C_ONE, C_STF, C_STB, C_BLK, C_EPS, C_1 = 0, 128, 256, 384, 512, 640, 768, 896, 897
KAPPA = float(np.exp(-0.5))
RW0 = 1568


def segs(c0, c1):
    out = []
    if c0 < CTX:
        out.append((c0, min(c1, CTX), True))
    if c1 > CTX:
        out.append((max(c0, CTX), c1, False))
    return out


class MK:
    def __init__(self, nc, nseq=4, layers=(0, 1, 2, 3), final=True, parts=('gla', 'rw', 'odd', 'ffn'), debug=False):
        self.nc = nc
        self.debug = debug
        self.dbgn = 0
        self.S = Sched(nc)
        self.nseq, self.layers, self.final, self.parts = nseq, layers, final, parts
        self.psn = 0
        self.stn = 0
        self.decl_inputs()
        self.alloc()
        self.prologue()
        for s in range(nseq):
            self.load_seq(s)
            for l in layers:
                self.layer(s, l)
            self.store_seq(s)
        self.S.wait_all('sp')
        self.S.emit()

    def decl_inputs(self):
        S, ns = self.S, self.nseq
        I = lambda n, sh, dt=F32: S.dram(n, sh, dt, kind="ExternalInput")
        self.x = I("x", [ns, 2048, D])
        self.ctx = I("ctx", [ns, CTX, D])
        self.cc = I("cc", [8, D])
        self.w_mod = I("w_mod", [4, D, 6 * D])
        self.vec32 = I("vec32", [384, 128])
        self.ffn_w13 = I("ffn_w13", [4, D, 2 * DFF])
        self.ffn_w2 = I("ffn_w2", [4, DFF, D])
        self.ev_w_in = I("ev_w_in", [2, D, EVEN_IN])
        self.ev_w_out = I("ev_w_out", [2, D, D])
        self.od_w_in = I("od_w_in", [2, D, ODD_IN])
        self.od_w_out = I("od_w_out", [2, D, D])
        self.gla_a_up = I("gla_a_up", [2, 2, 16, 256])
        self.rw_w_up = I("rw_w_up", [2, 2, 32, 512])
        self.rw_a_up = I("rw_a_up", [2, 32, 512])
        self.rw_g_up = I("rw_g_up", [2, 96, 512])
        self.vec64 = I("vec64", [64, 256])
        self.consts = I("consts", [128, 1792])
        self.rpbp = I("rpbp", [2, 120, 160])
        self.ropet = I("ropet", [2, 64, 2048])
        self.s_tab = [S.dram(f"s_tab_{j}", [64, 8, 960], BF16) for j in range(2)]
        self.rmask = I("rmask", [1, NT])
        self.vecrw = I("vecrw", [128, 256])
        self.rwp = S.dram("rwp", [5, 128, NT], BF16, kind=("ExternalOutput" if self.debug else "Internal"))
        self.out = S.dram("out", [ns, 2048, D], F32, kind="ExternalOutput")
        Wd = lambda n, nk, nc_: S.dram(n, [128, nk, nc_], BF16)
        self.s_w13 = [Wd(f"s_w13_{l}", 8, 2 * DFF) for l in range(4)]
        self.s_w2 = [Wd(f"s_w2_{l}", 22, D) for l in range(4)]
        self.s_evin = [Wd(f"s_evin_{j}", 8, EVEN_IN) for j in range(2)]
        self.s_evout = [Wd(f"s_evout_{j}", 8, D) for j in range(2)]
        self.s_odin = [Wd(f"s_odin_{j}", 8, ODD_IN) for j in range(2)]
        self.s_odout = [Wd(f"s_odout_{j}", 8, D) for j in range(2)]

    def alloc(self):
        S = self.S
        self.xT = S.sb("xT", [128, 8, NT], F32)
        self.hT = S.sb("hT", [128, 8, NT], BF16)
        self.ps = S.ps("ps", [128, 8, 512], F32)
        self.cst = S.sb("cst", [128, 328], F32, blk=328)
        self.cstb = S.sb("cstb", [128, 1792], BF16, blk=1792)
        self.modT = S.sb("modT", [128, 192, 5], F32, blk=192)
        self.modA = S.sb("modA", [128, 64, 5], F32, blk=64)
        self.vT = S.sb("vT", [128, 384], F32, blk=384)
        self.v64 = S.sb("v64", [64, 256], F32, blk=256)
        self.stage = [S.sb(f"stage{i}", [128, 3072], BF16, blk=3072) for i in range(3)]
        self.work = S.sb("work", [128, 10, TW], F32)
        self.scr = S.sb("scr", [128, NSLOT * SLOT], BF16, blk=SLOT)
        self.rm = S.sb("rm", [128, NT], BF16, blk=NT)
        self.rww = S.sb("rww", [128, 3, 512], BF16)
        self.vrw = S.sb("vrw", [128, 256], F32, blk=256)
        self.vd = S.sb("vd", [128, 128], F32, blk=128)

    def dump(self, name, view, shape, dt=F32):
        if not self.debug:
            return
        d = self.S.dram("dbg_" + name, list(shape), dt, kind="ExternalOutput")
        self.S.dma('sp', d.all(), view)

    nrot = 8

    def wv2(self, row, n, dt=F32, p0=0, p1=128):
        ap = self.work.h[p0:p1, row:row + 2, :].rearrange("p a t -> p (a t)")
        if dt == BF16:
            ap = ap.bitcast(BF16)
        return View(ap[:, 0:n], self.work, row, row + 2)

    def bank(self):
        b = self.psn % self.nrot
        self.psn += 1
        return b

    def sv(self, e0, n, dt=BF16, p0=0, p1=128):
        nb = n * (2 if dt == F32 else 1)
        ap = self.scr.h[p0:p1, e0:e0 + nb]
        if dt == F32:
            ap = ap.bitcast(F32)
        return View(ap, self.scr, e0 // SLOT, (e0 + nb - 1) // SLOT + 1)

    def wv(self, row, n=TW, dt=F32, p0=0, p1=128, c0=0):
        if dt == F32:
            ap = self.work.h[p0:p1, row, c0:c0 + n]
        else:
            ap = self.work.h[p0:p1, row, c0:c0 + (n + 1) // 2].bitcast(BF16)[:, 0:n]
        return View(ap, self.work, row, row + 1)

    def sub(self, v, key):
        return View(v.ap[key], v.buf, v.b0, v.b1)

    def ldw(self, src_view, nk, ncols, eng='sp'):
        st = self.stage[self.stn % 3]
        self.stn += 1
        dst = st[:, 0:nk * ncols].re(lambda a: a.rearrange("p (k c) -> p k c", k=nk))
        self.S.dma(eng, dst, src_view)
        return dst

    def ident(self, n=128):
        return self.cst[0:n, 0:n]

    def identb(self, n=128):
        return self.cstb[0:n, 0:n]

    def onesb(self, k=128, m=128):
        return self.cstb[0:k, C_ONE:C_ONE + m]

    def trib(self, d):
        return self.cstb[:, C_TRF + 128 * d:C_TRF + 128 * d + 128]

    def tris(self, d):
        return self.cstb[:, C_STF + 128 * d:C_STF + 128 * d + 128]

    def eps(self, p=128):
        return self.cst[0:p, 256:257]

    def one(self, p=128):
        return self.cst[0:p, 257:258]

    def gneps(self, p=128):
        return self.cst[0:p, 258:259]

    def blk32(self):
        return self.cst[:, 128:256]

    def blkb(self):
        return self.cstb[:, C_BLK:C_BLK + 128]

    def bmask(self, col, n):
        return self.cstb[:, col:col + 128].re(lambda a: a.rearrange("p (o t) -> p o t", o=1).broadcast_to([128, n, 128]))

    def cast_w(self, dst, src_view, ncols):
        for c0 in range(0, ncols, 1024):
            c1 = min(ncols, c0 + 1024)
            sv = src_view[:, c0:c1].re(lambda a: a.rearrange("(k p) c -> p k c", p=128))
            self.S.dma('pool', dst[:, :, c0:c1], sv)

    def prologue(self):
        S = self.S
        cf = self.sv(0, 1792, F32)
        S.dma('sp', cf, self.consts.all())
        S.copy(self.cstb.all(), cf)
        S.copy(self.cst[:, 0:128], self.sub(cf, (slice(None), slice(0, 128))))
        S.copy(self.cst[:, 128:256], self.sub(cf, (slice(None), slice(C_BLK, C_BLK + 128))))
        S.copy(self.cst[:, 256:264], self.sub(cf, (slice(None), slice(896, 904))))
        S.copy(self.cst[:, 264:328], self.sub(cf, (slice(None), slice(1664, 1728))))
        S.dma('sp', self.vrw.all(), self.vecrw.all())
        for jj in range(2):
            b0, b2 = jj * 128, jj * 64
            for (cp, cn, co, w) in ((0, 4, 0, 4), (8, 12, 4, 4), (16, 20, 8, 4), (24, 25, 12, 1), (26, 27, 13, 1)):
                S.stt(self.vd[:, b2 + co:b2 + co + w], self.vrw[:, b0 + cp:b0 + cp + w], -1.0, self.vrw[:, b0 + cn:b0 + cn + w], ALU.mult, ALU.subtract)
                S.ts(self.vd[:, b2 + co:b2 + co + w], self.vd[:, b2 + co:b2 + co + w], 1.0, None, ALU.add)
            S.ts(self.vd[:, b2 + 16:b2 + 20], self.vrw[:, b0 + 44:b0 + 48], -1.0, 1.0, ALU.mult, ALU.add)
        S.dma('sp', self.v64.all(), self.vec64.all())
        S.ts(self.v64[:, 0:8], self.v64[:, 0:8], -1.0, None, ALU.mult)
        S.ts(self.v64[:, 128:136], self.v64[:, 128:136], -1.0, None, ALU.mult)
        rmf = self.sv(4 * SLOT, NT, F32)
        S.dma('sp', rmf, self.rmask.all().re(lambda a: a.broadcast_to([128, NT])))
        S.copy(self.rm.all(), rmf)
        for q in range(3):
            tmp = self.wv(q, 128)
            S.dma('sp', tmp, self.vec32[q * 128:(q + 1) * 128, :])
            b = self.bank()
            S.tr(self.ps[:, b, 0:128], tmp, self.ident())
            S.copy(self.vT[:, q * 128:(q + 1) * 128], self.ps[:, b, 0:128])
        used = sorted(set(self.layers))
        for l in used:
            j = l // 2
            if l % 2 == 0:
                self.cast_w(self.s_evin[j], self.ev_w_in[j], EVEN_IN)
                self.cast_w(self.s_evout[j], self.ev_w_out[j], D)
            else:
                self.cast_w(self.s_odin[j], self.od_w_in[j], ODD_IN)
                self.cast_w(self.s_odout[j], self.od_w_out[j], D)
            self.cast_w(self.s_w13[l], self.ffn_w13[l], 2 * DFF)
            self.cast_w(self.s_w2[l], self.ffn_w2[l], D)
        for l in used:
            if l % 2 == 1:
                jj = l // 2
                for half in range(2):
                    tabf = self.sv(0, 60 * 64, F32, 0, 64)
                    tabb = self.sv(8 * SLOT, 60 * 64, BF16, 0, 64)
                    src = View(bass.AP(self.rpbp.h, jj * 120 * 160 + half * 60 * 160 + 16, [[1, 64], [160, 60], [1, 64]]),
                               self.rpbp, 0, 2)
                    S.dma('sp', tabf.re(lambda a: a.rearrange("p (n q) -> p n q", n=60)), src)
                    S.tt(tabb.re(lambda a: a.rearrange("p (n q) -> p n q", n=60)), tabf.re(lambda a: a.rearrange("p (n q) -> p n q", n=60)),
                         self.cst[0:64, 264:328].re(lambda a: a.rearrange("p (o q) -> p o q", o=1).broadcast_to([64, 60, 64])), ALU.add)
                    S.dma('sp', self.s_tab[jj][:, half * 4:half * 4 + 4, :].re(lambda a: a.rearrange("p h q -> p (h q)")), tabb)
        ccv = self.sv(16 * SLOT, D, F32, 0, 8)
        S.dma('sp', ccv, self.cc.all())
        S.act(ccv, ccv, AF.Silu)
        scT = self.wv(4, 64)
        for k in range(8):
            b = self.bank()
            S.tr(self.ps[:, b, 0:8], self.sub(ccv, (slice(None), slice(k * 128, (k + 1) * 128))), self.ident(8))
            S.copy(self.sub(scT, (slice(None), slice(k * 8, (k + 1) * 8))), self.ps[:, b, 0:8])
        for l in used:
            for piece in range(12):
                wv = self.sv((piece % 2) * 8 * SLOT, 8 * 512, F32).re(lambda a: a.rearrange("p (k c) -> p k c", k=8))
                S.dma('sp' if piece % 2 else 'act', wv, self.w_mod[l][:, piece * 512:(piece + 1) * 512].re(
                    lambda a: a.rearrange("(k p) c -> p k c", p=128)))
                for mm in range(4):
                    wm = piece * 4 + mm
                    b = self.bank()
                    for k in range(8):
                        S.mm(self.ps[:, b, 0:8], self.sub(wv, (slice(None), k, slice(mm * 128, (mm + 1) * 128))),
                             self.sub(scT, (slice(None), slice(k * 8, (k + 1) * 8))), start=(k == 0), stop=(k == 7))
                    r = R_BMOD + l * 48 + wm
                    S.ts(self.modT[:, l * 48 + wm, :], self.ps[:, b, 0:5], self.vT[:, r:r + 1], None, ALU.add)
            for n in range(2):
                for m in range(8):
                    r = R_NG + n * 32 + l * 8 + m
                    S.ts(self.modA[:, (l * 2 + n) * 8 + m, :], self.modT[:, l * 48 + (3 * n + 1) * 8 + m, :],
                         1.0, self.vT[:, r:r + 1], ALU.add, ALU.mult)

    def mod(self, l, w, m, col):
        return self.modT[:, l * 48 + w * 8 + m, col:col + 1]

    def mA(self, l, n, m, col):
        return self.modA[:, (l * 2 + n) * 8 + m, col:col + 1]

    def io_tile(self, t):
        return self.sv((t % 2) * 4 * SLOT, D, F32)

    def load_seq(self, s):
        S = self.S
        for t in range(NCH):
            src = self.ctx[s][t * 128:(t + 1) * 128, :] if t < 2 else self.x[s][(t - 2) * 128:(t - 1) * 128, :]
            tl = self.io_tile(t)
            S.dma('sp' if t % 2 == 0 else 'act', tl, src)
            for k in range(8):
                b = self.bank()
                S.tr(self.ps[:, b, 0:128], self.sub(tl, (slice(None), slice(k * 128, (k + 1) * 128))), self.ident())
                S.copy(self.xT[:, k, t * 128:(t + 1) * 128], self.ps[:, b, 0:128], eng='dve' if k % 2 else 'act')

    def store_seq(self, s):
        S = self.S
        if self.final:
            self.norm(None, None, s, final=True)
        for t in range(2, NCH):
            tl = self.io_tile(t)
            for k in range(8):
                b = self.bank()
                S.tr(self.ps[:, b, 0:128], self.xT[:, k, t * 128:(t + 1) * 128], self.ident())
                S.copy(self.sub(tl, (slice(None), slice(k * 128, (k + 1) * 128))), self.ps[:, b, 0:128],
                       eng='dve' if k % 2 else 'act')
            S.dma('sp' if t % 2 == 0 else 'act', self.out[s][(t - 2) * 128:(t - 1) * 128, :], tl)

    def norm(self, l, n, s, final=False):
        S = self.S
        for t in range(NTILE):
            c0, c1 = t * TW, (t + 1) * TW
            sq = [self.wv(k // 2, TW, BF16, c0=(k % 2) * (TW // 2)) for k in range(8)]
            for k in range(8):
                S.act(sq[k], self.xT[:, k, c0:c1], AF.Square)
            b = self.bank()
            for k in range(8):
                S.mm(self.ps[:, b, 0:TW], self.onesb(), sq[k], start=(k == 0), stop=(k == 7))
            rstd = self.wv(4)
            S.act(rstd, self.ps[:, b, 0:TW], AF.Sqrt, bias=self.eps(), scale=1.0 / D)
            S.recip(rstd, rstd)
            for k in range(8):
                if final:
                    S.stt(self.xT[:, k, c0:c1], self.xT[:, k, c0:c1], self.vT[:, R_FG + k:R_FG + k + 1], rstd,
                          ALU.mult, ALU.mult)
                    continue
                tmp = self.wv(5 + (k % 2))
                S.tt(tmp, self.xT[:, k, c0:c1], rstd, ALU.mult)
                for (a, e, isc) in segs(c0, c1):
                    col = 4 if isc else s
                    S.act(self.hT[:, k, a:e], self.sub(tmp, (slice(None), slice(a - c0, e - c0))), AF.Identity,
                          bias=self.mod(l, 3 * n, k, col), scale=self.mA(l, n, k, col))

    def resid(self, l, gate_w, m, c0, c1, ps_view, s):
        for (a, e, isc) in segs(c0, c1):
            col = 4 if isc else s
            self.S.stt(self.xT[:, m, a:e], self.sub(ps_view, (slice(None), slice(a - c0, e - c0))),
                       self.mod(l, gate_w, m, col), self.xT[:, m, a:e], ALU.mult, ALU.add)

    def ffn(self, l, s):
        S = self.S
        FB = 1024
        gj = lambda j, off, w: self.sv(j * FB + off, w)
        n = 0
        for base in range(0, NT, FB):
            bw = min(FB, NT - base)
            tiles = [(o, min(512, bw - o)) for o in range(0, bw, 512)]
            for j in range(22):
                wa = self.ldw(self.s_w13[l][:, :, j * 128:(j + 1) * 128], 8, 128, eng='sp')
                wb = self.ldw(self.s_w13[l][:, :, DFF + j * 128:DFF + (j + 1) * 128], 8, 128, eng='act')
                for (o, w) in tiles:
                    c0 = base + o
                    ba, bb = self.bank(), self.bank()
                    for k in range(8):
                        S.mm(self.ps[:, ba, 0:w], wa[:, k, :], self.hT[:, k, c0:c0 + w], start=(k == 0), stop=(k == 7))
                    for k in range(8):
                        S.mm(self.ps[:, bb, 0:w], wb[:, k, :], self.hT[:, k, c0:c0 + w], start=(k == 0), stop=(k == 7))
                    sl = self.wv2(5 + 2 * (n % 2), w)
                    n += 1
                    S.act(sl, self.ps[:, ba, 0:w], AF.Silu)
                    S.tt(gj(j, o, w), sl, self.ps[:, bb, 0:w], ALU.mult)
            for m in range(8):
                w2 = self.ldw(self.s_w2[l][:, :, m * 128:(m + 1) * 128], 22, 128, eng='sp' if m % 2 else 'act')
                for (o, w) in tiles:
                    c0 = base + o
                    b = self.bank()
                    for j in range(22):
                        S.mm(self.ps[:, b, 0:w], w2[:, j, :], gj(j, o, w), start=(j == 0), stop=(j == 21))
                    self.resid(l, 5, m, c0, c0 + w, self.ps[:, b, 0:w], s)

    def layer(self, s, l):
        self.norm(l, 0, s)
        if l % 2 == 0:
            if 'gla' in self.parts:
                self.gla(l, s)
            if 'rw' in self.parts:
                self.rwkv(l, s)
        elif 'odd' in self.parts:
            self.odd(l, s)
        if 'ffn' in self.parts:
            self.norm(l, 1, s)
            self.ffn(l, s)

    def proj(self, wsrc, c0, M, cb, p0=0):
        S = self.S
        w = self.ldw(wsrc[:, :, c0:c0 + M], 8, M)
        for t in range(NTILE):
            b = self.bank()
            for k in range(8):
                S.mm(self.ps[p0:p0 + M, b, 0:TW], w[:, k, :], self.hT[:, k, t * TW:(t + 1) * TW], start=(k == 0), stop=(k == 7))
            cb(t, self.ps[p0:p0 + M, b, 0:TW])

    def gla(self, l, s):
        S = self.S
        j = l // 2
        win, wout = self.s_evin[j], self.s_evout[j]
        tsl = lambda t: (slice(None), slice(t * TW, (t + 1) * TW))
        adT = self.sv(0, NT, BF16, 0, 48)
        vtok = self.sv(2 * SLOT, NT)
        qf = self.sv(4 * SLOT, NT, BF16, 0, 64)
        kf = self.sv(6 * SLOT, NT, BF16, 0, 64)
        cw = self.sv(8 * SLOT, NT, F32, 0, 64)
        qb = self.sv(12 * SLOT, NT, BF16, 0, 64)
        kb = self.sv(14 * SLOT, NT, BF16, 0, 64)
        oacc = self.sv(16 * SLOT, NT, F32)
        aup = self.wv(5, 256, BF16, 0, 48)
        aupf = self.wv(6, 256, F32, 0, 48)
        etot = self.wv(7, NCH, F32, 0, 64)
        Sst = self.wv(7, 128, F32, 0, 64, c0=64)
        Sbf = self.wv(7, 128, BF16, 0, 64, c0=192)
        kbtok = self.wv(7, 64, BF16, 0, 128, c0=256)
        attm = self.wv(8, 128, BF16)
        for d in range(2):
            S.dma('sp', self.sub(aupf, (slice(32 * d, 32 * d + 16), slice(None))), self.gla_a_up[j][d])
            S.copy(self.sub(aup, (slice(32 * d, 32 * d + 16), slice(None))),
                   self.sub(aupf, (slice(32 * d, 32 * d + 16), slice(None))))
            self.proj(win, 1536 + 16 * d, 16,
                      lambda t, ps, d=d: S.copy(self.sub(adT, (slice(32 * d, 32 * d + 16), slice(t * TW, (t + 1) * TW))), ps),
                      p0=32 * d)
        for h in range(4):
            self.proj(win, h * 64, 64, lambda t, ps: S.copy(self.sub(qf, tsl(t)), ps, eng='act'))
            self.proj(win, 256 + h * 64, 64, lambda t, ps: S.copy(self.sub(kf, tsl(t)), ps))
            wv = self.ldw(win[:, :, 512 + h * 128:512 + (h + 1) * 128], 8, 128)
            for c in range(NCH):
                b = self.bank()
                for k in range(8):
                    S.mm(self.ps[:, b, 0:128], self.hT[:, k, c * 128:(c + 1) * 128], wv[:, k, :], start=(k == 0), stop=(k == 7))
                S.copy(self.sub(vtok, (slice(None), slice(c * 128, (c + 1) * 128))), self.ps[:, b, 0:128],
                       eng='act' if c % 2 else 'dve')
            for d in range(2):
                rv = (lambda a: a) if d == 0 else (lambda a: a[:, ::-1])
                for t in range(NTILE):
                    b = self.bank()
                    S.mm(self.ps[0:64, b, 0:TW], self.sub(aup, (slice(32 * d, 32 * d + 16), slice(h * 64, (h + 1) * 64))),
                         self.sub(adT, (slice(32 * d, 32 * d + 16), slice(t * TW, (t + 1) * TW))))
                    cwt = self.sub(cw, tsl(t))
                    col = j * 128 + d * 4 + h
                    S.act(cwt, self.ps[0:64, b, 0:TW], AF.Exp, bias=self.v64[:, col:col + 1], scale=-1.0)
                    S.act(cwt, cwt, AF.Ln, bias=self.one(64), scale=1.0)
                if d == 0:
                    S.scan(cw, self.rm[0:64, :], cw, 0.0, ALU.mult, ALU.add)
                else:
                    S.scan(cw.re(rv), self.rm[0:64, :], cw.re(rv), 0.0, ALU.mult, ALU.add)
                S.act(etot, self.sub(cw, (slice(None), slice(127 if d == 0 else 0, NT, 128))), AF.Exp, scale=-1.0 / 16)
                for t in range(NTILE):
                    e1, e2 = self.wv(0, TW, F32, 0, 64), self.wv(1, TW, F32, 0, 64)
                    cwt = self.sub(cw, tsl(t))
                    S.act(e1, cwt, AF.Exp, scale=-1.0 / 16)
                    S.stt(self.sub(qb, tsl(t)), self.sub(qf, tsl(t)), 0.125, e1, ALU.mult, ALU.mult)
                    S.act(e2, cwt, AF.Exp, scale=1.0 / 16)
                    S.tt(self.sub(kb, tsl(t)), self.sub(kf, tsl(t)), e2, ALU.mult)
                S.memset(Sst, 0.0)
                S.memset(Sbf, 0.0)
                order = list(range(NCH)) if d == 0 else [1, 0] + list(range(NCH - 1, 1, -1))
                for c in order:
                    cs = (slice(None), slice(c * 128, (c + 1) * 128))
                    kbc, qbc, vc = self.sub(kb, cs), self.sub(qb, cs), self.sub(vtok, cs)
                    b1 = self.bank()
                    tp = View(self.ps.h[:, b1, 0:32].bitcast(BF16), self.ps, b1, b1 + 1)
                    S.tr(tp, kbc, self.identb(64))
                    S.copy(kbtok, tp, eng='act')
                    b2 = self.bank()
                    S.mm(self.ps[:, b2, 0:128], kbc, qbc)
                    S.tt(attm, self.ps[:, b2, 0:128], self.trib(d), ALU.mult)
                    b3 = self.bank()
                    S.mm(self.ps[:, b3, 0:128], vc, attm, start=True, stop=False)
                    S.mm(self.ps[:, b3, 0:128], Sbf, qbc, start=False, stop=True)
                    oc = self.sub(oacc, cs)
                    if d == 0:
                        S.copy(oc, self.ps[:, b3, 0:128], eng='act')
                    else:
                        S.tt(oc, oc, self.ps[:, b3, 0:128], ALU.add)
                    b4 = self.bank()
                    S.mm(self.ps[0:64, b4, 0:128], kbtok, vc)
                    S.tt(Sst, Sst, self.ps[0:64, b4, 0:128], ALU.add)
                    S.ts(Sst, Sst, self.sub(etot, (slice(None), slice(c, c + 1))), None, ALU.mult)
                    S.copy(Sbf, Sst, eng='act')
            wg = self.ldw(win[:, :, 1024 + h * 128:1024 + (h + 1) * 128], 8, 128)
            wo = self.ldw(wout[:, h:h + 1, :], 1, D)
            for t in range(NTILE):
                c0, c1 = t * TW, (t + 1) * TW
                ot = self.sub(oacc, tsl(t))
                sqb = self.wv(0, TW, BF16)
                S.act(sqb, ot, AF.Square)
                b = self.bank()
                S.mm(self.ps[:, b, 0:TW], self.onesb(), sqb)
                rstd = self.wv(1)
                S.act(rstd, self.ps[:, b, 0:TW], AF.Sqrt, bias=self.eps(), scale=1.0 / 128)
                S.recip(rstd, rstd)
                bg = self.bank()
                for k in range(8):
                    S.mm(self.ps[:, bg, 0:TW], wg[:, k, :], self.hT[:, k, c0:c1], start=(k == 0), stop=(k == 7))
                sg = self.wv(2)
                S.act(sg, self.ps[:, bg, 0:TW], AF.Silu)
                y1 = self.wv(3)
                S.stt(y1, ot, self.vT[:, R_GLAG + j:R_GLAG + j + 1], rstd, ALU.mult, ALU.mult)
                yb = self.wv(4, TW, BF16)
                S.tt(yb, y1, sg, ALU.mult)
                for m in range(8):
                    b = self.bank()
                    S.mm(self.ps[:, b, 0:TW], wo[:, 0, m * 128:(m + 1) * 128], yb)
                    self.resid(l, 2, m, c0, c1, self.ps[:, b, 0:TW], s)

    def tshift(self, zp, dst_fn, P, cmc, mpc, mnc):
        S = self.S
        i = 0
        for t in range(NTILE):
            for (a, e, isc) in segs(t * TW, (t + 1) * TW):
                z0 = a + (1 if isc else 2)
                n = e - a
                tmp = self.wv(5 + i % 2, n, F32, 0, P)
                i += 1
                zs = lambda o: self.sub(zp, (slice(0, P), slice(z0 + o, z0 + o + n)))
                S.ts(tmp, zs(0), cmc, None, ALU.mult)
                S.stt(tmp, zs(-1), mpc, tmp, ALU.mult, ALU.add)
                S.stt(dst_fn(a, e), zs(1), mnc, tmp, ALU.mult, ALU.add)

    def rwkv(self, l, s):
        S = self.S
        j = l // 2
        win, wout = self.s_evin[j], self.s_evout[j]
        vb, vb2 = j * 128, j * 64
        vc = lambda c, P=128: self.vrw[0:P, vb + c:vb + c + 1]
        vdc = lambda c, P=128: self.vd[0:P, vb2 + c:vb2 + c + 1]
        tsl = lambda t: (slice(None), slice(t * TW, (t + 1) * TW))
        LR1 = self.sv(0, NT, BF16, 0, 96)
        LR2 = self.sv(2 * SLOT, NT, BF16, 0, 96)
        oacc = self.sv(4 * SLOT, NT, F32)
        zp = self.sv(8 * SLOT, NT + 3, F32)
        rs_ = self.sv(13 * SLOT, NT)
        ks_ = self.sv(15 * SLOT, NT)
        vs_ = self.sv(17 * SLOT, NT)
        S.dma('pool', self.rww[0:64, 0, :], self.rw_w_up[j].re(lambda a: a.rearrange("d r c -> (d r) c")))
        S.dma('pool', self.rww[64:96, 1, :], self.rw_a_up[j])
        S.dma('pool', self.rww[0:96, 2, :], self.rw_g_up[j])
        for c in (0, 257, NT + 2):
            S.memset(self.sub(zp, (slice(None), slice(c, c + 1))), 0.0)

        def to_zp(P):
            def cb(t, ps):
                for (a, e, isc) in segs(t * TW, (t + 1) * TW):
                    o = 1 if isc else 2
                    S.copy(self.sub(zp, (slice(0, P), slice(a + o, e + o))), self.sub(ps, (slice(None), slice(a - t * TW, e - t * TW))),
                           eng='act' if t % 2 else 'dve')
            return cb
        self.proj(win, RW0 + 1536, 96, to_zp(96))
        self.tshift(zp, lambda a, e: self.sub(LR1, (slice(None), slice(a, e))), 96, vdc(12, 96), vc(24, 96), vc(25, 96))
        S.act(self.sub(LR1, (slice(0, 64), slice(None))), self.sub(LR1, (slice(0, 64), slice(None))), AF.Tanh)
        self.proj(win, RW0 + 1632, 96, to_zp(96))
        self.tshift(zp, lambda a, e: self.sub(LR2, (slice(None), slice(a, e))), 96, vdc(13, 96), vc(26, 96), vc(27, 96))
        S.act(LR2, LR2, AF.Sigmoid)
        self.dump('LR1', LR1, [96, NT], BF16)
        self.dump('LR2', LR2, [96, NT], BF16)
        for p in range(1 if self.debug else 4):
            for c in (0, 257, NT + 2):
                S.memset(self.sub(zp, (slice(None), slice(c, c + 1))), 0.0)
            for q, dst in enumerate((rs_, ks_, vs_)):
                self.proj(win, RW0 + q * 512 + p * 128, 128, to_zp(128))
                self.tshift(zp, lambda a, e, dst=dst: self.sub(dst, (slice(None), slice(a, e))), 128,
                            vdc(q * 4 + p), vc(q * 8 + p), vc(q * 8 + 4 + p))
            for t in range(NTILE):
                c0, c1 = t * TW, (t + 1) * TW
                ba = self.bank()
                for hh in range(2):
                    S.mm(self.ps[64 * hh:64 * hh + 64, ba, 0:TW], self.rww[64:96, 1, p * 128 + hh * 64:p * 128 + hh * 64 + 64],
                         self.sub(LR1, (slice(64, 96), slice(c0, c1))))
                a_ = self.wv(0)
                S.act(a_, self.ps[:, ba, 0:TW], AF.Sigmoid, bias=vc(36 + p))
                kk = self.wv(1)
                S.ts(kk, self.sub(ks_, tsl(t)), vc(40 + p), None, ALU.mult)
                sqb = self.wv(2, TW, BF16)
                S.tt(sqb, kk, kk, ALU.mult)
                bs = self.bank()
                S.mm(self.ps[:, bs, 0:TW], self.blkb(), sqb)
                nr = self.wv(3)
                S.act(nr, self.ps[:, bs, 0:TW], AF.Sqrt)
                S.ts(nr, nr, 1e-12, None, ALU.max)
                S.recip(nr, nr)
                S.tt(kk, kk, nr, ALU.mult)
                al = self.wv(2, TW, BF16)
                S.ts(al, kk, -1.0, None, ALU.mult)
                S.dma('sp', self.rwp[3][:, c0:c1], al)
                be = self.wv(3, TW, BF16)
                S.tt(be, kk, a_, ALU.mult)
                S.dma('act', self.rwp[4][:, c0:c1], be)
                S.ts(a_, a_, vc(44 + p), vdc(16 + p), ALU.mult, ALU.add)
                k2 = self.wv(4, TW, BF16)
                S.tt(k2, self.sub(ks_, tsl(t)), a_, ALU.mult)
                S.dma('sp', self.rwp[1][:, c0:c1], k2)
            S.dma('sp', self.rwp[0], rs_)
            S.dma('act', self.rwp[2], vs_)
            G0 = 8 * SLOT
            gin = [self.sv(G0 + i * 256, 256) for i in range(5)]
            dec = [self.sv(G0 + 2 * SLOT + i * 256, 256) for i in range(4)]
            tok = [self.sv(G0 + 3 * SLOT + i * 256, 256) for i in range(3)]
            amat = [self.sv(G0 + (4 + i // 2) * SLOT + (i % 2) * 512, 512) for i in range(4)]
            pm = [self.sv(G0 + (6 + i // 2) * SLOT + (i % 2) * 512, 512) for i in range(4)]
            rhsb = self.sv(G0 + 8 * SLOT, 128)
            ub = self.sv(G0 + 8 * SLOT + 128, 128)
            Mst = self.wv(4, 64, F32)
            Mbf = self.wv(4, 64, BF16, c0=64)
            gl = self.wv(4, 2, F32, c0=128)
            a4 = lambda v: v.re(lambda a: a.rearrange("p (h c t) -> p h c t", h=2, c=2))
            a3 = lambda v: v.re(lambda a: a.rearrange("p (c t) -> p c t", c=2))
            for d in range(2):
                S.memset(Mst, 0.0)
                S.memset(Mbf, 0.0)
                gorder = list(range(9)) if d == 0 else [0] + list(range(8, 0, -1))
                for gi in gorder:
                    g0 = gi * 256
                    gs = (slice(None), slice(g0, g0 + 256))
                    for i in range(5):
                        S.dma('sp' if i % 2 else 'act', gin[i], self.rwp[i][:, g0:g0 + 256])
                    bsg = self.bank()
                    for hh in range(2):
                        S.mm(self.ps[64 * hh:64 * hh + 64, bsg, 0:256],
                             self.rww[32 * d:32 * d + 32, 0, p * 128 + hh * 64:p * 128 + hh * 64 + 64],
                             self.sub(LR1, (slice(32 * d, 32 * d + 32), slice(g0, g0 + 256))))
                    cw = self.wv(0, 256)
                    S.act(cw, self.ps[:, bsg, 0:256], AF.Sigmoid, bias=vc(28 + d * 4 + p))
                    rmg = self.rm[:, 0:256]
                    if d == 0:
                        S.scan(cw, rmg, cw, 0.0, ALU.mult, ALU.add)
                    else:
                        S.scan(cw.re(lambda a: a[:, ::-1]), rmg, cw.re(lambda a: a[:, ::-1]), 0.0, ALU.mult, ALU.add)
                    E1, E2, E3 = self.wv(1, 256), self.wv(2, 256), self.wv(3, 256)
                    S.act(E1, cw, AF.Exp, scale=-KAPPA)
                    S.act(E2, cw, AF.Exp, scale=KAPPA)
                    S.memset(E3, 1.0)
                    if d == 0:
                        S.act(a3(E3)[:, :, 1:128], a3(cw)[:, :, 0:127], AF.Exp, scale=-KAPPA)
                    else:
                        S.act(a3(E3)[:, :, 0:127], a3(cw)[:, :, 1:128], AF.Exp, scale=-KAPPA)
                    S.copy(gl, a3(E1)[:, :, 127 if d == 0 else 0])
                    rt, at, bt, kt = dec
                    S.tt(rt, gin[0], E1, ALU.mult)
                    S.tt(at, gin[3], E3, ALU.mult)
                    S.tt(bt, gin[4], E2, ALU.mult)
                    S.tt(kt, gin[1], E2, ALU.mult)
                    for i, src in enumerate((bt, kt, gin[2])):
                        bt_ = self.bank()
                        tp = View(self.ps.h[:, bt_, 0:128].bitcast(BF16), self.ps, bt_, bt_ + 1)
                        for c in range(2):
                            S.tr(self.sub(tp, (slice(None), slice(c * 128, (c + 1) * 128))),
                                 self.sub(src, (slice(None), slice(c * 128, (c + 1) * 128))), self.identb())
                        S.copy(tok[i], tp, eng='act' if i % 2 else 'dve')
                    btok, ktok, vtok = tok

                    def amm(dst, lf, rf, maskcol, extra_ident=False):
                        b = self.bank()
                        for hh in range(2):
                            for c in range(2):
                                hs = slice(64 * hh, 64 * hh + 64)
                                cs = slice(c * 128, (c + 1) * 128)
                                S.mm(self.ps[:, b, (hh * 2 + c) * 128:(hh * 2 + c + 1) * 128],
                                     self.sub(lf, (hs, cs)), self.sub(rf, (hs, cs)))
                        S.tt(a4(dst).re(lambda a: a.rearrange("p h c t -> p (h c) t")),
                             self.ps[:, b, 0:512].re(lambda a: a.rearrange("p (n t) -> p n t", n=4)),
                             self.bmask(maskcol, 4), ALU.mult)
                    incl = C_TRF if d == 0 else C_TRB
                    strict = C_STF if d == 0 else C_STB
                    strictT = C_STB if d == 0 else C_STF
                    ArbT, ArkT, AakT, TT = amat
                    amm(ArbT, bt, rt, incl)
                    amm(ArkT, kt, rt, incl)
                    amm(AakT, kt, at, strict)
                    P, PT = pm[0], pm[1]
                    amm(PT, bt, at, strict)
                    amm(P, at, bt, strictT)


                    def bmm(lf, rf):
                        b = self.bank()
                        for n in range(4):
                            ns = (slice(None), slice(n * 128, (n + 1) * 128))
                            S.mm(self.ps[:, b, n * 128:(n + 1) * 128], self.sub(lf, ns), self.sub(rf, ns))
                        return self.ps[:, b, 0:512]
                    f4 = lambda v: v.re(lambda a: a.rearrange("p (n t) -> p n t", n=4))
                    Q0, Q0T = pm[2], pm[3]
                    S.tt(f4(Q0), f4(P), self.bmask(1024, 4), ALU.mult)
                    S.tt(f4(Q0T), f4(PT), self.bmask(1024, 4), ALU.mult)
                    NbT = [self.sv(G0 + 9 * SLOT + i * 512, 512) for i in range(4)]
                    for i in range(4):
                        S.tt(f4(NbT[i]), f4(PT), self.bmask(1152 + 128 * i, 4), ALU.mult)
                    Dm = self.sv(G0 + 11 * SLOT, 512)
                    S.tt(f4(Dm), f4(Q0), self.bmask(C_ID, 4), ALU.add)
                    S.tt(f4(TT), f4(Q0T), self.bmask(C_ID, 4), ALU.add)
                    Q1, Q1T = pm[0], pm[1]
                    S.copy(Q1, bmm(Q0T, Q0), eng='act')
                    S.copy(Q1T, bmm(Q0, Q0T))
                    S.tt(Dm, Dm, bmm(TT, Q1), ALU.add)
                    S.tt(TT, TT, bmm(Q1, TT), ALU.add)
                    Q2 = pm[2]
                    S.copy(Q2, bmm(Q1T, Q1), eng='act')
                    S.tt(Dm, Dm, bmm(TT, Q2), ALU.add)
                    S.tt(TT, TT, bmm(Q2, TT), ALU.add)
                    X = pm[0]
                    for i in range(4):
                        S.copy(X, bmm(NbT[i], Dm), eng='act')
                        if i < 3:
                            S.tt(Dm, Dm, bmm(TT, X), ALU.add)
                        S.tt(TT, TT, bmm(X, TT), ALU.add)
                    if d == 0 and gi == 0 and p == 0:
                        self.dump('cw', cw, [128, 256]); self.dump('E1', E1, [128, 256]); self.dump('E3', E3, [128, 256])
                        self.dump('TT', TT, [128, 512], BF16); self.dump('ArbT', ArbT, [128, 512], BF16); self.dump('AakT', AakT, [128, 512], BF16)
                        self.dump('P0', pm[0], [128, 512], BF16); self.dump('btok', btok, [128, 256], BF16)
                    corder = (0, 1) if d == 0 else (1, 0)
                    for c in corder:
                        cs = slice(c * 128, (c + 1) * 128)
                        b1 = self.bank()
                        for hh in range(2):
                            hs = slice(64 * hh, 64 * hh + 64)
                            n = hh * 2 + c
                            S.mm(self.ps[:, b1, hh * 64:hh * 64 + 64], self.sub(at, (hs, cs)), self.sub(Mbf, (hs, slice(None))),
                                 start=True, stop=False)
                            S.mm(self.ps[:, b1, hh * 64:hh * 64 + 64], self.sub(AakT, (slice(None), slice(n * 128, (n + 1) * 128))),
                                 self.sub(vtok, (slice(None), slice(c * 128 + hh * 64, c * 128 + hh * 64 + 64))), start=False, stop=True)
                        S.copy(rhsb, self.ps[:, b1, 0:128], eng='act')
                        b2 = self.bank()
                        for hh in range(2):
                            n = hh * 2 + c
                            S.mm(self.ps[:, b2, hh * 64:hh * 64 + 64], self.sub(TT, (slice(None), slice(n * 128, (n + 1) * 128))),
                                 self.sub(rhsb, (slice(None), slice(hh * 64, hh * 64 + 64))))
                        S.copy(ub, self.ps[:, b2, 0:128])
                        b3 = self.bank()
                        for hh in range(2):
                            hs = slice(64 * hh, 64 * hh + 64)
                            n = hh * 2 + c
                            us = self.sub(ub, (slice(None), slice(hh * 64, hh * 64 + 64)))
                            vv = self.sub(vtok, (slice(None), slice(c * 128 + hh * 64, c * 128 + hh * 64 + 64)))
                            S.mm(self.ps[hs, b3, 0:128], self.sub(Mbf, (hs, slice(None))), self.sub(rt, (hs, cs)), start=True, stop=False)
                            S.mm(self.ps[hs, b3, 0:128], us, self.sub(ArbT, (slice(None), slice(n * 128, (n + 1) * 128))), start=False, stop=False)
                            S.mm(self.ps[hs, b3, 0:128], vv, self.sub(ArkT, (slice(None), slice(n * 128, (n + 1) * 128))), start=False, stop=True)
                        oc = self.sub(oacc, (slice(None), slice(g0 + c * 128, g0 + (c + 1) * 128)))
                        if d == 0:
                            S.copy(oc, self.ps[:, b3, 0:128], eng='act')
                        else:
                            S.tt(oc, oc, self.ps[:, b3, 0:128], ALU.add)
                        b4 = self.bank()
                        for hh in range(2):
                            hs = slice(64 * hh, 64 * hh + 64)
                            us = self.sub(ub, (slice(None), slice(hh * 64, hh * 64 + 64)))
                            vv = self.sub(vtok, (slice(None), slice(c * 128 + hh * 64, c * 128 + hh * 64 + 64)))
                            S.mm(self.ps[hs, b4, 0:64], self.sub(btok, (slice(None), slice(c * 128 + hh * 64, c * 128 + hh * 64 + 64))), us,
                                 start=True, stop=False)
                            S.mm(self.ps[hs, b4, 0:64], self.sub(ktok, (slice(None), slice(c * 128 + hh * 64, c * 128 + hh * 64 + 64))), vv,
                                 start=False, stop=True)
                        S.tt(Mst, Mst, self.ps[:, b4, 0:64], ALU.add)
                        S.ts(Mst, Mst, self.sub(gl, (slice(None), slice(c, c + 1))), None, ALU.mult)
                        S.copy(Mbf, Mst, eng='act')
            self.dump(f'oacc{p}', oacc, [128, NT])
            wo = self.ldw(wout[:, 4 + p:5 + p, :], 1, D)
            for t in range(NTILE):
                c0, c1 = t * TW, (t + 1) * TW
                rr, kk2, vv = self.wv(5, TW, BF16), self.wv(5, TW, BF16, c0=192), self.wv(6, TW, BF16)
                S.dma('sp', rr, self.rwp[0][:, c0:c1])
                S.dma('act', kk2, self.rwp[1][:, c0:c1])
                S.dma('sp', vv, self.rwp[2][:, c0:c1])
                ot = self.sub(oacc, tsl(t))
                bm = self.bank()
                S.mm(self.ps[:, bm, 0:TW], self.blk32(), ot)
                cen = self.wv(0)
                S.stt(cen, self.ps[:, bm, 0:TW], -1.0 / 64, ot, ALU.mult, ALU.add)
                sq = self.wv(1)
                S.act(sq, cen, AF.Square)
                bv = self.bank()
                S.mm(self.ps[:, bv, 0:TW], self.blk32(), sq)
                rstd = self.wv(1)
                S.act(rstd, self.ps[:, bv, 0:TW], AF.Sqrt, bias=self.gneps(), scale=1.0 / 64)
                S.recip(rstd, rstd)
                S.tt(cen, cen, rstd, ALU.mult)
                S.ts(cen, cen, vc(52 + p), vc(56 + p), ALU.mult, ALU.add)
                rk = self.wv(2, TW, BF16)
                S.stt(rk, rr, vc(48 + p), kk2, ALU.mult, ALU.mult)
                bb = self.bank()
                S.mm(self.ps[:, bb, 0:TW], self.blkb(), rk)
                bon = self.wv(3)
                S.tt(bon, vv, self.ps[:, bb, 0:TW], ALU.mult)
                S.tt(cen, cen, bon, ALU.add)
                bg = self.bank()
                for hh in range(2):
                    S.mm(self.ps[64 * hh:64 * hh + 64, bg, 0:TW], self.rww[0:96, 2, p * 128 + hh * 64:p * 128 + hh * 64 + 64],
                         self.sub(LR2, (slice(None), slice(c0, c1))))
                yb = self.wv(4, TW, BF16)
                S.tt(yb, cen, self.ps[:, bg, 0:TW], ALU.mult)
                for m in range(8):
                    b = self.bank()
                    S.mm(self.ps[:, b, 0:TW], wo[:, 0, m * 128:(m + 1) * 128], yb)
                    self.resid(l, 2, m, c0, c1, self.ps[:, b, 0:TW], s)

    def odd(self, l, s):
        S = self.S
        j = l // 2
        win, wout = self.s_odin[j], self.s_odout[j]
        self.nrot = 4
        qa = self.sv(0, NT, BF16, 0, 65)
        ka = self.sv(2 * SLOT, NT, BF16, 0, 65)
        vtG = self.sv(4 * SLOT, 18 * 128)
        cosT = self.sv(6 * SLOT, 2048, F32, 0, 64)
        sinT = self.sv(10 * SLOT, 2048, F32, 0, 64)
        vrow = self.sv(14 * SLOT, 2048, BF16, 0, 64)
        vctx = self.sv(16 * SLOT, 128)
        tab = self.sv(17 * SLOT, 960, BF16, 0, 64)
        qn = self.sv(18 * SLOT, NT, BF16, 0, 64)
        negK = self.wv(9, 1, F32, 64, 65, c0=380)
        kmx = self.wv(9, 8, F32, 64, 65, c0=368)
        perm = self.cstb[0:64, 1728:1792]
        accn = [0]
        S.dma('sp', cosT, self.ropet[0])
        S.dma('act', sinT, self.ropet[1])

        def prep(dst, col0, gcol, rope, scale, is_q):
            w = self.ldw(win[:, :, col0:col0 + 64], 8, 64)
            for t in range(NTILE):
                c0, c1 = t * TW, (t + 1) * TW
                b = self.bank()
                for k in range(8):
                    S.mm(self.ps[0:64, b, 0:TW], w[:, k, :], self.hT[:, k, c0:c1], start=(k == 0), stop=(k == 7))
                pq = self.ps[0:64, b, 0:TW]
                qnt = self.sub(qn, (slice(None), slice(c0, c1)))
                if gcol is not None:
                    sq = self.wv(0, TW, BF16, 0, 64)
                    S.act(sq, pq, AF.Square)
                    b2 = self.bank()
                    S.mm(self.ps[0:64, b2, 0:TW], self.onesb(64, 64), sq)
                    rstd = self.wv(1, TW, F32, 0, 64)
                    S.act(rstd, self.ps[0:64, b2, 0:TW], AF.Sqrt, bias=self.eps(64), scale=1.0 / 64)
                    S.recip(rstd, rstd)
                    S.stt(qnt, pq, self.v64[:, gcol:gcol + 1], rstd, ALU.mult, ALU.mult)
                else:
                    S.copy(qnt, pq, eng='act')
                for (a, e, isc) in segs(c0, c1):
                    d_ = self.sub(dst, (slice(0, 64), slice(a, e)))
                    q_ = self.sub(qn, (slice(None), slice(a, e)))
                    if isc or not rope:
                        S.ts(d_, q_, scale, None, ALU.mult)
                    else:
                        n = e - a
                        b3 = self.bank()
                        S.mm(self.ps[0:64, b3, 0:n], perm, q_)
                        t1 = self.wv(2, n, F32, 0, 64)
                        t2 = self.wv(3, n, F32, 0, 64)
                        S.tt(t1, q_, self.sub(cosT, (slice(None), slice(a - CTX, e - CTX))), ALU.mult)
                        S.tt(t2, self.ps[0:64, b3, 0:n], self.sub(sinT, (slice(None), slice(a - CTX, e - CTX))), ALU.mult)
                        S.tt(t1, t1, t2, ALU.add)
                        S.ts(d_, t1, scale, None, ALU.mult)
                sq2 = self.wv(4, TW, BF16, 0, 64)
                dt_ = self.sub(dst, (slice(0, 64), slice(c0, c1)))
                S.tt(sq2, dt_, dt_, ALU.mult)
                b4 = self.bank()
                S.mm(self.ps[0:65, b4, 0:TW], self.onesb(64, 65), sq2)
                if is_q:
                    nq = self.wv(5, TW, F32, 64, 65)
                    S.act(nq, self.ps[64:65, b4, 0:TW], AF.Sqrt)
                    S.ts(self.sub(dst, (slice(64, 65), slice(c0, c1))), nq, negK, None, ALU.mult)
                else:
                    S.reduce(self.sub(kmx, (slice(None), slice(t, t + 1))), self.ps[64:65, b4, 0:TW], ALU.max)
            if not is_q:
                S.reduce(negK, self.sub(kmx, (slice(None), slice(0, NTILE))), ALU.max)
                S.act(negK, negK, AF.Sqrt)
                S.ts(negK, negK, -1.0, None, ALU.mult)
                S.memset(self.sub(dst, (slice(64, 65), slice(None))), 1.0)

        def accbanks():
            a = accn[0] % 2
            accn[0] += 1
            return 4 + 2 * a, 5 + 2 * a

        def finish(num, den, W):
            rden = self.wv2(2, W, F32, 0, 64)
            S.recip(rden, den)
            y = self.wv(4, W, BF16, 0, 64)
            S.tt(y, num, rden, ALU.mult)
            return y

        def dense(vfun, ktiles, q0, W):
            bn, bd = accbanks()
            num, den = self.ps[0:64, bn, 0:W], self.ps[0:64, bd, 0:W]
            last = len(ktiles) - 1
            accP = self.wv2(6, W, F32)

            def sc(i):
                kt = ktiles[i]
                b = self.bank()
                S.mm(self.ps[:, b, 0:W], self.sub(ka, (slice(0, 65), slice(kt * 128, (kt + 1) * 128))),
                     self.sub(qa, (slice(0, 65), slice(q0, q0 + W))))
                return b
            bcur = sc(0)
            for i, kt in enumerate(ktiles):
                bnext = sc(i + 1) if i < last else None
                pT = self.wv(i % 2, W, BF16)
                S.act(pT, self.ps[:, bcur, 0:W], AF.Exp)
                S.mm(num, vfun(kt), pT, start=(i == 0), stop=(i == last))
                if i == 0:
                    S.copy(accP, pT, eng='pool')
                else:
                    S.tt(accP, accP, pT, ALU.add, eng='pool')
                bcur = bnext
            S.mm(den, self.cst[0:64, 128:192], self.sub(accP, (slice(0, 64), slice(None))), start=True, stop=False)
            S.mm(den, self.cst[64:128, 192:256], self.sub(accP, (slice(64, 128), slice(None))), start=False, stop=True)
            return finish(num, den, W)

        def ld_wo(hrow):
            st = self.stage[self.stn % 3]
            self.stn += 1
            dst = st[0:64, 0:D]
            S.dma('sp', dst, wout[(hrow % 2) * 64:(hrow % 2) * 64 + 64, hrow // 2, :])
            return dst

        def apply_wout(y, W, q0, wo):
            for m in range(8):
                b = self.bank()
                S.mm(self.ps[:, b, 0:W], self.sub(wo, (slice(None), slice(m * 128, (m + 1) * 128))), y)
                self.resid(l, 2, m, q0, q0 + W, self.ps[:, b, 0:W], s)

        wv = self.ldw(win[:, :, 640:768], 8, 128)
        for c in range(NCH):
            b = self.bank()
            for k in range(8):
                S.mm(self.ps[:, b, 0:128], self.hT[:, k, c * 128:(c + 1) * 128], wv[:, k, :], start=(k == 0), stop=(k == 7))
            S.copy(self.sub(vtG, (slice(None), slice(c * 128, (c + 1) * 128))), self.ps[:, b, 0:128], eng='act' if c % 2 else 'dve')
        for kv in range(2):
            prep(ka, 512 + kv * 64, 142 + j, True, 1.0, False)
            for hq in range(4):
                h = kv * 4 + hq
                prep(qa, h * 64, 140 + j, True, 0.125, True)
                wo = ld_wo(h)
                vf = lambda kt, kv=kv: self.sub(vtG, (slice(None), slice(kt * 128 + kv * 64, kt * 128 + kv * 64 + 64)))
                y = dense(vf, [0, 1], 0, CTX)
                apply_wout(y, CTX, 0, wo)
                for qb_ in range(4):
                    q0 = CTX + qb_ * 512
                    y = dense(vf, list(range(NCH)), q0, 512)
                    apply_wout(y, 512, q0, wo)
        tok = lambda r: CTX + r * 64
        tab3 = tab.re(lambda a: a.rearrange("p (d q) -> p d q", d=15))
        vrowA = self.sv(4 * SLOT, 32 * 256, BF16, 0, 64)
        vctxA = self.sv(14 * SLOT, 512)
        for hg in range(2):
            wv4 = self.ldw(win[:, :, 1792 + hg * 256:1792 + (hg + 1) * 256], 8, 256)
            for r in range(32):
                b = self.bank()
                for k in range(8):
                    S.mm(self.ps[0:64, b, 0:256], self.hT[:, k, tok(r):tok(r) + 64], wv4[:, k, :], start=(k == 0), stop=(k == 7))
                S.copy(self.sub(vrowA, (slice(None), slice(r * 256, (r + 1) * 256))), self.ps[0:64, b, 0:256], eng='act' if r % 2 else 'dve')
            for ct in range(2):
                b = self.bank()
                for k in range(8):
                    S.mm(self.ps[:, b, 0:256], self.hT[:, k, ct * 128:(ct + 1) * 128], wv4[:, k, :], start=(k == 0), stop=(k == 7))
                S.copy(self.sub(vctxA, (slice(None), slice(ct * 256, (ct + 1) * 256))), self.ps[:, b, 0:256])
            for hh in range(4):
                h = hg * 4 + hh
                prep(ka, 1280 + h * 64, None, False, 1.0, False)
                prep(qa, 768 + h * 64, None, False, 0.125, True)
                S.dma('act', tab, self.s_tab[j][:, h, :])
                wo = ld_wo(8 + h)
                vcx = lambda kt, hh=hh: self.sub(vctxA, (slice(None), slice(kt * 256 + hh * 64, kt * 256 + hh * 64 + 64)))
                y = dense(vcx, [0, 1], 0, CTX)
                apply_wout(y, CTX, 0, wo)
                for r0 in range(0, 32, 8):
                    bn, bd = accbanks()
                    num, den = self.ps[0:64, bn, 0:512], self.ps[0:64, bd, 0:512]
                    qblk = self.sub(qa, (slice(0, 65), slice(tok(r0), tok(r0) + 512)))
                    pTc = self.wv2(8, 1024, BF16)
                    for ct in range(2):
                        bc = self.bank()
                        S.mm(self.ps[:, bc, 0:512], self.sub(ka, (slice(0, 65), slice(ct * 128, (ct + 1) * 128))), qblk)
                        S.act(self.sub(pTc, (slice(None), slice(ct * 512, (ct + 1) * 512))), self.ps[:, bc, 0:512], AF.Exp)
                    for ct in range(2):
                        S.mm(num, vcx(ct), self.sub(pTc, (slice(None), slice(ct * 512, (ct + 1) * 512))), start=(ct == 0), stop=False)
                    pc = self.wv2(2, 512, F32)
                    S.tt(pc, self.sub(pTc, (slice(None), slice(0, 512))), self.sub(pTc, (slice(None), slice(512, 1024))), ALU.add, eng='pool')
                    S.mm(den, self.cst[0:64, 128:192], self.sub(pc, (slice(0, 64), slice(None))), start=True, stop=False)
                    S.mm(den, self.cst[64:128, 192:256], self.sub(pc, (slice(64, 128), slice(None))), start=False, stop=False)

                    def scores(r):
                        rs = min(max(r - 4, 0), 24)
                        qv = self.sub(qa, (slice(0, 65), slice(tok(r), tok(r) + 64)))
                        b = self.bank()
                        for i in range(8):
                            S.mm(self.ps[0:64, b, i * 64:(i + 1) * 64], self.sub(ka, (slice(0, 65), slice(tok(rs + i), tok(rs + i) + 64))), qv)
                        return b
                    nxt = scores(r0)
                    for r in range(r0, r0 + 8):
                        b = nxt
                        rs = min(max(r - 4, 0), 24)
                        dr0 = rs - r + 7
                        lastrow = (r == r0 + 7)
                        sb = self.wv2(6, 512, F32, 0, 64)
                        S.tt(sb.re(lambda a: a.rearrange("p (i q) -> p i q", i=8)),
                             self.ps[0:64, b, 0:512].re(lambda a: a.rearrange("p (i q) -> p i q", i=8)),
                             View(tab3.ap[:, dr0:dr0 + 8, ::-1], tab.buf, tab.b0, tab.b1), ALU.add)
                        pT = self.wv(r % 2, 512, BF16, 0, 64)
                        S.act(pT, sb, AF.Exp)
                        if not lastrow:
                            nxt = scores(r + 1)
                        cs = slice((r - r0) * 64, (r - r0 + 1) * 64)
                        out = self.ps[0:64, bn, cs]
                        for i in range(8):
                            S.mm(out, self.sub(vrowA, (slice(None), slice((rs + i) * 256 + hh * 64, (rs + i) * 256 + hh * 64 + 64))),
                                 self.sub(pT, (slice(None), slice(i * 64, (i + 1) * 64))), start=False, stop=(lastrow and i == 7))
                        ps8 = self.wv(5, 64, F32, 0, 64, c0=(r % 2) * 192)
                        S.reduce(ps8, pT.re(lambda a: a.rearrange("p (i q) -> p q i", i=8)), ALU.add)
                        S.mm(self.ps[0:64, bd, cs], self.cst[0:64, 128:192], ps8, start=False, stop=lastrow)
                    y = finish(num, den, 512)
                    apply_wout(y, 512, tok(r0), wo)
        self.nrot = 8


def make_consts():
    c = np.zeros((128, 1792), np.float32)
    j = np.arange(128)[:, None]; i = np.arange(128)[None, :]
    c[:, C_ID:C_ID + 128] = (j == i)
    c[:, C_TRF:C_TRF + 128] = (j <= i)
    c[:, C_TRB:C_TRB + 128] = (j >= i)
    c[:, C_ONE:C_ONE + 128] = 1.0
    c[:, C_STF:C_STF + 128] = (j < i)
    c[:, C_STB:C_STB + 128] = (j > i)
    c[:, C_BLK:C_BLK + 128] = ((j // 64) == (i // 64))
    c[:, 1024:1152] = ((j // 8) == (i // 8))
    for n_, b_ in enumerate((8, 16, 32, 64)):
        c[:, 1152 + 128 * n_:1280 + 128 * n_] = ((j // (2 * b_)) == (i // (2 * b_))) & ((j // b_) != (i // b_))
    qc = 63 - np.arange(64)[None, :]; kc = np.arange(64)[:, None]
    start = np.clip(qc - 8, 0, 48)
    c[0:64, 1664:1728] = np.where((kc >= start) & (kc < start + 16), 0.0, -30000.0)
    for dst in range(64):
        a_, b_, f_ = dst // 32, (dst % 32) // 16, dst % 16
        if b_ == 0:
            c[dst + 16, 1728 + dst] = -1.0
        else:
            c[dst - 16, 1728 + dst] = 1.0
    c[:, 896] = 1e-6
    c[:, 897] = 1.0
    c[:, 898] = 64e-5
    c[:, 899] = 1e-12
    return c

def pack_common(inp):
    f = lambda k: np.asarray(inp[k], np.float32)
    v32 = np.zeros((384, 128), np.float32)
    v32[R_BMOD:R_BMOD + 192] = f("b_mod").reshape(192, 128)
    v32[R_NG:R_NG + 32] = f("norm1_g").reshape(32, 128)
    v32[R_NG + 32:R_NG + 64] = f("norm2_g").reshape(32, 128)
    v32[R_FG:R_FG + 8] = f("final_g").reshape(8, 128)
    v32[R_GLAG:R_GLAG + 2] = f("gla_norm_g")
    v64 = np.zeros((64, 256), np.float32)
    ab = f("gla_a_bias")
    for j in range(2):
        for d in range(2):
            for h in range(4):
                v64[:, j * 128 + d * 4 + h] = ab[j, d, h * 64:(h + 1) * 64]
    vrw = np.zeros((128, 256), np.float32)
    mu = f("rw_mu")
    for j in range(2):
        b = j * 128
        for p in range(4):
            for q in range(3):
                vrw[:, b + q * 8 + p] = mu[j, 0, q * 512 + p * 128:q * 512 + (p + 1) * 128]
                vrw[:, b + q * 8 + 4 + p] = mu[j, 1, q * 512 + p * 128:q * 512 + (p + 1) * 128]
            for d in range(2):
                vrw[:, b + 28 + d * 4 + p] = f("rw_w0")[j, d, p * 128:(p + 1) * 128]
            vrw[:, b + 36 + p] = f("rw_a0")[j, p * 128:(p + 1) * 128]
            vrw[:, b + 40 + p] = f("rw_k_k")[j, p * 128:(p + 1) * 128]
            vrw[:, b + 44 + p] = f("rw_k_a")[j, p * 128:(p + 1) * 128]
            vrw[:, b + 48 + p] = f("rw_r_k")[j].reshape(512)[p * 128:(p + 1) * 128]
            vrw[:, b + 52 + p] = f("rw_ln_g")[j, p * 128:(p + 1) * 128]
            vrw[:, b + 56 + p] = f("rw_ln_b")[j, p * 128:(p + 1) * 128]
        vrw[0:96, b + 24] = mu[j, 0, 1536:1632]; vrw[0:96, b + 25] = mu[j, 1, 1536:1632]
        vrw[0:96, b + 26] = mu[j, 0, 1632:1728]; vrw[0:96, b + 27] = mu[j, 1, 1632:1728]
    v64[:, 140:142] = f("cq_norm_g").T
    v64[:, 142:144] = f("ck_norm_g").T
    rp = np.zeros((2, 120, 160), np.float32)
    rp[:, :, 64:95] = f("na_rpb").reshape(2, 120, 31)
    t_ = np.arange(2048)
    pos = np.stack([t_ // 64, t_ % 64], axis=-1).astype(np.float32)
    inv = (np.float32(10000.0) ** (-np.arange(0, 32, 2, dtype=np.float32) / np.float32(32))).astype(np.float32)
    ang = pos[:, :, None] * inv
    rope = np.zeros((2, 64, 2048), np.float32)
    for d_ in range(64):
        a_, f_ = d_ // 32, d_ % 16
        rope[0, d_] = np.cos(ang[:, a_, f_]); rope[1, d_] = np.sin(ang[:, a_, f_])
    rmask = np.ones((1, NT), np.float32); rmask[0, ::128] = 0.0
    com = {"w_mod": f("w_mod"), "vec32": v32, "ffn_w13": f("ffn_w13"), "ffn_w2": f("ffn_w2"),
           "ev_w_in": f("ev_w_in"), "ev_w_out": f("ev_w_out"), "od_w_in": f("od_w_in"), "od_w_out": f("od_w_out"),
           "gla_a_up": f("gla_a_up"), "rw_w_up": f("rw_w_up"), "rw_a_up": f("rw_a_up"), "rw_g_up": f("rw_g_up"),
           "vec64": v64, "vecrw": vrw, "rpbp": rp, "ropet": rope, "consts": make_consts(), "rmask": rmask}
    return com

def pack_core(inp, com, seqs):
    f = lambda k: np.asarray(inp[k], np.float32)
    m = dict(com)
    m["x"] = np.ascontiguousarray(f("x")[seqs])
    m["ctx"] = np.ascontiguousarray(f("ctx")[seqs])
    cc = np.zeros((8, D), np.float32)
    cc[:len(seqs)] = f("c")[seqs]
    cc[4] = f("c_ctx")
    m["cc"] = cc
    return m


PARTS = ('gla', 'rw', 'odd', 'ffn')


def kernel(**inputs):
    from concourse.bass_utils import run_bass_kernel_spmd
    ncores, nseq = 8, 4
    nc = bass.Bass("TRN2", target_bir_lowering=False)
    MK(nc, nseq=nseq, layers=(0, 1, 2, 3), final=True, parts=PARTS)
    com = pack_common(inputs)
    in_maps = [pack_core(inputs, com, list(range(c * nseq, (c + 1) * nseq))) for c in range(ncores)]
    res = run_bass_kernel_spmd(nc, in_maps, core_ids=list(range(ncores)))
    return np.concatenate([np.asarray(r["out"], np.float32) for r in res.results], axis=0)
```
